# Optimizing a Trainium2 kernel written in Bass

```python
import jax, jax.numpy as jnp
from jax import lax
import numpy as np

D_MODEL = 1024
BATCH = 8
SEQ = 2048
DEPTH = 1
DEC_BATCH = 128
DEC_SEQ = 4
PAST_LEN = 16384
PAGE_SIZE = 128

SSD_EXPAND = 2
D_SSD = SSD_EXPAND * D_MODEL
SSD_HEADDIM = 64
SSD_HEADS = D_SSD // SSD_HEADDIM
SSD_GROUPS = 4
SSD_HPG = SSD_HEADS // SSD_GROUPS
D_STATE = 128
SSD_CONV = 4
SSD_CHUNK = 128
D_XBC = D_SSD + 2 * SSD_GROUPS * D_STATE
D_LRU = D_MODEL
LRU_BLOCKS = 8
LRU_BW = D_LRU // LRU_BLOCKS
LRU_CONV = 4
LRU_C = 8.0
D_FF = 3 * D_MODEL
FFN_CONV = 3
ALPHA = (2.0 * DEPTH) ** 0.25
BETA = (8.0 * DEPTH) ** -0.25
LN_EPS = 1e-5
RMS_EPS = 1e-6
D_IN = D_SSD + D_XBC + SSD_HEADS + 2 * D_LRU + 2 * D_MODEL

kernel_name = "hybrid_ssd_rglru_convffn_step"


def layer_norm(x, g, b):
    xf = x.astype(jnp.float32)
    mu = jnp.mean(xf, -1, keepdims=True)
    var = jnp.mean(jnp.square(xf - mu), -1, keepdims=True)
    return ((xf - mu) * lax.rsqrt(var + LN_EPS) * g + b).astype(x.dtype)


def rms_norm(x, g, dtype):
    xf = x.astype(jnp.float32)
    return (xf * lax.rsqrt(jnp.mean(jnp.square(xf), -1, keepdims=True) + RMS_EPS) * g).astype(dtype)


def causal_dwconv(u, buf, w, b):
    K = w.shape[0]
    L = u.shape[1]
    xp = jnp.concatenate([buf.astype(u.dtype), u], axis=1)
    y = b + sum(w[k] * xp[:, k:k + L] for k in range(K))
    return y, xp[:, L:]


def ssd_scan(x, dt, A, Bm, Cm, h0):
    f32 = jnp.float32
    b, L = x.shape[:2]
    Q = SSD_CHUNK if L % SSD_CHUNK == 0 else L
    nc = L // Q
    G, Hg, P, N = SSD_GROUPS, SSD_HPG, SSD_HEADDIM, D_STATE
    x = x.astype(f32).reshape(b, nc, Q, G, Hg, P)
    dt = dt.astype(f32).reshape(b, nc, Q, G, Hg)
    Bm = Bm.astype(f32).reshape(b, nc, Q, G, N)
    Cm = Cm.astype(f32).reshape(b, nc, Q, G, N)
    dA = dt * A.astype(f32).reshape(G, Hg)
    a_cs = jnp.cumsum(dA, axis=2)
    xdt = x * dt[..., None]
    seg = a_cs[:, :, :, None] - a_cs[:, :, None, :]
    causal = jnp.tril(jnp.ones((Q, Q), bool))[:, :, None, None]
    decay = jnp.exp(jnp.where(causal, seg, -jnp.inf))
    cb = jnp.einsum('bclgn,bcsgn->bclsg', Cm, Bm)
    y_diag = jnp.einsum('bclsg,bclsgh,bcsghp->bclghp', cb, decay, xdt)
    decay_to_end = jnp.exp(a_cs[:, :, -1:] - a_cs)
    states = jnp.einsum('bclgn,bclgh,bclghp->bcghpn', Bm, decay_to_end, xdt)
    chunk_decay = jnp.exp(a_cs[:, :, -1])

    def step(h, inp):
        s, d = inp
        return d[..., None, None] * h + s, h

    h_init = h0.astype(f32).reshape(b, G, Hg, P, N)
    h_last, h_prev = lax.scan(step, h_init,
                              (jnp.moveaxis(states, 1, 0), jnp.moveaxis(chunk_decay, 1, 0)))
    h_prev = jnp.moveaxis(h_prev, 0, 1)
    y_off = jnp.einsum('bclgn,bcghpn,bclgh->bclghp', Cm, h_prev, jnp.exp(a_cs))
    y = (y_diag + y_off).reshape(b, L, SSD_HEADS, P)
    return y, h_last.reshape(b, SSD_HEADS, P, N)


def ssd_branch(z, xbc, dt_raw, conv_buf, h0, conv_w, conv_b, dt_bias, a_log, d_skip, norm_g):
    xbc, new_buf = causal_dwconv(xbc, conv_buf, conv_w, conv_b)
    xbc = jax.nn.silu(xbc)
    b, L = xbc.shape[:2]
    GN = SSD_GROUPS * D_STATE
    xs = xbc[..., :D_SSD].reshape(b, L, SSD_HEADS, SSD_HEADDIM)
    Bm = xbc[..., D_SSD:D_SSD + GN].reshape(b, L, SSD_GROUPS, D_STATE)
    Cm = xbc[..., D_SSD + GN:].reshape(b, L, SSD_GROUPS, D_STATE)
    dt = jax.nn.softplus(dt_raw.astype(jnp.float32) + dt_bias)
    A = -jnp.exp(a_log.astype(jnp.float32))
    y, h_last = ssd_scan(xs, dt, A, Bm, Cm, h0)
    y = y + d_skip[:, None] * xs
    y = y.reshape(b, L, D_SSD) * jax.nn.silu(z)
    return rms_norm(y, norm_g, z.dtype), h_last.astype(h0.dtype), new_buf


def rglru_branch(xr, gy, conv_buf, h0, start_pos, conv_w, conv_b, wa, ba, wx, bx, lam):
    f32 = jnp.float32
    xr, new_buf = causal_dwconv(xr, conv_buf, conv_w, conv_b)
    b, L = xr.shape[:2]
    xb = xr.reshape(b, L, LRU_BLOCKS, LRU_BW)
    r = jax.nn.sigmoid(jnp.einsum('blnc,ncd->blnd', xb, wa) + ba).reshape(b, L, D_LRU).astype(f32)
    i = jax.nn.sigmoid(jnp.einsum('blnc,ncd->blnd', xb, wx) + bx).reshape(b, L, D_LRU).astype(f32)
    log_a = -LRU_C * r * jax.nn.softplus(-lam.astype(f32))
    a = jnp.exp(log_a)
    mult = jnp.sqrt(1.0 - jnp.exp(2.0 * log_a))
    first = (start_pos + jnp.arange(L)) == 0
    mult = jnp.where(first[None, :, None], 1.0, mult)
    u = mult * i * xr.astype(f32)
    u = u.at[:, 0].add(a[:, 0] * h0.astype(f32))

    def combine(c1, c2):
        a1, b1 = c1
        a2, b2 = c2
        return a1 * a2, a2 * b1 + b2

    _, h = lax.associative_scan(combine, (a, u), axis=1)
    y = h * jax.nn.gelu(gy.astype(f32))
    return y.astype(xr.dtype), h[:, -1].astype(h0.dtype), new_buf


def conv_ffn(x, buf, w_gate, w_up, conv_w, conv_b, w_down):
    g, new_buf = causal_dwconv(x @ w_gate, buf, conv_w, conv_b)
    h = jax.nn.gelu(g) * (x @ w_up)
    return h @ w_down, new_buf


def block(x, start_pos, ssd_h, ssd_buf, lru_h, lru_buf, ffn_buf, p):
    proj = x @ p['w_in']
    sizes = (D_SSD, D_XBC, SSD_HEADS, D_LRU, D_LRU, D_MODEL, D_MODEL)
    cuts = tuple(int(c) for c in np.cumsum(sizes)[:-1])
    z, xbc, dt_raw, lru_x, lru_y, g_ssd, g_lru = jnp.split(proj, cuts, axis=-1)
    y_ssd, ssd_h_new, ssd_buf_new = ssd_branch(
        z, xbc, dt_raw, ssd_buf, ssd_h, p['ssd_conv_w'], p['ssd_conv_b'], p['ssd_dt_bias'],
        p['ssd_a_log'], p['ssd_d'], p['ssd_norm_g'])
    y_lru, lru_h_new, lru_buf_new = rglru_branch(
        lru_x, lru_y, lru_buf, lru_h, start_pos, p['lru_conv_w'], p['lru_conv_b'],
        p['lru_wa'], p['lru_ba'], p['lru_wx'], p['lru_bx'], p['lru_lambda'])
    gate_ssd = jax.nn.sigmoid(g_ssd + p['b_gate'][:D_MODEL])
    gate_lru = jax.nn.sigmoid(g_lru + p['b_gate'][D_MODEL:])
    merged = gate_ssd * (y_ssd @ p['w_ssd_out']) + gate_lru * (y_lru @ p['w_lru_out'])
    x1 = layer_norm(ALPHA * x + merged @ p['w_o'], p['ln1_g'], p['ln1_b'])
    f, ffn_buf_new = conv_ffn(x1, ffn_buf, p['ffn_w_gate'], p['ffn_w_up'],
                              p['ffn_conv_w'], p['ffn_conv_b'], p['ffn_w_down'])
    y = layer_norm(ALPHA * x1 + f, p['ln2_g'], p['ln2_b'])
    return y, (ssd_h_new, ssd_buf_new, lru_h_new, lru_buf_new, ffn_buf_new)


def trunk(x, start_pos, ssd_h, ssd_buf, lru_h, lru_buf, ffn_buf, w):
    outs = [[] for _ in range(5)]
    for l in range(DEPTH):
        p = {k: v[l] for k, v in w.items()}
        x, new = block(x, start_pos, ssd_h[l], ssd_buf[l], lru_h[l], lru_buf[l], ffn_buf[l], p)
        for o, s in zip(outs, new):
            o.append(s)
    return x, tuple(jnp.stack(o) for o in outs)


def setup_inputs(seed: int = 0) -> dict:
    key = jax.random.key(seed)
    ks = iter(jax.random.split(key, 40))
    nrm = lambda shape, s: jax.random.normal(next(ks), shape, jnp.float32) * s
    unif = lambda shape, lo, hi: jax.random.uniform(next(ks), shape, jnp.float32, lo, hi)
    Dp = DEPTH
    dt0 = jnp.exp(unif((Dp, SSD_HEADS), np.log(1e-3), np.log(1e-1)))
    a0 = unif((Dp, D_LRU), 0.9, 0.999)
    return {
        'x_prompt': nrm((BATCH, SEQ, D_MODEL), 1.0),
        'x_sample': nrm((DEC_BATCH, DEC_SEQ, D_MODEL), 1.0),
        'state_ssd': nrm((Dp, DEC_BATCH, SSD_HEADS, SSD_HEADDIM, D_STATE), 0.1),
        'cache_ssd_conv': nrm((Dp, DEC_BATCH, SSD_CONV - 1, D_XBC), 1.0),
        'state_lru': nrm((Dp, DEC_BATCH, D_LRU), 0.5),
        'cache_lru_conv': nrm((Dp, DEC_BATCH, LRU_CONV - 1, D_LRU), 1.0),
        'cache_ffn_conv': nrm((Dp, DEC_BATCH, FFN_CONV - 1, D_FF), 1.0),
        'w_in': nrm((Dp, D_MODEL, D_IN), D_MODEL ** -0.5),
        'b_gate': nrm((Dp, 2 * D_MODEL), 0.01),
        'ssd_conv_w': nrm((Dp, SSD_CONV, D_XBC), SSD_CONV ** -0.5),
        'ssd_conv_b': nrm((Dp, D_XBC), 0.01),
        'ssd_dt_bias': dt0 + jnp.log(-jnp.expm1(-dt0)),
        'ssd_a_log': jnp.log(unif((Dp, SSD_HEADS), 1.0, 16.0)),
        'ssd_d': 1.0 + nrm((Dp, SSD_HEADS), 0.01),
        'ssd_norm_g': 1.0 + nrm((Dp, D_SSD), 0.01),
        'w_ssd_out': nrm((Dp, D_SSD, D_MODEL), BETA * D_SSD ** -0.5),
        'lru_conv_w': nrm((Dp, LRU_CONV, D_LRU), LRU_CONV ** -0.5),
        'lru_conv_b': nrm((Dp, D_LRU), 0.01),
        'lru_wa': nrm((Dp, LRU_BLOCKS, LRU_BW, LRU_BW), LRU_BW ** -0.5),
        'lru_ba': nrm((Dp, LRU_BLOCKS, LRU_BW), 0.01),
        'lru_wx': nrm((Dp, LRU_BLOCKS, LRU_BW, LRU_BW), LRU_BW ** -0.5),
        'lru_bx': nrm((Dp, LRU_BLOCKS, LRU_BW), 0.01),
        'lru_lambda': jnp.log(a0) - jnp.log1p(-a0),
        'w_lru_out': nrm((Dp, D_LRU, D_MODEL), BETA * D_LRU ** -0.5),
        'w_o': nrm((Dp, D_MODEL, D_MODEL), BETA * D_MODEL ** -0.5),
        'ln1_g': 1.0 + nrm((Dp, D_MODEL), 0.01),
        'ln1_b': nrm((Dp, D_MODEL), 0.01),
        'ffn_w_gate': nrm((Dp, D_MODEL, D_FF), D_MODEL ** -0.5),
        'ffn_w_up': nrm((Dp, D_MODEL, D_FF), BETA * D_MODEL ** -0.5),
        'ffn_conv_w': nrm((Dp, FFN_CONV, D_FF), FFN_CONV ** -0.5),
        'ffn_conv_b': nrm((Dp, D_FF), 0.01),
        'ffn_w_down': nrm((Dp, D_FF, D_MODEL), BETA * D_FF ** -0.5),
        'ln2_g': 1.0 + nrm((Dp, D_MODEL), 0.01),
        'ln2_b': nrm((Dp, D_MODEL), 0.01),
    }


def reference(x_prompt, x_sample, state_ssd, cache_ssd_conv, state_lru, cache_lru_conv,
              cache_ffn_conv, w_in, b_gate, ssd_conv_w, ssd_conv_b, ssd_dt_bias, ssd_a_log,
              ssd_d, ssd_norm_g, w_ssd_out, lru_conv_w, lru_conv_b, lru_wa, lru_ba, lru_wx,
              lru_bx, lru_lambda, w_lru_out, w_o, ln1_g, ln1_b, ffn_w_gate, ffn_w_up,
              ffn_conv_w, ffn_conv_b, ffn_w_down, ln2_g, ln2_b):
    w = dict(w_in=w_in, b_gate=b_gate, ssd_conv_w=ssd_conv_w, ssd_conv_b=ssd_conv_b,
             ssd_dt_bias=ssd_dt_bias, ssd_a_log=ssd_a_log, ssd_d=ssd_d, ssd_norm_g=ssd_norm_g,
             w_ssd_out=w_ssd_out, lru_conv_w=lru_conv_w, lru_conv_b=lru_conv_b, lru_wa=lru_wa,
             lru_ba=lru_ba, lru_wx=lru_wx, lru_bx=lru_bx, lru_lambda=lru_lambda,
             w_lru_out=w_lru_out, w_o=w_o, ln1_g=ln1_g, ln1_b=ln1_b, ffn_w_gate=ffn_w_gate,
             ffn_w_up=ffn_w_up, ffn_conv_w=ffn_conv_w, ffn_conv_b=ffn_conv_b,
             ffn_w_down=ffn_w_down, ln2_g=ln2_g, ln2_b=ln2_b)
    bp = x_prompt.shape[0]
    z_ssd = jnp.zeros((DEPTH, bp) + state_ssd.shape[2:], state_ssd.dtype)
    z_ssd_buf = jnp.zeros((DEPTH, bp) + cache_ssd_conv.shape[2:], cache_ssd_conv.dtype)
    z_lru = jnp.zeros((DEPTH, bp) + state_lru.shape[2:], state_lru.dtype)
    z_lru_buf = jnp.zeros((DEPTH, bp) + cache_lru_conv.shape[2:], cache_lru_conv.dtype)
    z_ffn_buf = jnp.zeros((DEPTH, bp) + cache_ffn_conv.shape[2:], cache_ffn_conv.dtype)
    y_prompt, (p_ssd, p_ssd_buf, p_lru, p_lru_buf, p_ffn_buf) = trunk(
        x_prompt, 0, z_ssd, z_ssd_buf, z_lru, z_lru_buf, z_ffn_buf, w)
    y_sample, (s_ssd, s_ssd_buf, s_lru, s_lru_buf, s_ffn_buf) = trunk(
        x_sample, PAST_LEN, state_ssd, cache_ssd_conv, state_lru, cache_lru_conv,
        cache_ffn_conv, w)
    return (y_prompt, y_sample, p_ssd, p_ssd_buf, p_lru, p_lru_buf, p_ffn_buf,
            s_ssd, s_ssd_buf, s_lru, s_lru_buf, s_ffn_buf)
```

```python
import math
from contextlib import ExitStack

import numpy as np
import concourse.bass as bass
import concourse.mybir as mybir
from concourse.bass_utils import run_bass_kernel_spmd

F32 = mybir.dt.float32
BF16 = mybir.dt.bfloat16
AF = mybir.ActivationFunctionType
ALU = mybir.AluOpType

NCORES = 8
D = 1024
SEQ = 2048
SB_ = 16
SL = 4
D_IN = 9248
C_Z, C_XBC, C_DT, C_LX, C_LY, C_GS, C_GL = 0, 2048, 5120, 5152, 6176, 7200, 8224
ALPHA = 2.0 ** 0.25
LN_EPS = 1e-5
RMS_EPS = 1e-6
GK = 1.0 / 0.044715
GC = math.sqrt(2.0 / math.pi) * 0.044715
NEG = -30000.0

ENGS = ("pe", "act", "dve", "pool", "sp")


class Buf:
    __slots__ = ("name", "w", "r", "dsem", "excl")

    def __init__(self, name, excl=False):
        self.name = name
        self.w = []
        self.r = []
        self.dsem = {}
        self.excl = excl


class Prog:
    def __init__(self, nc):
        self.nc = nc
        self.es = ExitStack()
        self.ops = []
        self.sems = {}
        for e in ENGS:
            self.sems[e] = self.es.enter_context(nc.semaphore("s_" + e))
        self.ndsem = 0
        self.banks = []
        self.bank_i = 0
        self.reserved = set()
        self.pe_unit = None
        self.tag = ""

    def sb(self, name, shape, dt):
        return self.es.enter_context(self.nc.sbuf_tensor("sb_" + name, list(shape), dt))

    def ps(self, name, shape, dt=F32):
        return self.es.enter_context(self.nc.psum_tensor("ps_" + name, list(shape), dt))

    def _dsem(self, buf, sw):
        if sw not in buf.dsem:
            key = "d%d" % self.ndsem
            self.ndsem += 1
            self.sems[key] = self.es.enter_context(self.nc.semaphore("s_" + key))
            buf.dsem[sw] = key
        return buf.dsem[sw]

    def _collect(self, oid, reads, writes, deps):
        for b in reads:
            for i in b.w:
                if i != oid:
                    deps[i] = True
        for b in writes:
            if IGNORE_FALSE_DEPS and not b.excl:
                continue
            for i in b.w:
                if i != oid:
                    deps.setdefault(i, False)
            for i in b.r:
                if i != oid:
                    deps.setdefault(i, False)
        for b in reads:
            if not b.r or b.r[-1] != oid:
                b.r.append(oid)
        for b in writes:
            b.w = [oid]
            b.r = []

    def op(self, eng, fn, reads=(), writes=(), inc=True, dur=500.0):
        reads = list(reads)
        writes = list(writes)
        if any(b.excl for b in reads):
            writes = writes + [b for b in reads if b.excl and b not in writes]
            reads = [b for b in reads if not b.excl]
        if eng == "pe":
            if self.pe_unit is None:
                self.pe_unit = dict(id=len(self.ops), eng="pe", fns=[], kind="c", deps={}, dur=0.0, key="pe", tag=self.tag)
                self.ops.append(self.pe_unit)
            u = self.pe_unit
            u["fns"].append(fn)
            u["dur"] += dur
            self._collect(u["id"], reads, writes, u["deps"])
            if inc:
                self.pe_unit = None
            return u["id"]
        o = dict(id=len(self.ops), eng=eng, fns=[fn], kind="c", deps={}, dur=dur, key=eng, tag=self.tag)
        self.ops.append(o)
        self._collect(o["id"], reads, writes, o["deps"])
        return o["id"]

    def dma(self, eng, fn, reads=(), writes=(), sembuf=None, nbytes=65536):
        assert self.pe_unit is None or eng != "pe"
        sw = eng == "pool"
        key = self._dsem(sembuf, sw)
        o = dict(id=len(self.ops), eng=eng, fns=[fn], kind="d", deps={}, key=key, tag=self.tag,
                 dur=(1200.0 if sw else 150.0), lat=2000.0 + nbytes / 200.0)
        self.ops.append(o)
        self._collect(o["id"], list(reads), list(writes), o["deps"])
        return o["id"]

    def bank(self):
        n = len(self.banks)
        while True:
            i = self.bank_i % n
            self.bank_i += 1
            if i not in self.reserved:
                return self.banks[i]

    def finish(self):
        assert self.pe_unit is None

    def _schedule(self):
        import heapq
        ops = self.ops
        n = len(ops)
        ndep = [len(o["deps"]) for o in ops]
        users = [[] for _ in range(n)]
        for o in ops:
            for d in o["deps"]:
                users[d].append(o["id"])
        finish = [0.0] * n
        bl = [0.0] * n
        for o in reversed(ops):
            d = o.get("lat", o["dur"]) if o["kind"] == "d" else o["dur"]
            m = 0.0
            for u in users[o["id"]]:
                if bl[u] > m:
                    m = bl[u]
            bl[o["id"]] = d + m
        if SCHED_PRIO == "bl":
            prio = [(-bl[i], i) for i in range(n)]
        elif SCHED_PRIO == "mix":
            prio = [(-(bl[i] - SCHED_MIX * i), i) for i in range(n)]
        else:
            prio = [(i, i) for i in range(n)]
        wait_heap = {e: [] for e in ENGS}
        now_heap = {e: [] for e in ENGS}
        free = {e: 0.0 for e in ENGS}
        order = {e: [] for e in ENGS}
        for o in ops:
            if ndep[o["id"]] == 0:
                heapq.heappush(wait_heap[o["eng"]], (0.0, o["id"]))
        done = 0
        while done < n:
            best = None
            for e in ENGS:
                wh, nh = wait_heap[e], now_heap[e]
                while wh and wh[0][0] <= free[e]:
                    heapq.heappush(nh, prio[heapq.heappop(wh)[1]])
                if nh:
                    cand = (free[e], nh[0][1], e, True)
                elif wh:
                    cand = (wh[0][0], wh[0][1], e, False)
                else:
                    continue
                if best is None or cand[:2] < best[:2]:
                    best = cand
            assert best is not None, "scheduler deadlock"
            start, oid, e, from_now = best
            if from_now:
                heapq.heappop(now_heap[e])
            else:
                heapq.heappop(wait_heap[e])
            o = ops[oid]
            rt = max([finish[d] + (SYNC_LAT if ops[d]["eng"] != e else SYNC_LAT_SAME) for d in o["deps"]] + [0.0])
            if rt >= start - 1e-6 and o["deps"]:
                o["crit"] = max(o["deps"], key=lambda d: finish[d])
            else:
                o["crit"] = order[e][-1] if order[e] else None
            free[e] = start + o["dur"]
            finish[oid] = start + (o["lat"] if o["kind"] == "d" else o["dur"])
            o["start"] = start
            order[e].append(oid)
            done += 1
            for u in users[oid]:
                ndep[u] -= 1
                if ndep[u] == 0:
                    ue = ops[u]["eng"]
                    rt = max(finish[d] + (SYNC_LAT if ops[d]["eng"] != ue else SYNC_LAT_SAME) for d in ops[u]["deps"])
                    heapq.heappush(wait_heap[ops[u]["eng"]], (rt, u))
        self.sim_end = max(finish) if finish else 0.0
        return order

    def emit(self):
        nc = self.nc
        sems = self.sems
        ops = self.ops
        order = self._schedule()
        val = {}
        cnt = {}
        for e in ENGS:
            for oid in order[e]:
                o = ops[oid]
                if o["kind"] == "c":
                    cnt[e] = cnt.get(e, 0) + 1
                    val[oid] = cnt[e]
        dmas = sorted((o for o in ops if o["kind"] == "d"), key=lambda o: (o["start"], o["id"]))
        final = {}
        for o in dmas:
            cnt[o["key"]] = cnt.get(o["key"], 0) + 16
            val[o["id"]] = cnt[o["key"]]
            final[o["key"]] = cnt[o["key"]]
        waited = {}
        plan = {e: [] for e in ENGS}
        for e in ENGS:
            for oid in order[e]:
                o = ops[oid]
                need = {}
                for d, raw in o["deps"].items():
                    od = ops[d]
                    if o["kind"] == "c" and od["kind"] == "c" and od["eng"] == e:
                        if e == "pe" or (not raw and not STRICT_SAME_ENGINE):
                            continue
                    k = od["key"]
                    if need.get(k, 0) < val[d]:
                        need[k] = val[d]
                waits = []
                for k, v in need.items():
                    if waited.get((e, k), 0) >= v:
                        continue
                    waited[(e, k)] = v
                    waits.append((k, v))
                plan[e].append((o, waits))
        fin = [(k, v) for k, v in final.items() if waited.get(("sp", k), 0) < v]

        def run(engine, e):
            for o, waits in plan[e]:
                for k, v in waits:
                    engine.wait_ge(sems[k], v)
                ins = None
                for fn in o["fns"]:
                    ins = fn(engine)
                ins.then_inc(sems[o["key"]], 16 if o["kind"] == "d" else 1)
            if e == "sp":
                for k, v in fin:
                    engine.wait_ge(sems[k], v)

        with nc.Block() as block:
            @block.tensor
            def _(eng):
                run(eng, "pe")

            @block.scalar
            def _(eng):
                run(eng, "act")

            @block.vector
            def _(eng):
                run(eng, "dve")

            @block.gpsimd
            def _(eng):
                run(eng, "pool")

            @block.sync
            def _(eng):
                run(eng, "sp")
        self.es.close()


def _compact(evs):
    best = {}
    for k, v in evs:
        if best.get(k, 0) < v:
            best[k] = v
    return list(best.items())


def view(ap, *dims):
    if len(dims) == 1:
        return ap
    names = "abcdefg"[: len(dims)]
    s = "p (" + " ".join(names) + ") -> p " + " ".join(names)
    kw = {names[i]: dims[i] for i in range(1, len(dims))}
    return ap.rearrange(s, **kw)


class Reg:
    def __init__(self, arena, lo, n):
        self.arena, self.lo, self.n = arena, lo, n
        self.b = arena.bufs[lo:lo + n]

    def f(self, *dims, parts=128):
        ap = self.arena.tile[0:parts, self.lo * 512:(self.lo + self.n) * 512]
        tot = int(np.prod(dims))
        ap = ap[:, 0:tot]
        return view(ap, *dims)

    def h(self, *dims, parts=128):
        ap = self.arena.tile[0:parts, self.lo * 512:(self.lo + self.n) * 512].bitcast(BF16)
        tot = int(np.prod(dims))
        ap = ap[:, 0:tot]
        return view(ap, *dims)

    def sub(self, i, nbytes):
        lo = (i * nbytes) // 2048
        hi = ((i + 1) * nbytes - 1) // 2048
        return self.b[lo:hi + 1]


class Arena:
    def __init__(self, P, npages):
        self.tile = P.sb("arena", [128, npages * 512], F32)
        self.bufs = [Buf("pg%d" % i) for i in range(npages)]
        self.free_list = [(0, npages)]
        self.npages = npages

    def alloc(self, n):
        for idx, (lo, cnt) in enumerate(self.free_list):
            if cnt >= n:
                if cnt == n:
                    self.free_list.pop(idx)
                else:
                    self.free_list[idx] = (lo + n, cnt - n)
                return Reg(self, lo, n)
        raise RuntimeError("arena OOM: want %d pages, free %s" % (n, self.free_list))

    def free(self, *regs):
        for r in regs:
            self.free_list.append((r.lo, r.n))
        self.free_list.sort()
        merged = []
        for lo, cnt in self.free_list:
            if merged and merged[-1][0] + merged[-1][1] == lo:
                merged[-1] = (merged[-1][0], merged[-1][1] + cnt)
            else:
                merged.append((lo, cnt))
        self.free_list = merged


class WStream:
    def __init__(self, P, nslots, blocks, per_pass, scratch):
        self.P = P
        self.tile = P.sb("wslots", [128, nslots, 8, 512], BF16)
        self.bufs = [Buf("ws%d" % i) for i in range(nslots)]
        self.blocks = blocks
        self.per_pass = per_pass
        self.scratch = scratch
        self.scrbufs = [Buf("wscr%d" % i) for i in range(per_pass)]
        self.free_slots = list(range(nslots))
        self.slot_of = {}
        self.next_issue = 0
        self.next_get = 0
        self.pump()

    def pump(self):
        while self.free_slots and self.next_issue < len(self.blocks):
            s = self.free_slots.pop(0)
            i = self.next_issue
            w, k0, nkc, c0, ncols = self.blocks[i]
            bid = i % self.per_pass
            dst = self.tile[:, s, 0:nkc, 0:ncols]
            if i < self.per_pass:
                src = w[k0:k0 + nkc * 128, c0:c0 + ncols].rearrange("(kc p) n -> p kc n", p=128)
                self.P.dma("pool", _mk("dma_start", out=dst, in_=src), writes=[self.bufs[s]], sembuf=self.bufs[s],
                           nbytes=nkc * ncols * 512)
                if len(self.blocks) > self.per_pass:
                    self.P.dma("sp", _mk("dma_start", out=self.scratch[bid, :, 0:nkc, 0:ncols], in_=dst),
                               reads=[self.bufs[s]], writes=[self.scrbufs[bid]], sembuf=self.bufs[s], nbytes=nkc * ncols * 256)
            else:
                self.P.dma("sp", _mk("dma_start", out=dst, in_=self.scratch[bid, :, 0:nkc, 0:ncols]),
                           reads=[self.scrbufs[bid]], writes=[self.bufs[s]], sembuf=self.bufs[s], nbytes=nkc * ncols * 256)
            self.slot_of[i] = s
            self.next_issue += 1

    def get(self, expect=None):
        i = self.next_get
        self.next_get += 1
        assert i in self.slot_of, "weight block %d not issued (not enough slots)" % i
        if expect is not None:
            assert self.blocks[i][0] is expect[0] and self.blocks[i][1:] == tuple(expect[1:]), (i, self.blocks[i][1:], expect[1:])
        s = self.slot_of[i]
        return i, self.tile[:, s], self.bufs[s]

    def release(self, i):
        self.free_slots.append(self.slot_of.pop(i))
        self.pump()


def _mk(meth, *args, **kw):
    return lambda e: getattr(e, meth)(*args, **kw)


def _chk(P, aps, reads, writes):
    have = set(id(b) for b in list(reads) + list(writes))
    for ap in aps:
        nm = getattr(ap, "name", None)
        if isinstance(nm, str) and nm.startswith("ps_bank"):
            i = int(nm[len("ps_bank"):].split("_")[0])
            assert id(P.banks[i][1]) in have, "PSUM bank %d accessed without declaring its Buf" % i


def _fsz(ap):
    n = 1
    for d in ap.shape[1:]:
        n *= int(d)
    return n


def ACT(P, out, in_, func, reads, writes, bias=None, scale=None, accum=None):
    kw = {}
    if bias is not None:
        kw["bias"] = bias
    if scale is not None:
        kw["scale"] = scale
    if accum is not None:
        kw["accum_out"] = accum
    _chk(P, [out, in_, bias, scale], reads, writes)
    return P.op("act", lambda e: e.activation(out=out, in_=in_, func=func, **kw), reads, writes,
                dur=220.0 + _fsz(out) / 1.2)


def _vdur(eng, out):
    if eng == "pool":
        return 400.0 + _fsz(out) * 8.0
    return (151.0 + _fsz(out)) / 0.96


def TT(P, eng, out, in0, in1, op, reads, writes):
    _chk(P, [out, in0, in1], reads, writes)
    return P.op(eng, lambda e: e.tensor_tensor(out=out, in0=in0, in1=in1, op=op), reads, writes, dur=_vdur(eng, out))


def TS(P, eng, out, in0, s1, s2, op0, op1, reads, writes):
    _chk(P, [out, in0, s1, s2], reads, writes)
    if s2 is None:
        return P.op(eng, lambda e: e.tensor_scalar(out=out, in0=in0, scalar1=s1, scalar2=None, op0=op0), reads, writes,
                    dur=_vdur(eng, out))
    return P.op(eng, lambda e: e.tensor_scalar(out=out, in0=in0, scalar1=s1, scalar2=s2, op0=op0, op1=op1), reads, writes,
                dur=_vdur(eng, out))


def STT(P, eng, out, in0, scalar, in1, op0, op1, reads, writes):
    _chk(P, [out, in0, scalar, in1], reads, writes)
    return P.op(eng, lambda e: e.scalar_tensor_tensor(out=out, in0=in0, scalar=scalar, in1=in1, op0=op0, op1=op1),
                reads, writes, dur=_vdur(eng, out))


def CP(P, eng, out, in_, reads, writes):
    _chk(P, [out, in_], reads, writes)
    if eng == "act":
        return P.op("act", lambda e: e.activation(out=out, in_=in_, func=AF.Copy), reads, writes,
                    dur=220.0 + _fsz(out) / 1.2)
    return P.op(eng, lambda e: e.tensor_copy(out=out, in_=in_), reads, writes, dur=_vdur(eng, out))


def MM(P, out, lhsT, rhs, start, stop, reads, bank, inc):
    _chk(P, [out], reads, [bank])
    n = max(_fsz(rhs), 64)
    d = n / 2.3 * (4.0 if lhsT.dtype == F32 else 1.0) + 40.0
    return P.op("pe", lambda e: e.matmul(out, lhsT=lhsT, rhs=rhs, start=start, stop=stop, skip_group_check=True),
                reads, [bank], inc=inc, dur=d)


def build_program(debug=None):
    nc = bass.Bass("TRN2", target_bir_lowering=False)
    P = Prog(nc)

    def din(name, shape):
        return nc.dram_tensor(name, list(shape), F32, kind="ExternalInput").ap()

    def dout(name, shape):
        return nc.dram_tensor(name, list(shape), F32, kind="ExternalOutput").ap()

    xp = din("xp", [SEQ, D])
    xs = din("xs", [SB_ * SL, D])
    st_ssd = din("st_ssd", [SB_, 16, 128, 128])
    c_ssd = din("c_ssd", [SB_ * 3, 3072])
    st_lru = din("st_lru", [SB_, D])
    c_lru = din("c_lru", [SB_ * 3, D])
    c_ffn = din("c_ffn", [SB_ * 2, 3072])
    w_in = din("w_in", [D, D_IN])
    w_so = din("w_so", [2048, D])
    w_lo = din("w_lo", [D, D])
    w_o = din("w_o", [D, D])
    w_fg = din("w_fg", [D, 3072])
    w_fu = din("w_fu", [D, 3072])
    w_fd = din("w_fd", [3072, D])
    lru_w = din("lru_w", [2, 8, 128, 128])
    pp_d = din("pp", [128, NPP])
    bc_d = din("bc", [128, 4096])
    cst_d = din("cst", [128, NCST])
    cbf_d = din("cbfsrc", [128, NCBF])

    yp = dout("yp", [SEQ, D])
    ys = dout("ys", [SB_ * SL, D])
    o_pssd = dout("o_pssd", [16, 128, 128])
    o_pssdb = dout("o_pssdb", [3, 3072])
    o_plru = dout("o_plru", [8, 128])
    o_plrub = dout("o_plrub", [3, D])
    o_pffnb = dout("o_pffnb", [2, 3072])
    o_sssd = dout("o_sssd", [SB_, 16, 128, 128])
    o_sssdb = dout("o_sssdb", [SB_, 3, 3072])
    o_slru = dout("o_slru", [SB_, D])
    o_slrub = dout("o_slrub", [SB_, 3, D])
    o_sffnb = dout("o_sffnb", [SB_, 2, 3072])

    dbg_out = {}
    if debug:
        for name, shape in debug.items():
            dbg_out[name] = dout("dbg_" + name, shape)

    for i in range(8):
        t = P.ps("bank%d" % i, [128, 512])
        P.banks.append((t, Buf("bank%d" % i, True)))

    pp = P.sb("pp", [128, NPP], F32)
    b_pp = Buf("pp")
    cst = P.sb("cst", [128, NCST], F32)
    b_cst = Buf("cst")
    cbf = P.sb("cbf", [128, NCBF], BF16)
    b_cbf = Buf("cbf")
    pq = P.sb("pq", [128, NPQ], F32)
    b_pq = Buf("pq")
    lruw = P.sb("lruw", [128, 2, 8, 128], BF16)
    b_lruw = Buf("lruw")
    diagD = P.sb("diagD", [128, 16, 128], BF16)
    b_diagD = Buf("diagD")
    halo_x = P.sb("halo_x", [128, 24, 3], F32)
    b_halo_x = Buf("halo_x")
    halo_l = P.sb("halo_l", [128, 8, 3], F32)
    b_halo_l = Buf("halo_l")
    halo_f = P.sb("halo_f", [128, 24, 2], F32)
    b_halo_f = Buf("halo_f")
    hcar = P.sb("hcar", [128, 8], F32)
    b_hcar = Buf("hcar")
    Hst = P.sb("Hst", [128, 16, 128], F32)
    b_H = Buf("Hst")

    P.dma("sp", _mk("dma_start", out=pp[:], in_=pp_d[:, :]), writes=[b_pp], sembuf=b_pp)
    P.dma("sp", _mk("dma_start", out=cst[:], in_=cst_d[:, :]), writes=[b_cst], sembuf=b_cst)
    P.dma("pool", _mk("dma_start", out=lruw[:], in_=lru_w.rearrange("t n c d -> c t n d")),
          writes=[b_lruw], sembuf=b_lruw)

    ident = cst[:, CS_ID:CS_ID + 128]
    identb = cbf[:, 0:128]
    negm4 = cbf[:, 128:640]
    negs4 = cbf[0:64, 640:896]
    selb = view(cbf[0:32, 896:896 + 4096], 32, 128)
    selpar = cst[0:32, CS_SELPAR:CS_SELPAR + 128]
    mask2 = cst[0:32, CS_MASK2:CS_MASK2 + 16]
    ones32 = cst[0:32, CS_ONES:CS_ONES + 128]
    rmask = cst[0:32, CS_RMASK:CS_RMASK + 64]
    bmask = cst[0:64, CS_BMASK:CS_BMASK + 16]
    bmaskT = view(cst[:, CS_BMASKT:CS_BMASKT + 1024], 16, 64)

    P.dma("pool", _mk("dma_start", out=cbf[:], in_=cbf_d[:, :]), writes=[b_cbf], sembuf=b_cbf)

    def ppc(off, n=1):
        return pp[:, off:off + n]

    def pqc(off, n=1):
        return pq[:, off:off + n]

    TS(P, "dve", pqc(PQ_SCW, 96), ppc(PP_SCW, 96), 0.5, None, ALU.mult, None, [b_pp], [b_pq])
    TS(P, "dve", pqc(PQ_SCB, 24), ppc(PP_SCB, 24), 0.5, None, ALU.mult, None, [b_pp], [b_pq])
    TS(P, "dve", pqc(PQ_HBA, 8), ppc(PP_BA, 8), 0.5, None, ALU.mult, None, [b_pp], [b_pq])
    TS(P, "dve", pqc(PQ_HBX, 8), ppc(PP_BX, 8), 0.5, None, ALU.mult, None, [b_pp], [b_pq])
    TS(P, "dve", pqc(PQ_HBG, 16), ppc(PP_BG, 16), 0.5, None, ALU.mult, None, [b_pp], [b_pq])
    ACT(P, pqc(PQ_CL, 8), ppc(PP_LAM, 8), AF.Exp, [b_pp], [b_pq], scale=-1.0)
    ACT(P, pqc(PQ_CL, 8), pqc(PQ_CL, 8), AF.Ln, [b_pq], [b_pq], bias=1.0)
    TS(P, "dve", pqc(PQ_CL, 8), pqc(PQ_CL, 8), -4.0, None, ALU.mult, None, [b_pq], [b_pq])
    ACT(P, pq[0:32, PQ_A:PQ_A + 1], pp[0:32, PP_ALOG:PP_ALOG + 1], AF.Exp, [b_pp], [b_pq])
    TS(P, "dve", pq[0:32, PQ_A:PQ_A + 1], pq[0:32, PQ_A:PQ_A + 1], -1.0, None, ALU.mult, None, [b_pq], [b_pq])
    for j in range(16):
        TS(P, "dve", diagD[:, j, :], ident, ppc(PP_D + j), None, ALU.mult, None, [b_cst, b_pp], [b_diagD])
    P.op("pool", _mk("memset", halo_x[:], 0.0), [], [b_halo_x])
    P.op("pool", _mk("memset", halo_l[:], 0.0), [], [b_halo_l])
    P.op("pool", _mk("memset", halo_f[:], 0.0), [], [b_halo_f])
    P.op("pool", _mk("memset", hcar[:], 0.0), [], [b_hcar])
    P.op("pool", _mk("memset", Hst[:], 0.0), [], [b_H])

    A = Arena(P, NPAGES)

    def npg(nbytes):
        return max(1, (nbytes + 2047) // 2048)

    blocks = []
    tiles = TILES if TILES is not None else [("p", i) for i in range(4)] + [("s", 0)]
    for _ in tiles:
        for cb in range(2):
            blocks.append((w_in, 0, 8, C_LX + cb * 512, 512))
        for cb in range(2):
            blocks.append((w_in, 0, 8, C_LY + cb * 512, 512))
        for cb in range(6):
            blocks.append((w_in, 0, 8, C_XBC + cb * 512, 512))
        blocks.append((w_in, 0, 8, C_DT, 32))
        for cb in range(4):
            blocks.append((w_in, 0, 8, C_Z + cb * 512, 512))
        for cb in range(2):
            blocks.append((w_in, 0, 8, C_GS + cb * 512, 512))
            blocks.append((w_in, 0, 8, C_GL + cb * 512, 512))
            blocks.append((w_lo, 0, 8, cb * 512, 512))
        for cb in range(2):
            blocks.append((w_so, 0, 8, cb * 512, 512))
            blocks.append((w_so, 1024, 8, cb * 512, 512))
        for cb in range(2):
            blocks.append((w_o, 0, 8, cb * 512, 512))
        for fb in range(6):
            blocks.append((w_fg, 0, 8, fb * 512, 512))
            blocks.append((w_fu, 0, 8, fb * 512, 512))
        for cb in range(2):
            for kb in range(3):
                blocks.append((w_fd, kb * 1024, 8, cb * 512, 512))
    per_pass = len(blocks) // len(tiles)
    wscr = nc.dram_tensor("wscr", [per_pass, 128, 8, 512], BF16, kind="Internal").ap()
    WS = WStream(P, NSLOTS, blocks, per_pass, wscr)

    def dbg(name, ap, bufs):
        if name in dbg_out:
            o = dbg_out[name]
            idx = tuple(slice(None) for _ in o.shape)
            b0 = bufs[0]
            P.dma("sp", _mk("dma_start", out=o[idx], in_=ap), reads=bufs, sembuf=b0)

    def gelu2(dst, src, tmp, rb, wb_dst, wb_tmp):
        ACT(P, tmp, src, AF.Square, rb, wb_tmp)
        STT(P, "dve", tmp, tmp, GK, src, ALU.add, ALU.mult, rb + wb_tmp, wb_tmp)
        ACT(P, tmp, tmp, AF.Tanh, wb_tmp, wb_tmp, scale=GC)
        STT(P, "dve", dst, tmp, 1.0, src, ALU.add, ALU.mult, rb + wb_tmp, wb_dst)

    def load_bc(g0):
        r = A.alloc(4)
        t = r.f(2048)
        P.dma("sp", _mk("dma_start", out=t, in_=bc_d[:, g0:g0 + 2048]), writes=r.b, sembuf=r.b[0])
        return r, t

    def ln_stats(src, CL, c, rb, st, b_st, scratch, b_scr):
        bs = [b_st]
        ACT(P, scratch, src, AF.Identity, rb, b_scr + bs, accum=st[0:CL, 0, c:c + 1])
        ACT(P, scratch, src, AF.Square, rb, b_scr + bs, accum=st[0:CL, 1, c:c + 1])

    def ln_rstd(CL, st, b_st):
        bs = [b_st]
        TS(P, "dve", st[0:CL, 2, :], st[0:CL, 0, :], 1.0 / 1024, None, ALU.mult, None, bs, bs)
        TT(P, "dve", st[0:CL, 3, :], st[0:CL, 2, :], st[0:CL, 2, :], ALU.mult, bs, bs)
        STT(P, "dve", st[0:CL, 4, :], st[0:CL, 1, :], 1.0 / 1024, st[0:CL, 3, :], ALU.mult, ALU.subtract, bs, bs)
        TS(P, "dve", st[0:CL, 4, :], st[0:CL, 4, :], LN_EPS / (ALPHA * ALPHA), None, ALU.add, None, bs, bs)
        ACT(P, st[0:CL, 4, :], st[0:CL, 4, :], AF.Sqrt, bs, bs)
        P.op("dve", _mk("reciprocal", out=st[0:CL, 5, :], in_=st[0:CL, 4, :]), bs, bs)
        STT(P, "dve", st[0:CL, 6, :], st[0:CL, 2, :], -1.0, st[0:CL, 5, :], ALU.mult, ALU.mult, bs, bs)

    def ln_apply(src, dst, CL, c, rb, wb, st, b_st, r_bc, bct, scratch, b_scr):
        TS(P, "dve", scratch, src, st[0:CL, 5, c:c + 1], st[0:CL, 6, c:c + 1], ALU.mult, ALU.add, rb + [b_st], b_scr)
        TT(P, "dve", scratch, scratch, bct[0:CL, 0:1024], ALU.mult, b_scr + r_bc.b, b_scr)
        TT(P, "dve", dst, scratch, bct[0:CL, 1024:2048], ALU.add, b_scr + r_bc.b, wb)

    def tile_params(kind, ti):
        if kind == "s":
            return dict(T=64, CL=64, NCH=1, xsrc=xs, t0=0)
        return dict(T=512, CL=128, NCH=4, xsrc=xp, t0=ti * 512)

    def load_x_for(tp):
        r = A.alloc(npg(tp["NCH"] * 4096))
        xt_ = r.f(tp["NCH"], 1024)
        P.dma("sp", _mk("dma_start", out=xt_[0:tp["CL"]],
                        in_=tp["xsrc"][tp["t0"]:tp["t0"] + tp["T"], :].rearrange("(c p) d -> p c d", p=tp["CL"])),
              writes=r.b, sembuf=r.b[0], nbytes=tp["T"] * 4096)
        return r, xt_

    prefetched_x = None
    for tidx, (kind, ti) in enumerate(tiles):
        samp = kind == "s"
        if samp:
            T, NB, L, CL, NCH = 64, SB_, SL, 64, 1
            xsrc = xs
            ydst = ys
            t0 = 0
        else:
            T, NB, L, CL, NCH = 512, 1, 512, 128, 4
            xsrc = xp
            ydst = yp
            t0 = ti * 512
        last_p = (not samp) and ti == 3
        first_p = (not samp) and ti == 0

        P.tag = "%s%s%d" % ("xT_", kind, ti)
        def load_x():
            r = A.alloc(npg(NCH * 4096))
            xt_ = r.f(NCH, 1024)
            P.dma("sp", _mk("dma_start",
                out=xt_[0:CL], in_=xsrc[t0:t0 + T, :].rearrange("(c p) d -> p c d", p=CL)),
                writes=r.b, sembuf=r.b[0], nbytes=T * 4096)
            return r, xt_

        def build_xT(r_x, x_tm):
            r_xT = A.alloc(npg(8 * T * 2))
            xT = r_xT.h(8, T)
            r_x16 = A.alloc(npg(NCH * 2048))
            x16 = r_x16.h(NCH, 1024)
            for c in range(NCH):
                CP(P, "act" if c % 2 == 0 else "dve", x16[0:CL, c, :], x_tm[0:CL, c, :], r_x.sub(c, 4096), r_x16.sub(c, 2048))
                for half in range(2):
                    bt, bb = P.bank()
                    for j in range(4):
                        kc = half * 4 + j
                        MM(P, bt[:, j * CL:(j + 1) * CL], x16[0:CL, c, kc * 128:(kc + 1) * 128], identb[0:CL, 0:CL],
                           True, True, r_x16.sub(c, 2048) + [b_cbf], bb, j == 3)
                    eng = "act" if (c + half) % 2 == 0 else "dve"
                    CP(P, eng, xT[:, half * 4:half * 4 + 4, c * CL:(c + 1) * CL], view(bt[:, 0:4 * CL], 4, CL),
                       [bb], r_xT.b)
            A.free(r_x16)
            return r_xT, xT

        if prefetched_x is not None:
            r_x, x_tm = prefetched_x
            prefetched_x = None
        else:
            r_x, x_tm = load_x()
        r_xT, xT = build_xT(r_x, x_tm)
        A.free(r_x)

        P.tag = "%s%s%d" % ("lru_", kind, ti)
        r_lx = A.alloc(npg(8 * NB * (3 + L) * 4))
        lx = r_lx.f(8, NB, 3 + L)
        r_gy = A.alloc(npg(8 * T * 4))
        gy = r_gy.f(8, T)
        lx_raw_s = None
        if samp:
            r_c = A.alloc(4)
            ctm = r_c.f(1024)
            P.dma("sp", _mk("dma_start", out=ctm[0:48], in_=c_lru[:, :]), writes=r_c.b, sembuf=r_c.b[0])
            for half in range(2):
                bt, bb = P.bank()
                for j in range(4):
                    ch = half * 4 + j
                    MM(P, bt[:, j * 48:(j + 1) * 48], ctm[0:48, ch * 128:(ch + 1) * 128], ident[0:48, 0:48],
                       True, True, r_c.b + [b_cst], bb, j == 3)
                CP(P, "dve", lx[:, half * 4:half * 4 + 4, :, 0:3], view(bt[:, 0:192], 4, NB, 3), [bb], r_lx.b)
            A.free(r_c)
            r_lxs = A.alloc(1)
            lx_raw_s = r_lxs.f(8, 64)
        for cb in range(2):
            wi, wt, wb_ = WS.get((w_in, 0, 8, C_LX + cb * 512, 512))
            for oc in range(4):
                ch = cb * 4 + oc
                bt, bb = P.bank()
                for kc in range(8):
                    MM(P, bt[:, 0:T], wt[:, kc, oc * 128:(oc + 1) * 128], xT[:, kc, :], kc == 0, kc == 7,
                       [wb_] + r_xT.b, bb, kc == 7)
                if not samp:
                    CP(P, "pool", lx[:, ch, 0, 0:3], halo_l[:, ch, :], [b_halo_l], r_lx.sub(ch, (3 + L) * 4 * NB))
                CP(P, "act", lx[:, ch, :, 3:3 + L], view(bt[:, 0:T], NB, L), [bb], r_lx.sub(ch, (3 + L) * 4 * NB))
                if samp:
                    CP(P, "dve", lx_raw_s[:, ch, :], bt[:, 0:T], [bb], r_lxs.b)
                else:
                    CP(P, "pool", halo_l[:, ch, :], lx[:, ch, 0, L:L + 3], r_lx.sub(ch, (3 + L) * 4 * NB), [b_halo_l])
            WS.release(wi)
        for cb in range(2):
            wi, wt, wb_ = WS.get((w_in, 0, 8, C_LY + cb * 512, 512))
            for oc in range(4):
                ch = cb * 4 + oc
                bt, bb = P.bank()
                for kc in range(8):
                    MM(P, bt[:, 0:T], wt[:, kc, oc * 128:(oc + 1) * 128], xT[:, kc, :], kc == 0, kc == 7,
                       [wb_] + r_xT.b, bb, kc == 7)
                CP(P, "act", gy[:, ch, :], bt[:, 0:T], [bb], r_gy.sub(ch, T * 4))
            WS.release(wi)

        r_tm = A.alloc(2)
        stg = r_tm.f(1024)
        if samp or last_p:
            M = 64 if samp else 128
            for half in range(2):
                bt, bb = P.bank()
                for j in range(4):
                    ch = half * 4 + j
                    src = lx_raw_s[:, ch, :] if samp else lx[:, ch, 0, 3 + L - 128:3 + L]
                    MM(P, bt[0:M, j * 128:(j + 1) * 128], src, ident, True, True,
                       (r_lxs.b if samp else r_lx.b) + [b_cst], bb, j == 3)
                CP(P, "dve", stg[0:M, half * 512:(half + 1) * 512], bt[0:M, :], [bb], r_tm.b)
            if samp:
                for l in range(1, 4):
                    P.dma("sp", _mk("dma_start", out=o_slrub[:, l - 1, :], in_=stg[l:64:4, :]),
                          reads=r_tm.b, sembuf=r_tm.b[0])
            else:
                P.dma("sp", _mk("dma_start", out=o_plrub[:, :], in_=stg[125:128, :]), reads=r_tm.b, sembuf=r_tm.b[0])
        if samp:
            A.free(r_lxs)

        r_xr = A.alloc(npg(8 * T * 4))
        xr = r_xr.f(8, T)
        r_xrb = A.alloc(npg(8 * T * 2))
        xrb = r_xrb.h(8, T)
        for ch in range(8):
            lb = r_lx.sub(ch, (3 + L) * 4 * NB)
            xb_ = r_xr.sub(ch, T * 4)
            o = view(xr[:, ch, :], NB, L)
            ACT(P, o, lx[:, ch, :, 0:L], AF.Identity, lb + [b_pp], xb_,
                scale=ppc(PP_LCW + 0 * 8 + ch), bias=ppc(PP_LCB + ch))
            for k in range(1, 4):
                STT(P, "dve", o, lx[:, ch, :, k:k + L], ppc(PP_LCW + k * 8 + ch), o, ALU.mult, ALU.add,
                    lb + xb_ + [b_pp], xb_)
            CP(P, "act", xrb[:, ch, :], xr[:, ch, :], xb_, r_xrb.sub(ch, T * 2))
        A.free(r_lx)

        if NATIVE_GELU_LRU:
            for ch in range(8):
                gb = r_gy.sub(ch, T * 4)
                ACT(P, gy[:, ch, :], gy[:, ch, :], AF.Gelu_apprx_tanh, gb, gb)
        else:
            r_t1 = A.alloc(npg(2 * T * 4))
            for ch in range(8):
                tmp = r_t1.f(2, T)[:, ch % 2, :]
                gb = r_gy.sub(ch, T * 4)
                tb = [r_t1.b[ch % 2]] if T == 512 else r_t1.b
                gelu2(gy[:, ch, :], gy[:, ch, :], tmp, gb, gb, tb)
            A.free(r_t1)

        h0fm = None
        if samp:
            r_h0 = A.alloc(1)
            r_c = A.alloc(2)
            ctm = r_c.f(1024)
            P.dma("sp", _mk("dma_start", out=ctm[0:16], in_=st_lru[:, :]), writes=r_c.b, sembuf=r_c.b[0])
            h0fm = r_h0.f(8, 16)
            bt, bb = P.bank()
            for ch in range(8):
                MM(P, bt[:, ch * 16:(ch + 1) * 16], ctm[0:16, ch * 128:(ch + 1) * 128], ident[0:16, 0:16],
                   True, True, r_c.b + [b_cst], bb, ch == 7)
            CP(P, "dve", h0fm, view(bt[:, 0:128], 8, 16), [bb], r_h0.b)
            A.free(r_c)

        r_h = A.alloc(npg(8 * T * 4))
        hh = r_h.f(8, T)
        r_ga = A.alloc(npg(8 * T * 4))
        r_gi = A.alloc(npg(8 * T * 4))
        r_gm = A.alloc(npg(8 * T * 4))
        ga, gi, gm = r_ga.f(8, T), r_gi.f(8, T), r_gm.f(8, T)
        for ch in range(8):
            ta, ti_, tm_ = ga[:, ch, :], gi[:, ch, :], gm[:, ch, :]
            ab, ibf, mb = r_ga.sub(ch, T * 4), r_gi.sub(ch, T * 4), r_gm.sub(ch, T * 4)
            xb_ = r_xr.sub(ch, T * 4)
            btr, bbr = P.bank()
            MM(P, btr[:, 0:T], lruw[:, 0, ch, :], xrb[:, ch, :], True, True, [b_lruw] + r_xrb.sub(ch, T * 2), bbr, True)
            bti, bbi = P.bank()
            MM(P, bti[:, 0:T], lruw[:, 1, ch, :], xrb[:, ch, :], True, True, [b_lruw] + r_xrb.sub(ch, T * 2), bbi, True)
            ACT(P, ta, btr[:, 0:T], AF.Tanh, [bbr, b_pq], ab, scale=0.5, bias=pqc(PQ_HBA + ch))
            ACT(P, ta, ta, AF.Exp, ab + [b_pq], ab, scale=pqc(PQ_CL + ch), bias=pqc(PQ_CL + ch))
            ACT(P, ti_, bti[:, 0:T], AF.Tanh, [bbi, b_pq], ibf, scale=0.5, bias=pqc(PQ_HBX + ch))
            STT(P, "dve", ti_, ti_, 1.0, xr[:, ch, :], ALU.add, ALU.mult, ibf + xb_, ibf)
            TT(P, "dve", tm_, ta, ta, ALU.mult, ab, mb)
            TS(P, "dve", tm_, tm_, -0.25, 0.25, ALU.mult, ALU.add, mb, mb)
        for ch in range(8):
            mb = r_gm.sub(ch, T * 4)
            ACT(P, gm[:, ch, :], gm[:, ch, :], AF.Sqrt, mb, mb)
        for ch in range(8):
            ta, ti_, tm_ = ga[:, ch, :], gi[:, ch, :], gm[:, ch, :]
            ab, ibf, mb = r_ga.sub(ch, T * 4), r_gi.sub(ch, T * 4), r_gm.sub(ch, T * 4)
            tu = tm_
            TT(P, "dve", tu, tm_, ti_, ALU.mult, mb + ibf, mb)
            if first_p:
                TS(P, "dve", tu[:, 0:1], ti_[:, 0:1], 0.5, None, ALU.mult, None, ibf, mb)
            if samp:
                a3 = view(ta, NB, L)
                u3 = view(tu, NB, L)
                TT(P, "dve", ti_[:, 0:16], a3[:, :, 0], h0fm[:, ch, :], ALU.mult, ab + r_h0.b + ibf, ibf)
                TT(P, "dve", u3[:, :, 0], u3[:, :, 0], ti_[:, 0:16], ALU.add, mb + ibf, mb)
                P.op("dve", _mk("memset", a3[:, :, 0], 0.0), [], ab)
                init = 0.0
                ib = []
            else:
                init = hcar[:, ch:ch + 1]
                ib = [b_hcar]
            hb = r_h.sub(ch, T * 4)
            P.op("dve", _mk("tensor_tensor_scan", out=hh[:, ch, :], data0=ta, data1=tu, initial=init,
                            op0=ALU.mult, op1=ALU.add), ab + mb + ib, hb)
            if not samp:
                CP(P, "pool", hcar[:, ch:ch + 1], hh[:, ch, T - 1:T], hb, [b_hcar])
        A.free(r_ga, r_gi, r_gm, r_xr, r_xrb)
        if samp:
            A.free(r_h0)
            r_c = A.alloc(1)
            hl = r_c.f(8, 16)
            CP(P, "dve", hl, hh.rearrange("p c (b l) -> p c b l", l=L)[:, :, :, 3], r_h.b, r_c.b)
            for half in range(2):
                bt, bb = P.bank()
                for j in range(4):
                    ch = half * 4 + j
                    MM(P, bt[0:16, j * 128:(j + 1) * 128], hl[:, ch, :], ident, True, True, r_c.b + [b_cst], bb, j == 3)
                CP(P, "dve", stg[0:16, half * 512:(half + 1) * 512], bt[0:16, :], [bb], r_tm.b)
            P.dma("sp", _mk("dma_start", out=o_slru[:, :], in_=stg[0:16, :]), reads=r_tm.b, sembuf=r_tm.b[0])
            A.free(r_c)
        if last_p:
            bt, bb = P.bank()
            MM(P, bt[0:8, 0:128], hcar[:, :], ident, True, True, [b_hcar, b_cst], bb, True)
            CP(P, "dve", stg[0:8, 0:128], bt[0:8, 0:128], [bb], r_tm.b)
            P.dma("sp", _mk("dma_start", out=o_plru[:, :], in_=stg[0:8, 0:128]), reads=r_tm.b, sembuf=r_tm.b[0])
        A.free(r_tm)
        r_yl = A.alloc(npg(8 * T * 2))
        ylru = r_yl.h(8, T)
        for ch in range(8):
            STT(P, "dve", ylru[:, ch, :], hh[:, ch, :], (1.0 if NATIVE_GELU_LRU else 0.5), gy[:, ch, :], ALU.mult, ALU.mult,
                r_h.sub(ch, T * 4) + r_gy.sub(ch, T * 4), r_yl.sub(ch, T * 2))
        A.free(r_h, r_gy)
        dbg("ylru%s%d" % (kind, ti), ylru, r_yl.b)

        P.tag = "%s%s%d" % ("xbc_", kind, ti)
        r_xa = A.alloc(npg(24 * T * 2))
        xact = r_xa.h(24, T)
        r_ext = A.alloc(6)
        r_cs = None
        xraw_s = None
        if samp:
            r_cs = A.alloc(3)
            cfm = r_cs.f(24, 48)
            r_c = A.alloc(6)
            ctm = r_c.f(3072)
            P.dma("sp", _mk("dma_start", out=ctm[0:48], in_=c_ssd[:, :]), writes=r_c.b, sembuf=r_c.b[0])
            for q in range(6):
                bt, bb = P.bank()
                for j in range(4):
                    ch = q * 4 + j
                    MM(P, bt[:, j * 48:(j + 1) * 48], ctm[0:48, ch * 128:(ch + 1) * 128], ident[0:48, 0:48],
                       True, True, r_c.b + [b_cst], bb, j == 3)
                CP(P, "dve", cfm[:, q * 4:q * 4 + 4, :], view(bt[:, 0:192], 4, 48), [bb], r_cs.b)
            A.free(r_c)
            r_xraw = A.alloc(3)
            xraw_s = r_xraw.f(24, 64)
        r_tc = A.alloc(npg(2 * T * 4))
        for cb in range(6):
            wi, wt, wb_ = WS.get((w_in, 0, 8, C_XBC + cb * 512, 512))
            for oc in range(4):
                ch = cb * 4 + oc
                slot = ch % 3
                eb = r_ext.b[slot * 2:slot * 2 + 2]
                ext = view(A.tile[:, (r_ext.lo + slot * 2) * 512:(r_ext.lo + slot * 2) * 512 + NB * (3 + L)], NB, 3 + L)
                bt, bb = P.bank()
                for kc in range(8):
                    MM(P, bt[:, 0:T], wt[:, kc, oc * 128:(oc + 1) * 128], xT[:, kc, :], kc == 0, kc == 7,
                       [wb_] + r_xT.b, bb, kc == 7)
                if samp:
                    CP(P, "pool", ext[:, :, 0:3], view(cfm[:, ch, :], NB, 3), r_cs.b, eb)
                    CP(P, "dve", xraw_s[:, ch, :], bt[:, 0:T], [bb], r_xraw.b)
                else:
                    CP(P, "pool", ext[:, 0, 0:3], halo_x[:, ch, :], [b_halo_x], eb)
                CP(P, "act", ext[:, :, 3:3 + L], view(bt[:, 0:T], NB, L), [bb], eb)
                if not samp:
                    CP(P, "pool", halo_x[:, ch, :], ext[:, 0, L:L + 3], eb, [b_halo_x])
                if last_p:
                    pass
                tcb = [r_tc.b[ch % 2]] if T == 512 else r_tc.b
                tc = view(r_tc.f(2, T)[:, ch % 2, :], NB, L)
                ACT(P, tc, ext[:, :, 0:L], AF.Identity, eb + [b_pq], tcb,
                    scale=pqc(PQ_SCW + 0 * 24 + ch), bias=pqc(PQ_SCB + ch))
                for k in range(1, 4):
                    STT(P, "dve", tc, ext[:, :, k:k + L], pqc(PQ_SCW + k * 24 + ch), tc, ALU.mult, ALU.add,
                        eb + tcb + [b_pq], tcb)
                if last_p:
                    if oc == 0:
                        pbt, pbb = P.bank()
                    MM(P, pbt[:, oc * 128:(oc + 1) * 128], ext[:, 0, 3 + L - 128:3 + L], ident, True, True,
                       eb + [b_cst], pbb, oc == 3)
                    if oc == 3:
                        if cb == 0:
                            r_pb = A.alloc(6)
                        CP(P, "dve", r_pb.f(3072)[:, cb * 512:(cb + 1) * 512], pbt[:, :], [pbb], r_pb.b)
                if NATIVE_SILU:
                    ACT(P, view(xact[:, ch, :], NB, L), tc, AF.Silu, tcb, r_xa.sub(ch, T * 2), scale=2.0)
                else:
                    r_ts = A.alloc(1)
                    ts_ = view(r_ts.f(T), NB, L)
                    ACT(P, ts_, tc, AF.Tanh, tcb, r_ts.b)
                    STT(P, "dve", view(xact[:, ch, :], NB, L), ts_, 1.0, tc, ALU.add, ALU.mult, r_ts.b + tcb,
                        r_xa.sub(ch, T * 2))
                    A.free(r_ts)
            WS.release(wi)
        A.free(r_tc, r_ext)
        if last_p:
            P.dma("sp", _mk("dma_start", out=o_pssdb[:, :], in_=r_pb.f(3072)[125:128, :]), reads=r_pb.b, sembuf=r_pb.b[0])
            A.free(r_pb)
        if samp:
            A.free(r_cs)
            r_st = A.alloc(6)
            stm = r_st.f(3072)
            for q in range(6):
                bt, bb = P.bank()
                for j in range(4):
                    ch = q * 4 + j
                    MM(P, bt[0:64, j * 128:(j + 1) * 128], xraw_s[:, ch, :], ident, True, True, r_xraw.b + [b_cst], bb, j == 3)
                CP(P, "dve", stm[0:64, q * 512:(q + 1) * 512], bt[0:64, :], [bb], r_st.b)
            for l in range(1, 4):
                P.dma("sp", _mk("dma_start", out=o_sssdb[:, l - 1, :], in_=stm[l:64:4, :]),
                      reads=r_st.b, sembuf=r_st.b[0])
            A.free(r_st, r_xraw)
        dbg("xact%s%d" % (kind, ti), xact, r_xa.b)

        P.tag = "%s%s%d" % ("dt_", kind, ti)
        r_dt = A.alloc(4)
        dtf = r_dt.f(4, 512)
        dt_ = dtf[0:32, 0, 0:T]
        acs = dtf[0:32, 1, 0:T]
        rev = dtf[0:32, 2, 0:T]
        tmpd = dtf[0:32, 3, 0:T]
        r_hl = A.alloc(1)
        hilo = r_hl.h(2, 512)
        wi, wt, wb_ = WS.get((w_in, 0, 8, C_DT, 32))
        bt, bb = P.bank()
        for kc in range(8):
            MM(P, bt[0:32, 0:T], wt[:, kc, 0:32], xT[:, kc, :], kc == 0, kc == 7, [wb_] + r_xT.b, bb, kc == 7)
        WS.release(wi)
        dbs = r_dt.b
        TS(P, "dve", dt_, bt[0:32, 0:T], pp[0:32, PP_DTB:PP_DTB + 1], None, ALU.add, None, [bb, b_pp], dbs)
        ACT(P, tmpd, dt_, AF.Abs, dbs, dbs)
        ACT(P, tmpd, tmpd, AF.Exp, dbs, dbs, scale=-1.0)
        ACT(P, tmpd, tmpd, AF.Ln, dbs, dbs, bias=1.0)
        STT(P, "dve", dt_, dt_, 0.0, tmpd, ALU.max, ALU.add, dbs, dbs)
        TS(P, "dve", tmpd, dt_, pq[0:32, PQ_A:PQ_A + 1], None, ALU.mult, None, dbs + [b_pq], dbs)
        if samp:
            P.op("dve", _mk("tensor_tensor_scan", out=acs, data0=rmask, data1=tmpd, initial=0.0,
                                                       op0=ALU.mult, op1=ALU.add), dbs + [b_cst], dbs)
            a3 = view(acs, NB, L)
            TT(P, "dve", view(rev, NB, L), a3[:, :, L - 1:L].to_broadcast([32, NB, L]), a3, ALU.subtract, dbs, dbs)
        else:
            for c in range(NCH):
                sl = slice(c * CL, (c + 1) * CL)
                P.op("dve", _mk("tensor_tensor_scan", out=acs[:, sl], data0=ones32[:, 0:CL], data1=tmpd[:, sl],
                                                                   initial=0.0, op0=ALU.mult, op1=ALU.add),
                     dbs + [b_cst], dbs)
                TS(P, "dve", rev[:, sl], acs[:, sl], -1.0, acs[:, (c + 1) * CL - 1:(c + 1) * CL], ALU.mult, ALU.add, dbs, dbs)
        CP(P, "act", hilo[0:32, 0, 0:T], acs, dbs, r_hl.b)
        TT(P, "dve", hilo[0:32, 1, 0:T], acs, hilo[0:32, 0, 0:T], ALU.subtract, dbs + r_hl.b, r_hl.b)

        P.tag = "%s%s%d" % ("z_", kind, ti)
        r_z = A.alloc(npg(NCH * 4096))
        ztm = r_z.h(NCH, 2048)
        r_zt = A.alloc(2)
        for cb in range(4):
            wi, wt, wb_ = WS.get((w_in, 0, 8, C_Z + cb * 512, 512))
            for c in range(NCH):
                bt, bb = P.bank()
                for kc in range(8):
                    MM(P, bt[0:CL, :], xT[:, kc, c * CL:(c + 1) * CL], wt[:, kc, :], kc == 0, kc == 7,
                       [wb_] + r_xT.b, bb, kc == 7)
                zt = r_zt.f(2, 512)[0:CL, (cb * NCH + c) % 2, :]
                zb = [r_zt.b[(cb * NCH + c) % 2]]
                if NATIVE_SILU:
                    ACT(P, ztm[0:CL, c, cb * 512:(cb + 1) * 512], bt[0:CL, :], AF.Silu, [bb], r_z.sub(c, 4096))
                else:
                    ACT(P, zt, bt[0:CL, :], AF.Tanh, [bb], zb, scale=0.5)
                    STT(P, "dve", ztm[0:CL, c, cb * 512:(cb + 1) * 512], zt, 1.0, bt[0:CL, :], ALU.add, ALU.mult,
                        zb + [bb], r_z.sub(c, 4096))
            WS.release(wi)
        A.free(r_zt)

        P.tag = "%s%s%d" % ("s4a_", kind, ti)
        r_tsg = A.alloc(npg(8 * T * 2))
        tss = r_tsg.h(8, T)
        r_m2 = A.alloc(npg(8 * T * 2))
        m2 = r_m2.h(8, T)
        r_t4 = A.alloc(4)
        tl4 = r_t4.f(4, 512)
        for cb in range(2):
            wi, wt, wb_ = WS.get((w_in, 0, 8, C_GS + cb * 512, 512))
            for oc in range(4):
                bt, bb = P.bank()
                for kc in range(8):
                    MM(P, bt[:, 0:T], wt[:, kc, oc * 128:(oc + 1) * 128], xT[:, kc, :], kc == 0, kc == 7,
                       [wb_] + r_xT.b, bb, kc == 7)
                ACT(P, tss[:, cb * 4 + oc, :], bt[:, 0:T], AF.Tanh, [bb, b_pq], r_tsg.sub(cb * 4 + oc, T * 2), scale=0.5,
                    bias=pqc(PQ_HBG + cb * 4 + oc))
            WS.release(wi)
            wi, wt, wb_ = WS.get((w_in, 0, 8, C_GL + cb * 512, 512))
            for oc in range(4):
                bt, bb = P.bank()
                for kc in range(8):
                    MM(P, bt[:, 0:T], wt[:, kc, oc * 128:(oc + 1) * 128], xT[:, kc, :], kc == 0, kc == 7,
                       [wb_] + r_xT.b, bb, kc == 7)
                ACT(P, tl4[:, oc, 0:T], bt[:, 0:T], AF.Tanh, [bb, b_pq], [r_t4.b[oc]], scale=0.5,
                    bias=pqc(PQ_HBG + 8 + cb * 4 + oc))
            WS.release(wi)
            wi, wt, wb_ = WS.get((w_lo, 0, 8, cb * 512, 512))
            for oc in range(4):
                bt, bb = P.bank()
                for kc in range(8):
                    MM(P, bt[:, 0:T], wt[:, kc, oc * 128:(oc + 1) * 128], ylru[:, kc, :], kc == 0, kc == 7,
                       [wb_] + r_yl.b, bb, kc == 7)
                STT(P, "dve", m2[:, cb * 4 + oc, :], tl4[:, oc, 0:T], 1.0, bt[:, 0:T], ALU.add, ALU.mult,
                    [r_t4.b[oc], bb], r_m2.sub(cb * 4 + oc, T * 2))
            WS.release(wi)
        A.free(r_t4, r_xT, r_yl)
        P.tag = "%s%s%d" % ("core_", kind, ti)
        P.tag = "%s%s%d" % ("core_", kind, ti)
        r_ys = A.alloc(npg(16 * T * 2))
        yssd = r_ys.h(16, T)

        def ssd_front(c):
            X = {}
            tok = slice(c * CL, (c + 1) * CL)
            r_sm = A.alloc(1)
            sm = r_sm.f(4, 32)
            bt, bb = P.bank()
            for q, srcq in enumerate((dt_, acs, rev)):
                MM(P, bt[0:CL, q * 32:(q + 1) * 32], srcq[:, tok], ident[0:32, 0:32], True, True, dbs + [b_cst], bb, q == 2)
            dt_tm = sm[0:CL, 0, :]
            ea_tm = sm[0:CL, 1, :]
            dte_tm = sm[0:CL, 2, :]
            nacs_tm = sm[0:CL, 3, :]
            CP(P, "act", dt_tm, bt[0:CL, 0:32], [bb], r_sm.b)
            ACT(P, ea_tm, bt[0:CL, 32:64], AF.Exp, [bb], r_sm.b)
            ACT(P, dte_tm, bt[0:CL, 64:96], AF.Exp, [bb], r_sm.b)
            TS(P, "dve", nacs_tm, bt[0:CL, 32:64], -1.0, None, ALU.mult, None, [bb], r_sm.b)
            r_xdt = A.alloc(2)
            r_xdte = A.alloc(2)
            xdt = r_xdt.h(2048)[0:CL]
            xdte = r_xdte.h(2048)[0:CL]
            for q in range(4):
                bt, bb = P.bank()
                for j in range(4):
                    jj = q * 4 + j
                    MM(P, bt[0:CL, j * 128:(j + 1) * 128], xact[:, jj, tok], identb, True, True,
                       r_xa.sub(jj, T * 2) + [b_cbf], bb, j == 3)
                TT(P, "dve", view(xdt[:, q * 512:(q + 1) * 512], 8, 64), view(bt[0:CL, :], 8, 64),
                   dt_tm[:, q * 8:(q + 1) * 8].unsqueeze(2).to_broadcast([CL, 8, 64]), ALU.mult,
                   [bb] + r_sm.b, r_xdt.b)
            dte_b = r_sm.h(16, 32)[0:CL, 8, :]
            CP(P, "dve", dte_b, dte_tm, r_sm.b, r_sm.b)
            TT(P, "dve", view(xdte, 32, 64), view(xdt, 32, 64), dte_b.unsqueeze(2).to_broadcast([CL, 32, 64]),
               ALU.mult, r_xdt.b + r_sm.b, r_xdte.b)
            r_b = A.alloc(2)
            Btm = r_b.h(2, 1024)[0:CL, 0, 0:512]
            CBT = r_b.h(2, 1024)[0:CL, 1, 0:4 * CL]
            bt, bb = P.bank()
            for g in range(4):
                MM(P, bt[0:CL, g * 128:(g + 1) * 128], xact[:, 16 + g, tok], identb, True, True,
                   r_xa.sub(16 + g, T * 2) + [b_cbf], bb, g == 3)
            CP(P, "act", Btm, bt[0:CL, :], [bb], [r_b.b[0]])
            bt, bb = P.bank()
            for g in range(4):
                MM(P, bt[0:CL, g * CL:(g + 1) * CL], xact[:, 16 + g, tok], xact[:, 20 + g, tok], True, True,
                   r_xa.sub(16 + g, T * 2) + r_xa.sub(20 + g, T * 2), bb, g == 3)
            CP(P, "act", CBT, bt[0:CL, 0:4 * CL], [bb], [r_b.b[1]])
            r_cd = A.alloc(1)
            if not samp:
                Rm = r_cd.f(2, 256)[0:32, 0, 0:16]
                cdcol = r_cd.f(2, 256)[:, 1, 0:16]
                TS(P, "dve", Rm, mask2, acs[:, (c + 1) * CL - 1:(c + 1) * CL], None, ALU.mult, None, dbs + [b_cst], r_cd.b)
                bt, bb = P.bank()
                MM(P, bt[:, 0:16], selpar, Rm, True, True, r_cd.b + [b_cst], bb, True)
                ACT(P, cdcol, bt[:, 0:16], AF.Exp, [bb], r_cd.b)
            else:
                Rm = r_cd.f(2, 256)[0:32, 0, :]
                cdcol = r_cd.f(2, 256)[:, 1, :]
                a3 = view(acs, NB, L)
                TT(P, "dve", view(Rm, 16, 16), mask2.unsqueeze(1).to_broadcast([32, 16, 16]),
                   a3[:, :, L - 1:L].to_broadcast([32, 16, 16]), ALU.mult, dbs + [b_cst], r_cd.b)
                bt, bb = P.bank()
                MM(P, bt[:, 0:256], selpar, Rm, True, True, r_cd.b + [b_cst], bb, True)
                ACT(P, cdcol, bt[:, 0:256], AF.Exp, [bb], r_cd.b)
            have_off = samp or not (first_p and c == 0)
            r_ht = None
            HT = None
            yo_banks = None
            if not samp:
                if have_off:
                    r_ht = A.alloc(2)
                    HT = r_ht.h(2048)
                    r_hb = A.alloc(2)
                    Hb16 = r_hb.h(16, 128)
                    CP(P, "act", Hb16[:, 0:8, :], Hst[:, 0:8, :], [b_H], [r_hb.b[0]])
                    CP(P, "act", Hb16[:, 8:16, :], Hst[:, 8:16, :], [b_H], [r_hb.b[1]])
                    for q in range(4):
                        bt, bb = P.bank()
                        for j in range(4):
                            jj = q * 4 + j
                            MM(P, bt[:, j * 128:(j + 1) * 128], Hb16[:, jj, :], identb, True, True,
                               [r_hb.b[jj // 8], b_cbf], bb, j == 3)
                        CP(P, "act", HT[:, q * 512:(q + 1) * 512], bt[:, :], [bb], r_ht.b)
                    A.free(r_hb)
                for q in range(4):
                    bt, bb = P.bank()
                    for j in range(4):
                        jj = q * 4 + j
                        g = jj // 4
                        MM(P, bt[:, j * 128:(j + 1) * 128], xdte[:, jj * 128:(jj + 1) * 128], Btm[:, g * 128:(g + 1) * 128],
                           True, True, r_xdte.b + [r_b.b[0]], bb, j == 3)
                    for j in range(4):
                        jj = q * 4 + j
                        STT(P, "dve", Hst[:, jj, :], Hst[:, jj, :], cdcol[:, jj:jj + 1], bt[:, j * 128:(j + 1) * 128],
                            ALU.mult, ALU.add, [b_H, bb] + r_cd.b, [b_H])
            else:
                yo_banks = []
                for g in range(4):
                    bk = P.bank()
                    yo_banks.append(bk)
                    P.reserved.add(P.banks.index(bk))
                r_ctm = A.alloc(4)
                r_bm = A.alloc(8)
                for g in range(4):
                    TT(P, "dve", r_ctm.h(4, 16, 64)[:, g], xact[:, 20 + g, :].unsqueeze(1).to_broadcast([128, 16, 64]),
                       bmaskT, ALU.mult, r_xa.sub(20 + g, T * 2) + [b_cst], [r_ctm.b[g]])
                    TT(P, "dve", r_bm.h(4, 16, 128)[0:64, g], Btm[:, g * 128:(g + 1) * 128].unsqueeze(1).to_broadcast([64, 16, 128]),
                       bmask.unsqueeze(2).to_broadcast([64, 16, 128]), ALU.mult, [r_b.b[0], b_cst], r_bm.b[g * 2:g * 2 + 2])
                r_hs = A.alloc(4 * NHS)
                r_hts = A.alloc(2 * NHT)
                r_hbs = A.alloc(2 * NHT)
                for b in range(SB_):
                    hs_b = r_hs.b[(b % NHS) * 4:(b % NHS) * 4 + 4]
                    Hb = r_hs.f(NHS, 16, 128)[:, b % NHS]
                    P.dma("sp", _mk("dma_start", out=Hb, in_=st_ssd[b].rearrange("j q n -> q j n")),
                          writes=hs_b, sembuf=hs_b[0], nbytes=5 << 19)
                    ht_b = r_hts.b[(b % NHT) * 2:(b % NHT) * 2 + 2]
                    HTb = r_hts.h(NHT, 2048)[:, b % NHT]
                    hb_b = r_hbs.b[(b % NHT) * 2:(b % NHT) * 2 + 2]
                    Hbb = r_hbs.h(NHT, 16, 128)[:, b % NHT]
                    CP(P, "act", Hbb[:, 0:8, :], Hb[:, 0:8, :], hs_b, [hb_b[0]])
                    CP(P, "act", Hbb[:, 8:16, :], Hb[:, 8:16, :], hs_b, [hb_b[1]])
                    for q in range(4):
                        bt, bb = P.bank()
                        for j in range(4):
                            jj = q * 4 + j
                            MM(P, bt[:, j * 128:(j + 1) * 128], Hbb[:, jj, :], identb, True, True,
                               [hb_b[jj // 8], b_cbf], bb, j == 3)
                        CP(P, "dve" if q == 3 else "act", HTb[:, q * 512:(q + 1) * 512], bt[:, :], [bb], ht_b)
                    for g in range(4):
                        MM(P, yo_banks[g][0][0:64, :], r_ctm.h(4, 16, 64)[:, g, b, :], HTb[:, g * 512:(g + 1) * 512],
                           b == 0, b == SB_ - 1, [r_ctm.b[g]] + ht_b, yo_banks[g][1], b == SB_ - 1 or g == 3)
                    for q in range(4):
                        bt, bb = P.bank()
                        for j in range(4):
                            jj = q * 4 + j
                            g = jj // 4
                            MM(P, bt[:, j * 128:(j + 1) * 128], xdte[:, jj * 128:(jj + 1) * 128],
                               r_bm.h(4, 16, 128)[0:64, g, b, :], True, True, r_xdte.b + r_bm.b[g * 2:g * 2 + 2], bb, j == 3)
                        for j in range(4):
                            jj = q * 4 + j
                            STT(P, "dve", Hb[:, jj, :], Hb[:, jj, :], cdcol[:, b * 16 + jj:b * 16 + jj + 1],
                                bt[:, j * 128:(j + 1) * 128], ALU.mult, ALU.add, hs_b + r_cd.b + [bb], hs_b)
                    P.dma("sp", _mk("dma_start", out=o_sssd[b].rearrange("j q n -> q j n"), in_=Hb),
                          reads=hs_b, sembuf=hs_b[0], nbytes=5 << 19)
                A.free(r_hs, r_hts, r_hbs, r_ctm, r_bm)
            A.free(r_xdte, r_cd)
            X.update(tok=tok, r_sm=r_sm, ea_tm=ea_tm, nacs_tm=nacs_tm, r_xdt=r_xdt, xdt=xdt, r_b=r_b, CBT=CBT,
                     r_ht=r_ht, HT=HT, yo_banks=yo_banks, have_off=have_off)
            return X

        def ssd_back(c, X):
            tok = X["tok"]
            r_sm, ea_tm, nacs_tm, r_xdt, xdt, r_b, CBT = X["r_sm"], X["ea_tm"], X["nacs_tm"], X["r_xdt"], X["xdt"], X["r_b"], X["CBT"]
            r_ht, HT, yo_banks, have_off = X["r_ht"], X["HT"], X["yo_banks"], X["have_off"]
            r_y = A.alloc(4)
            ytm = r_y.f(2048)[0:CL]
            r_dec = A.alloc(2)
            r_mt = A.alloc(2)
            r_yo = A.alloc(2)
            for g in range(4):
                par = g % 2
                dec = r_dec.h(2, 8, 128)[0:CL, par, :, 0:CL]
                decb = [r_dec.b[par]]
                MT = r_mt.h(2, 8, 128)[0:CL, par, :, 0:CL]
                mtb = [r_mt.b[par]]
                for hq in range(2):
                    bt, bb = P.bank()
                    MM(P, bt[0:CL, 0:4 * CL], identb[0:CL, 0:CL], (negs4 if samp else negm4), True, False, [b_cbf], bb, False)
                    for i4 in range(4):
                        h = g * 8 + hq * 4 + i4
                        MM(P, bt[0:CL, i4 * CL:(i4 + 1) * CL], selb[:, h, 0:CL], hilo[0:32, 0, tok], False, False,
                           [b_cbf] + r_hl.b, bb, False)
                        MM(P, bt[0:CL, i4 * CL:(i4 + 1) * CL], selb[:, h, 0:CL], hilo[0:32, 1, tok], False, True,
                           [b_cbf] + r_hl.b, bb, i4 == 3)
                    for i4 in range(4):
                        h = g * 8 + hq * 4 + i4
                        ACT(P, dec[:, hq * 4 + i4, :], bt[0:CL, i4 * CL:(i4 + 1) * CL], AF.Exp, [bb] + r_sm.b, decb,
                            bias=nacs_tm[:, h:h + 1])
                TT(P, "dve", MT, dec, CBT[:, g * CL:(g + 1) * CL].unsqueeze(1).to_broadcast([CL, 8, CL]), ALU.mult,
                   decb + [r_b.b[1]], mtb)
                yt, yb = P.bank()
                for i8 in range(8):
                    h = g * 8 + i8
                    MM(P, yt[0:CL, i8 * 64:(i8 + 1) * 64], MT[:, i8, :], xdt[:, h * 64:(h + 1) * 64], i8 == 0, False,
                       mtb + r_xdt.b, yb, False)
                for j4 in range(4):
                    jj = g * 4 + j4
                    MM(P, yt[0:CL, j4 * 128:(j4 + 1) * 128], xact[:, jj, tok], diagD[:, jj, :], False, True,
                       r_xa.sub(jj, T * 2) + [b_diagD], yb, j4 == 3)
                yo = r_yo.f(2, 512)[0:CL, par]
                yob = [r_yo.b[par]]
                if have_off:
                    if samp:
                        ot, ob = yo_banks[g]
                    else:
                        ot, ob = P.bank()
                        MM(P, ot[0:CL, :], xact[:, 20 + g, tok], HT[:, g * 512:(g + 1) * 512], True, True,
                           r_xa.sub(20 + g, T * 2) + r_ht.b, ob, True)
                    TT(P, "dve", view(yo, 8, 64), view(ot[0:CL, :], 8, 64),
                       ea_tm[:, g * 8:(g + 1) * 8].unsqueeze(2).to_broadcast([CL, 8, 64]), ALU.mult, [ob] + r_sm.b, yob)
                    TT(P, "dve", ytm[:, g * 512:(g + 1) * 512], yt[0:CL, :], yo, ALU.add, [yb] + yob, [r_y.b[g]])
                else:
                    CP(P, "act", ytm[:, g * 512:(g + 1) * 512], yt[0:CL, :], [yb], [r_y.b[g]])
            if samp:
                for g in range(4):
                    P.reserved.discard(P.banks.index(yo_banks[g]))
            A.free(r_dec, r_mt, r_yo, r_xdt, r_b, r_sm)
            if r_ht is not None:
                A.free(r_ht)
            dbg("ytm%s%d_%d" % (kind, ti, c), ytm, r_y.b)
            r_g = A.alloc(1)
            stt_ = r_g.f(8)[0:CL]
            TT(P, "dve", ytm, ytm, ztm[0:CL, c, :], ALU.mult, r_y.b + r_z.sub(c, 4096), r_y.b)
            r_sq = A.alloc(2)
            ACT(P, r_sq.h(2048)[0:CL], ytm, AF.Square, r_y.b, r_sq.b + r_g.b, accum=stt_[:, 0:1])
            A.free(r_sq)
            TS(P, "dve", stt_[:, 1:2], stt_[:, 0:1], 1.0 / 2048, (1.0 if NATIVE_SILU else 4.0) * RMS_EPS, ALU.mult, ALU.add,
               r_g.b, r_g.b)
            ACT(P, stt_[:, 1:2], stt_[:, 1:2], AF.Sqrt, r_g.b, r_g.b)
            P.op("dve", _mk("reciprocal", out=stt_[:, 2:3], in_=stt_[:, 1:2]), r_g.b, r_g.b)
            r_gn = A.alloc(2)
            gyn = r_gn.h(2048)[0:CL]
            TS(P, "dve", gyn, ytm, stt_[:, 2:3], None, ALU.mult, None, r_y.b + r_g.b, r_gn.b)
            A.free(r_y)
            for q in range(4):
                bt, bb = P.bank()
                for j in range(4):
                    jj = q * 4 + j
                    MM(P, bt[:, j * CL:(j + 1) * CL], gyn[:, jj * 128:(jj + 1) * 128], identb[0:CL, 0:CL], True, True,
                       r_gn.b + [b_cbf], bb, j == 3)
                TT(P, "dve", yssd[:, q * 4:q * 4 + 4, tok], view(bt[:, 0:4 * CL], 4, CL),
                   ppc(PP_NG + q * 4, 4).unsqueeze(2).to_broadcast([128, 4, CL]), ALU.mult, [bb, b_pp], r_ys.b)
            A.free(r_g, r_gn)

        Xc = ssd_front(0)
        for c in range(NCH):
            Xn = ssd_front(c + 1) if c + 1 < NCH else None
            ssd_back(c, Xc)
            Xc = Xn
        if last_p:
            P.dma("sp", _mk("dma_start", out=o_pssd.rearrange("j q n -> q j n"), in_=Hst[:]), reads=[b_H], sembuf=b_H)
        A.free(r_dt, r_hl, r_z, r_xa)
        dbg("yssd%s%d" % (kind, ti), yssd, r_ys.b)

        P.tag = "%s%s%d" % ("s4_", kind, ti)
        r_x, x_tm = load_x()
        r_mg = A.alloc(npg(8 * T * 2))
        mg = r_mg.h(8, T)
        r_t4 = A.alloc(4)
        m14 = r_t4.f(4, 512)
        for cb in range(2):
            wi0, wt0, wb0 = WS.get((w_so, 0, 8, cb * 512, 512))
            wi1, wt1, wb1 = WS.get((w_so, 1024, 8, cb * 512, 512))
            for oc in range(4):
                ch = cb * 4 + oc
                bt, bb = P.bank()
                for kc in range(16):
                    wt_, wbb = (wt0, wb0) if kc < 8 else (wt1, wb1)
                    MM(P, bt[:, 0:T], wt_[:, kc % 8, oc * 128:(oc + 1) * 128], yssd[:, kc, :], kc == 0, kc == 15,
                       [wbb] + r_ys.b, bb, kc == 15)
                STT(P, "dve", m14[:, oc, 0:T], tss[:, ch, :], 1.0, bt[:, 0:T], ALU.add, ALU.mult,
                    r_tsg.sub(ch, T * 2) + [bb], [r_t4.b[oc]])
                TT(P, "dve", mg[:, ch, :], m14[:, oc, 0:T], m2[:, ch, :], ALU.add,
                   [r_t4.b[oc]] + r_m2.sub(ch, T * 2), r_mg.sub(ch, T * 2))
            WS.release(wi0)
            WS.release(wi1)
        A.free(r_t4, r_ys, r_tsg, r_m2)
        dbg("mg%s%d" % (kind, ti), mg, r_mg.b)

        P.tag = "%s%s%d" % ("ln1_", kind, ti)
        r_x1 = A.alloc(npg(NCH * 4096))
        x1 = r_x1.f(NCH, 1024)
        r_x1T = A.alloc(npg(8 * T * 2))
        x1T = r_x1T.h(8, T)
        r_st = A.alloc(1)
        stt4 = r_st.f(8, NCH)
        r_scr = A.alloc(2)
        scr = r_scr.f(1024)[0:CL]
        r_xb = A.alloc(2)
        r_bc1, bct1 = load_bc(0)
        wia, wta, wba = WS.get((w_o, 0, 8, 0, 512))
        wib, wtb, wbb_ = WS.get((w_o, 0, 8, 512, 512))
        for c in range(NCH):
            xb = r_x.sub(c, 4096)
            for cb, (wt, wb_) in enumerate(((wta, wba), (wtb, wbb_))):
                bt, bb = P.bank()
                for kc in range(8):
                    MM(P, bt[0:CL, :], mg[:, kc, c * CL:(c + 1) * CL], wt[:, kc, :], kc == 0, kc == 7,
                       [wb_] + r_mg.sub(kc, T * 2), bb, kc == 7)
                xs_ = x_tm[0:CL, c, cb * 512:(cb + 1) * 512]
                STT(P, "dve", xs_, bt[0:CL, :], 0.5 / ALPHA, xs_, ALU.mult, ALU.add, [bb] + xb, xb)
            ln_stats(x_tm[0:CL, c, :], CL, c, xb, stt4, r_st.b[0], scr, r_scr.b)
        ln_rstd(CL, stt4, r_st.b[0])
        for c in range(NCH):
            xb = r_x.sub(c, 4096)
            x1b = r_x1.sub(c, 4096)
            ln_apply(x_tm[0:CL, c, :], x1[0:CL, c, :], CL, c, xb, x1b, stt4, r_st.b[0], r_bc1, bct1, scr, r_scr.b)
            xbf = r_xb.h(2, 1024)[0:CL, c % 2]
            xbfb = [r_xb.b[c % 2]]
            CP(P, "act", xbf, x1[0:CL, c, :], x1b, xbfb)
            for half in range(2):
                bt, bb = P.bank()
                for j in range(4):
                    kc = half * 4 + j
                    MM(P, bt[:, j * CL:(j + 1) * CL], xbf[:, kc * 128:(kc + 1) * 128], identb[0:CL, 0:CL], True, True,
                       xbfb + [b_cbf], bb, j == 3)
                CP(P, "act" if half == 0 else "dve", x1T[:, half * 4:half * 4 + 4, c * CL:(c + 1) * CL],
                   view(bt[:, 0:4 * CL], 4, CL), [bb], r_x1T.b)
        WS.release(wia)
        WS.release(wib)
        A.free(r_x, r_mg, r_xb, r_bc1)
        dbg("x1T%s%d" % (kind, ti), x1T, r_x1T.b)

        P.tag = "%s%s%d" % ("ffn_", kind, ti)
        r_hT = A.alloc(npg(24 * T * 2))
        hT = r_hT.h(24, T)
        r_ef = A.alloc(6)
        r_fc = A.alloc(npg(4 * T * 4))
        r_cf = None
        fraw_s = None
        if samp:
            r_cf = A.alloc(2)
            cff = r_cf.f(24, 32)
            r_c = A.alloc(6)
            ctm = r_c.f(3072)
            P.dma("sp", _mk("dma_start", out=ctm[0:32], in_=c_ffn[:, :]), writes=r_c.b, sembuf=r_c.b[0])
            for q in range(6):
                bt, bb = P.bank()
                for j in range(4):
                    ch = q * 4 + j
                    MM(P, bt[:, j * 32:(j + 1) * 32], ctm[0:32, ch * 128:(ch + 1) * 128], ident[0:32, 0:32],
                       True, True, r_c.b + [b_cst], bb, j == 3)
                CP(P, "dve", cff[:, q * 4:q * 4 + 4, :], view(bt[:, 0:128], 4, 32), [bb], r_cf.b)
            A.free(r_c)
            r_fraw = A.alloc(3)
            fraw_s = r_fraw.f(24, 64)
        if last_p:
            r_pb = A.alloc(6)
        for fb in range(6):
            wig, wtg, wbg = WS.get((w_fg, 0, 8, fb * 512, 512))
            wiu, wtu, wbu = WS.get((w_fu, 0, 8, fb * 512, 512))
            for oc in range(4):
                ch = fb * 4 + oc
                slot = ch % 3
                eb = r_ef.b[slot * 2:slot * 2 + 2]
                ext = view(A.tile[:, (r_ef.lo + slot * 2) * 512:(r_ef.lo + slot * 2) * 512 + NB * (2 + L)], NB, 2 + L)
                bt, bb = P.bank()
                for kc in range(8):
                    MM(P, bt[:, 0:T], wtg[:, kc, oc * 128:(oc + 1) * 128], x1T[:, kc, :], kc == 0, kc == 7,
                       [wbg] + r_x1T.b, bb, kc == 7)
                ut, ub = P.bank()
                for kc in range(8):
                    MM(P, ut[:, 0:T], wtu[:, kc, oc * 128:(oc + 1) * 128], x1T[:, kc, :], kc == 0, kc == 7,
                       [wbu] + r_x1T.b, ub, kc == 7)
                if samp:
                    CP(P, "pool", ext[:, :, 0:2], view(cff[:, ch, :], NB, 2), r_cf.b, eb)
                    CP(P, "dve", fraw_s[:, ch, :], bt[:, 0:T], [bb], r_fraw.b)
                else:
                    CP(P, "pool", ext[:, 0, 0:2], halo_f[:, ch, :], [b_halo_f], eb)
                CP(P, "act", ext[:, :, 2:2 + L], view(bt[:, 0:T], NB, L), [bb], eb)
                if not samp:
                    CP(P, "pool", halo_f[:, ch, :], ext[:, 0, L:L + 2], eb, [b_halo_f])
                if last_p:
                    if oc == 0:
                        pbt, pbb = P.bank()
                    MM(P, pbt[:, oc * 128:(oc + 1) * 128], ext[:, 0, 2 + L - 128:2 + L], ident, True, True,
                       eb + [b_cst], pbb, oc == 3)
                    if oc == 3:
                        CP(P, "dve", r_pb.f(3072)[:, fb * 512:(fb + 1) * 512], pbt[:, :], [pbb], r_pb.b)
                par = ch % 2
                fcb = [r_fc.b[par]] if T == 512 else r_fc.b
                ftb = [r_fc.b[2 + par]] if T == 512 else r_fc.b
                fc2 = r_fc.f(4, T)[:, par, :]
                ft2 = r_fc.f(4, T)[:, 2 + par, :]
                fc = view(fc2, NB, L)
                ACT(P, fc, ext[:, :, 0:L], AF.Identity, eb + [b_pp], fcb,
                    scale=ppc(PP_FCW + 0 * 24 + ch), bias=ppc(PP_FCB + ch))
                for k in range(1, 3):
                    STT(P, "dve", fc, ext[:, :, k:k + L], ppc(PP_FCW + k * 24 + ch), fc, ALU.mult, ALU.add,
                        eb + fcb + [b_pp], fcb)
                if NATIVE_GELU:
                    ACT(P, fc2, fc2, AF.Gelu_apprx_tanh, fcb, fcb)
                    TT(P, "dve", hT[:, ch, :], ut[:, 0:T], fc2, ALU.mult, [ub] + fcb, r_hT.sub(ch, T * 2))
                else:
                    gelu2(fc2, fc2, ft2, fcb, fcb, ftb)
                    STT(P, "dve", hT[:, ch, :], ut[:, 0:T], 0.5, fc2, ALU.mult, ALU.mult, [ub] + fcb, r_hT.sub(ch, T * 2))
            WS.release(wig)
            WS.release(wiu)
        A.free(r_ef, r_fc, r_x1T)
        if last_p:
            P.dma("sp", _mk("dma_start", out=o_pffnb[:, :], in_=r_pb.f(3072)[126:128, :]), reads=r_pb.b, sembuf=r_pb.b[0])
            A.free(r_pb)
        if samp:
            A.free(r_cf)
            r_sf = A.alloc(6)
            sfm = r_sf.f(3072)
            for q in range(6):
                bt, bb = P.bank()
                for j in range(4):
                    ch = q * 4 + j
                    MM(P, bt[0:64, j * 128:(j + 1) * 128], fraw_s[:, ch, :], ident, True, True, r_fraw.b + [b_cst], bb, j == 3)
                CP(P, "dve", sfm[0:64, q * 512:(q + 1) * 512], bt[0:64, :], [bb], r_sf.b)
            for l in range(2, 4):
                P.dma("sp", _mk("dma_start", out=o_sffnb[:, l - 2, :], in_=sfm[l:64:4, :]),
                      reads=r_sf.b, sembuf=r_sf.b[0])
            A.free(r_sf, r_fraw)
        dbg("hT%s%d" % (kind, ti), hT, r_hT.b)

        P.tag = "%s%s%d" % ("down_", kind, ti)
        if tidx + 1 < len(tiles):
            prefetched_x = load_x_for(tile_params(*tiles[tidx + 1]))
        r_yo2 = A.alloc(npg(NCH * 4096))
        yout = r_yo2.f(NCH, 1024)
        for cb in range(2):
            wks = [WS.get((w_fd, kb * 1024, 8, cb * 512, 512)) for kb in range(3)]
            for c in range(NCH):
                bt, bb = P.bank()
                for kc in range(24):
                    wi, wt, wb_ = wks[kc // 8]
                    MM(P, bt[0:CL, :], hT[:, kc, c * CL:(c + 1) * CL], wt[:, kc % 8, :], kc == 0, kc == 23,
                       [wb_] + r_hT.sub(kc, T * 2), bb, kc == 23)
                STT(P, "dve", x1[0:CL, c, cb * 512:(cb + 1) * 512], bt[0:CL, :], 1.0 / ALPHA,
                    x1[0:CL, c, cb * 512:(cb + 1) * 512], ALU.mult, ALU.add, [bb] + r_x1.sub(c, 4096), r_x1.sub(c, 4096))
            for wi, wt, wb_ in wks:
                WS.release(wi)
        r_bc2, bct2 = load_bc(2048)
        for c in range(NCH):
            ln_stats(x1[0:CL, c, :], CL, c, r_x1.sub(c, 4096), stt4, r_st.b[0], scr, r_scr.b)
        ln_rstd(CL, stt4, r_st.b[0])
        for c in range(NCH):
            ln_apply(x1[0:CL, c, :], yout[0:CL, c, :], CL, c, r_x1.sub(c, 4096), r_yo2.sub(c, 4096), stt4, r_st.b[0],
                     r_bc2, bct2, scr, r_scr.b)
        P.dma("sp", _mk("dma_start",
            out=ydst[t0:t0 + T, :].rearrange("(c p) d -> p c d", p=CL), in_=yout[0:CL]),
            reads=r_yo2.b, sembuf=r_yo2.b[0], nbytes=T * 4096)
        A.free(r_hT, r_x1, r_st, r_scr, r_yo2, r_bc2)

    P.finish()
    P.emit()
    nc._sim_end = P.sim_end
    nc._P = P
    return nc


NSLOTS = 5
STRICT_SAME_ENGINE = True
NHS = 6
NHT = 2
SYNC_LAT = 400.0
SYNC_LAT_SAME = 150.0
NATIVE_GELU = True
NATIVE_SILU = True
NATIVE_GELU_LRU = True
SCHED_PRIO = "id"
SCHED_MIX = 0.0
IGNORE_FALSE_DEPS = False
TILES = None
NPAGES = 66

PP_SCW = 0
PP_SCB = 96
PP_NG = 120
PP_LCW = 136
PP_LCB = 168
PP_BA = 176
PP_BX = 184
PP_LAM = 192
PP_BG = 200
PP_FCW = 216
PP_FCB = 288
PP_D = 312
PP_DTB = 328
PP_ALOG = 329
NPP = 330
PQ_SCW = 0
PQ_SCB = 96
PQ_HBA = 120
PQ_HBX = 128
PQ_HBG = 136
PQ_CL = 152
PQ_A = 160
NPQ = 161
CS_ID = 0
CS_SELPAR = 128
CS_MASK2 = 256
CS_ONES = 272
CS_RMASK = 400
CS_BMASK = 464
CS_BMASKT = 480
NCST = 1504
NCBF = 896 + 4096


def _fm(v, nch):
    return np.ascontiguousarray(v.reshape(nch, 128).T)


def _build_consts():
    cst = np.zeros((128, NCST), np.float32)
    cbs = np.zeros((128, NCBF), np.float32)
    cst[:, CS_ID:CS_ID + 128] = np.eye(128, dtype=np.float32)
    cbs[:, 0:128] = np.eye(128, dtype=np.float32)
    s = np.arange(128)[:, None]
    l = np.arange(128)[None, :]
    nm = np.where(l >= s, 0.0, NEG).astype(np.float32)
    cbs[:, 128:640] = np.tile(nm, (1, 4))
    s = np.arange(64)[:, None]
    l = np.arange(64)[None, :]
    nms = np.where((l >= s) & (l // 4 == s // 4), 0.0, NEG).astype(np.float32)
    cbs[0:64, 640:896] = np.tile(nms, (1, 4))
    sel = np.zeros((32, 32, 128), np.float32)
    for h in range(32):
        sel[h, h, :] = 1.0
    cbs[0:32, 896:896 + 4096] = sel.reshape(32, 4096)
    sp = np.zeros((32, 128), np.float32)
    for k in range(32):
        hh = k % 2
        sp[k, hh * 64:(hh + 1) * 64] = 1.0
    cst[0:32, CS_SELPAR:CS_SELPAR + 128] = sp
    m2 = np.zeros((32, 16), np.float32)
    for k in range(32):
        m2[k, k // 2] = 1.0
    cst[0:32, CS_MASK2:CS_MASK2 + 16] = m2
    cst[0:32, CS_ONES:CS_ONES + 128] = 1.0
    rm = np.ones((32, 64), np.float32)
    rm[:, 0::4] = 0.0
    cst[0:32, CS_RMASK:CS_RMASK + 64] = rm
    bm = np.zeros((64, 16), np.float32)
    for t in range(64):
        bm[t, t // 4] = 1.0
    cst[0:64, CS_BMASK:CS_BMASK + 16] = bm
    bmt = np.zeros((16, 64), np.float32)
    for t in range(64):
        bmt[t // 4, t] = 1.0
    cst[:, CS_BMASKT:CS_BMASKT + 1024] = np.broadcast_to(bmt.reshape(1, 1024), (128, 1024))
    return cst, cbs


_CACHE = {}


def kernel(x_prompt, x_sample, state_ssd, cache_ssd_conv, state_lru, cache_lru_conv, cache_ffn_conv,
           w_in, b_gate, ssd_conv_w, ssd_conv_b, ssd_dt_bias, ssd_a_log, ssd_d, ssd_norm_g, w_ssd_out,
           lru_conv_w, lru_conv_b, lru_wa, lru_ba, lru_wx, lru_bx, lru_lambda, w_lru_out, w_o,
           ln1_g, ln1_b, ffn_w_gate, ffn_w_up, ffn_conv_w, ffn_conv_b, ffn_w_down, ln2_g, ln2_b,
           _debug=None):
    f = lambda a: np.ascontiguousarray(np.asarray(a, dtype=np.float32))
    x_prompt, x_sample = f(x_prompt), f(x_sample)
    pp = np.zeros((128, NPP), np.float32)
    scw = f(ssd_conv_w)[0]
    pp[:, PP_SCW:PP_SCW + 96] = scw.reshape(4, 24, 128).transpose(2, 0, 1).reshape(128, 96)
    pp[:, PP_SCB:PP_SCB + 24] = _fm(f(ssd_conv_b)[0], 24)
    pp[:, PP_NG:PP_NG + 16] = _fm(f(ssd_norm_g)[0], 16)
    pp[:, PP_LCW:PP_LCW + 32] = f(lru_conv_w)[0].reshape(4, 8, 128).transpose(2, 0, 1).reshape(128, 32)
    pp[:, PP_LCB:PP_LCB + 8] = _fm(f(lru_conv_b)[0], 8)
    pp[:, PP_BA:PP_BA + 8] = f(lru_ba)[0].T
    pp[:, PP_BX:PP_BX + 8] = f(lru_bx)[0].T
    pp[:, PP_LAM:PP_LAM + 8] = _fm(f(lru_lambda)[0], 8)
    pp[:, PP_BG:PP_BG + 16] = _fm(f(b_gate)[0], 16)
    pp[:, PP_FCW:PP_FCW + 72] = f(ffn_conv_w)[0].reshape(3, 24, 128).transpose(2, 0, 1).reshape(128, 72)
    pp[:, PP_FCB:PP_FCB + 24] = _fm(f(ffn_conv_b)[0], 24)
    dd = f(ssd_d)[0]
    pp[:, PP_D:PP_D + 16] = np.repeat(dd.reshape(16, 2).T, 64, axis=0)
    pp[0:32, PP_DTB] = f(ssd_dt_bias)[0]
    pp[0:32, PP_ALOG] = f(ssd_a_log)[0]
    bcv = np.concatenate([f(ln1_g)[0], f(ln1_b)[0], f(ln2_g)[0], f(ln2_b)[0]])
    bc = np.ascontiguousarray(np.broadcast_to(bcv[None, :], (128, 4096)))
    cst, cbs = _build_consts()
    lru_w = np.ascontiguousarray(np.stack([f(lru_wa)[0], f(lru_wx)[0]]))
    shared = {
        "w_in": f(w_in)[0], "w_so": f(w_ssd_out)[0], "w_lo": f(w_lru_out)[0], "w_o": f(w_o)[0],
        "w_fg": f(ffn_w_gate)[0], "w_fu": f(ffn_w_up)[0], "w_fd": f(ffn_w_down)[0],
        "lru_w": lru_w, "pp": pp, "bc": bc, "cst": cst, "cbfsrc": cbs,
    }
    st = f(state_ssd)[0]
    cs = f(cache_ssd_conv)[0]
    sl = f(state_lru)[0]
    cl = f(cache_lru_conv)[0]
    cf = f(cache_ffn_conv)[0]
    in_maps = []
    for c in range(NCORES):
        b0, b1 = c * SB_, (c + 1) * SB_
        m = dict(shared)
        m["xp"] = x_prompt[c]
        m["xs"] = x_sample[b0:b1].reshape(SB_ * SL, D)
        m["st_ssd"] = st[b0:b1].reshape(SB_, 16, 128, 128)
        m["c_ssd"] = cs[b0:b1].reshape(SB_ * 3, 3072)
        m["st_lru"] = sl[b0:b1]
        m["c_lru"] = cl[b0:b1].reshape(SB_ * 3, D)
        m["c_ffn"] = cf[b0:b1].reshape(SB_ * 2, 3072)
        in_maps.append(m)
    key = repr(sorted(_debug.items())) if _debug else ""
    if key not in _CACHE:
        _CACHE[key] = build_program(_debug)
    nc = _CACHE[key]
    res = run_bass_kernel_spmd(nc, in_maps, core_ids=list(range(NCORES)))
    R = res.results
    cat = lambda name: np.stack([R[c][name] for c in range(NCORES)])
    y_prompt = cat("yp")
    y_sample = np.concatenate([R[c]["ys"].reshape(SB_, SL, D) for c in range(NCORES)])
    p_ssd = cat("o_pssd").reshape(1, NCORES, 32, 64, 128)
    p_ssdb = cat("o_pssdb")[None]
    p_lru = cat("o_plru").reshape(1, NCORES, D)
    p_lrub = cat("o_plrub")[None]
    p_ffnb = cat("o_pffnb")[None]
    s_ssd = np.concatenate([R[c]["o_sssd"] for c in range(NCORES)]).reshape(1, NCORES * SB_, 32, 64, 128)
    s_ssdb = np.concatenate([R[c]["o_sssdb"] for c in range(NCORES)])[None]
    s_lru = np.concatenate([R[c]["o_slru"] for c in range(NCORES)])[None]
    s_lrub = np.concatenate([R[c]["o_slrub"] for c in range(NCORES)])[None]
    s_ffnb = np.concatenate([R[c]["o_sffnb"] for c in range(NCORES)])[None]
    outs = (y_prompt, y_sample, p_ssd, p_ssdb, p_lru, p_lrub, p_ffnb, s_ssd, s_ssdb, s_lru, s_lrub, s_ffnb)
    outs = tuple(np.ascontiguousarray(o, dtype=np.float32) for o in outs)
    if _debug:
        return outs, R
    return outs
```

```python
import math
from contextlib import ExitStack

import numpy as np
import concourse.bass as bass
import concourse.mybir as mybir
from concourse.bass_utils import run_bass_kernel_spmd

F32 = mybir.dt.float32
BF16 = mybir.dt.bfloat16
AF = mybir.ActivationFunctionType
ALU = mybir.AluOpType

NCORES = 8
D = 1024
SEQ = 2048
SB_ = 16
SL = 4
D_IN = 9248
C_Z, C_XBC, C_DT, C_LX, C_LY, C_GS, C_GL = 0, 2048, 5120, 5152, 6176, 7200, 8224
ALPHA = 2.0 ** 0.25
LN_EPS = 1e-5
RMS_EPS = 1e-6
GK = 1.0 / 0.044715
GC = math.sqrt(2.0 / math.pi) * 0.044715
NEG = -30000.0

ENGS = ("pe", "act", "dve", "pool", "sp")


class Buf:
    __slots__ = ("name", "w", "r", "dsem", "excl")

    def __init__(self, name, excl=False):
        self.name = name
        self.w = []
        self.r = []
        self.dsem = {}
        self.excl = excl


class Prog:
    def __init__(self, nc):
        self.nc = nc
        self.es = ExitStack()
        self.ops = []
        self.sems = {}
        for e in ENGS:
            self.sems[e] = self.es.enter_context(nc.semaphore("s_" + e))
        self.ndsem = 0
        self.banks = []
        self.bank_i = 0
        self.reserved = set()
        self.pe_unit = None
        self.tag = ""

    def sb(self, name, shape, dt):
        return self.es.enter_context(self.nc.sbuf_tensor("sb_" + name, list(shape), dt))

    def ps(self, name, shape, dt=F32):
        return self.es.enter_context(self.nc.psum_tensor("ps_" + name, list(shape), dt))

    def _dsem(self, buf, sw):
        if sw not in buf.dsem:
            key = "d%d" % self.ndsem
            self.ndsem += 1
            self.sems[key] = self.es.enter_context(self.nc.semaphore("s_" + key))
            buf.dsem[sw] = key
        return buf.dsem[sw]

    def _collect(self, oid, reads, writes, deps):
        for b in reads:
            for i in b.w:
                if i != oid:
                    deps[i] = True
        for b in writes:
            if IGNORE_FALSE_DEPS and not b.excl:
                continue
            for i in b.w:
                if i != oid:
                    deps.setdefault(i, False)
            for i in b.r:
                if i != oid:
                    deps.setdefault(i, False)
        for b in reads:
            if not b.r or b.r[-1] != oid:
                b.r.append(oid)
        for b in writes:
            b.w = [oid]
            b.r = []

    def op(self, eng, fn, reads=(), writes=(), inc=True, dur=500.0):
        reads = list(reads)
        writes = list(writes)
        if any(b.excl for b in reads):
            writes = writes + [b for b in reads if b.excl and b not in writes]
            reads = [b for b in reads if not b.excl]
        if eng == "pe":
            if self.pe_unit is None:
                self.pe_unit = dict(id=len(self.ops), eng="pe", fns=[], kind="c", deps={}, dur=0.0, key="pe", tag=self.tag)
                self.ops.append(self.pe_unit)
            u = self.pe_unit
            u["fns"].append(fn)
            u["dur"] += dur
            self._collect(u["id"], reads, writes, u["deps"])
            if inc:
                self.pe_unit = None
            return u["id"]
        o = dict(id=len(self.ops), eng=eng, fns=[fn], kind="c", deps={}, dur=dur, key=eng, tag=self.tag)
        self.ops.append(o)
        self._collect(o["id"], reads, writes, o["deps"])
        return o["id"]

    def dma(self, eng, fn, reads=(), writes=(), sembuf=None, nbytes=65536):
        assert self.pe_unit is None or eng != "pe"
        sw = eng == "pool"
        key = self._dsem(sembuf, sw)
        o = dict(id=len(self.ops), eng=eng, fns=[fn], kind="d", deps={}, key=key, tag=self.tag,
                 dur=(1200.0 if sw else 150.0), lat=2000.0 + nbytes / 200.0)
        self.ops.append(o)
        self._collect(o["id"], list(reads), list(writes), o["deps"])
        return o["id"]

    def bank(self):
        n = len(self.banks)
        while True:
            i = self.bank_i % n
            self.bank_i += 1
            if i not in self.reserved:
                return self.banks[i]

    def finish(self):
        assert self.pe_unit is None

    def _schedule(self):
        import heapq
        ops = self.ops
        n = len(ops)
        ndep = [len(o["deps"]) for o in ops]
        users = [[] for _ in range(n)]
        for o in ops:
            for d in o["deps"]:
                users[d].append(o["id"])
        finish = [0.0] * n
        bl = [0.0] * n
        for o in reversed(ops):
            d = o.get("lat", o["dur"]) if o["kind"] == "d" else o["dur"]
            m = 0.0
            for u in users[o["id"]]:
                if bl[u] > m:
                    m = bl[u]
            bl[o["id"]] = d + m
        if SCHED_PRIO == "bl":
            prio = [(-bl[i], i) for i in range(n)]
        elif SCHED_PRIO == "mix":
            prio = [(-(bl[i] - SCHED_MIX * i), i) for i in range(n)]
        else:
            prio = [(i, i) for i in range(n)]
        wait_heap = {e: [] for e in ENGS}
        now_heap = {e: [] for e in ENGS}
        free = {e: 0.0 for e in ENGS}
        order = {e: [] for e in ENGS}
        for o in ops:
            if ndep[o["id"]] == 0:
                heapq.heappush(wait_heap[o["eng"]], (0.0, o["id"]))
        done = 0
        while done < n:
            best = None
            for e in ENGS:
                wh, nh = wait_heap[e], now_heap[e]
                while wh and wh[0][0] <= free[e]:
                    heapq.heappush(nh, prio[heapq.heappop(wh)[1]])
                if nh:
                    cand = (free[e], nh[0][1], e, True)
                elif wh:
                    cand = (wh[0][0], wh[0][1], e, False)
                else:
                    continue
                if best is None or cand[:2] < best[:2]:
                    best = cand
            assert best is not None, "scheduler deadlock"
            start, oid, e, from_now = best
            if from_now:
                heapq.heappop(now_heap[e])
            else:
                heapq.heappop(wait_heap[e])
            o = ops[oid]
            rt = max([finish[d] + (SYNC_LAT if ops[d]["eng"] != e else SYNC_LAT_SAME) for d in o["deps"]] + [0.0])
            if rt >= start - 1e-6 and o["deps"]:
                o["crit"] = max(o["deps"], key=lambda d: finish[d])
            else:
                o["crit"] = order[e][-1] if order[e] else None
            free[e] = start + o["dur"]
            finish[oid] = start + (o["lat"] if o["kind"] == "d" else o["dur"])
            o["start"] = start
            order[e].append(oid)
            done += 1
            for u in users[oid]:
                ndep[u] -= 1
                if ndep[u] == 0:
                    ue = ops[u]["eng"]
                    rt = max(finish[d] + (SYNC_LAT if ops[d]["eng"] != ue else SYNC_LAT_SAME) for d in ops[u]["deps"])
                    heapq.heappush(wait_heap[ops[u]["eng"]], (rt, u))
        self.sim_end = max(finish) if finish else 0.0
        return order

    def emit(self):
        nc = self.nc
        sems = self.sems
        ops = self.ops
        order = self._schedule()
        val = {}
        cnt = {}
        for e in ENGS:
            for oid in order[e]:
                o = ops[oid]
                if o["kind"] == "c":
                    cnt[e] = cnt.get(e, 0) + 1
                    val[oid] = cnt[e]
        dmas = sorted((o for o in ops if o["kind"] == "d"), key=lambda o: (o["start"], o["id"]))
        final = {}
        for o in dmas:
            cnt[o["key"]] = cnt.get(o["key"], 0) + 16
            val[o["id"]] = cnt[o["key"]]
            final[o["key"]] = cnt[o["key"]]
        waited = {}
        plan = {e: [] for e in ENGS}
        for e in ENGS:
            for oid in order[e]:
                o = ops[oid]
                need = {}
                for d, raw in o["deps"].items():
                    od = ops[d]
                    if o["kind"] == "c" and od["kind"] == "c" and od["eng"] == e:
                        if e == "pe" or (not raw and not STRICT_SAME_ENGINE):
                            continue
                    k = od["key"]
                    if need.get(k, 0) < val[d]:
                        need[k] = val[d]
                waits = []
                for k, v in need.items():
                    if waited.get((e, k), 0) >= v:
                        continue
                    waited[(e, k)] = v
                    waits.append((k, v))
                plan[e].append((o, waits))
        fin = [(k, v) for k, v in final.items() if waited.get(("sp", k), 0) < v]

        def run(engine, e):
            for o, waits in plan[e]:
                for k, v in waits:
                    engine.wait_ge(sems[k], v)
                ins = None
                for fn in o["fns"]:
                    ins = fn(engine)
                ins.then_inc(sems[o["key"]], 16 if o["kind"] == "d" else 1)
            if e == "sp":
                for k, v in fin:
                    engine.wait_ge(sems[k], v)

        with nc.Block() as block:
            @block.tensor
            def _(eng):
                run(eng, "pe")

            @block.scalar
            def _(eng):
                run(eng, "act")

            @block.vector
            def _(eng):
                run(eng, "dve")

            @block.gpsimd
            def _(eng):
                run(eng, "pool")

            @block.sync
            def _(eng):
                run(eng, "sp")
        self.es.close()


def _compact(evs):
    best = {}
    for k, v in evs:
        if best.get(k, 0) < v:
            best[k] = v
    return list(best.items())


def view(ap, *dims):
    if len(dims) == 1:
        return ap
    names = "abcdefg"[: len(dims)]
    s = "p (" + " ".join(names) + ") -> p " + " ".join(names)
    kw = {names[i]: dims[i] for i in range(1, len(dims))}
    return ap.rearrange(s, **kw)


class Reg:
    def __init__(self, arena, lo, n):
        self.arena, self.lo, self.n = arena, lo, n
        self.b = arena.bufs[lo:lo + n]

    def f(self, *dims, parts=128):
        ap = self.arena.tile[0:parts, self.lo * 512:(self.lo + self.n) * 512]
        tot = int(np.prod(dims))
        ap = ap[:, 0:tot]
        return view(ap, *dims)

    def h(self, *dims, parts=128):
        ap = self.arena.tile[0:parts, self.lo * 512:(self.lo + self.n) * 512].bitcast(BF16)
        tot = int(np.prod(dims))
        ap = ap[:, 0:tot]
        return view(ap, *dims)

    def sub(self, i, nbytes):
        lo = (i * nbytes) // 2048
        hi = ((i + 1) * nbytes - 1) // 2048
        return self.b[lo:hi + 1]


class Arena:
    def __init__(self, P, npages):
        self.tile = P.sb("arena", [128, npages * 512], F32)
        self.bufs = [Buf("pg%d" % i) for i in range(npages)]
        self.free_list = [(0, npages)]
        self.npages = npages

    def alloc(self, n):
        for idx, (lo, cnt) in enumerate(self.free_list):
            if cnt >= n:
                if cnt == n:
                    self.free_list.pop(idx)
                else:
                    self.free_list[idx] = (lo + n, cnt - n)
                return Reg(self, lo, n)
        raise RuntimeError("arena OOM: want %d pages, free %s" % (n, self.free_list))

    def free(self, *regs):
        for r in regs:
            self.free_list.append((r.lo, r.n))
        self.free_list.sort()
        merged = []
        for lo, cnt in self.free_list:
            if merged and merged[-1][0] + merged[-1][1] == lo:
                merged[-1] = (merged[-1][0], merged[-1][1] + cnt)
            else:
                merged.append((lo, cnt))
        self.free_list = merged


class WStream:
    def __init__(self, P, nslots, blocks, per_pass, scratch):
        self.P = P
        self.tile = P.sb("wslots", [128, nslots, 8, 512], BF16)
        self.bufs = [Buf("ws%d" % i) for i in range(nslots)]
        self.blocks = blocks
        self.per_pass = per_pass
        self.scratch = scratch
        self.scrbufs = [Buf("wscr%d" % i) for i in range(per_pass)]
        self.free_slots = list(range(nslots))
        self.slot_of = {}
        self.next_issue = 0
        self.next_get = 0
        self.pump()

    def pump(self):
        while self.free_slots and self.next_issue < len(self.blocks):
            s = self.free_slots.pop(0)
            i = self.next_issue
            w, k0, nkc, c0, ncols = self.blocks[i]
            bid = i % self.per_pass
            dst = self.tile[:, s, 0:nkc, 0:ncols]
            if i < self.per_pass:
                src = w[k0:k0 + nkc * 128, c0:c0 + ncols].rearrange("(kc p) n -> p kc n", p=128)
                self.P.dma("pool", _mk("dma_start", out=dst, in_=src), writes=[self.bufs[s]], sembuf=self.bufs[s],
                           nbytes=nkc * ncols * 512)
                if len(self.blocks) > self.per_pass:
                    self.P.dma("sp", _mk("dma_start", out=self.scratch[bid, :, 0:nkc, 0:ncols], in_=dst),
                               reads=[self.bufs[s]], writes=[self.scrbufs[bid]], sembuf=self.bufs[s], nbytes=nkc * ncols * 256)
            else:
                self.P.dma("sp", _mk("dma_start", out=dst, in_=self.scratch[bid, :, 0:nkc, 0:ncols]),
                           reads=[self.scrbufs[bid]], writes=[self.bufs[s]], sembuf=self.bufs[s], nbytes=nkc * ncols * 256)
            self.slot_of[i] = s
            self.next_issue += 1

    def get(self, expect=None):
        i = self.next_get
        self.next_get += 1
        assert i in self.slot_of, "weight block %d not issued (not enough slots)" % i
        if expect is not None:
            assert self.blocks[i][0] is expect[0] and self.blocks[i][1:] == tuple(expect[1:]), (i, self.blocks[i][1:], expect[1:])
        s = self.slot_of[i]
        return i, self.tile[:, s], self.bufs[s]

    def release(self, i):
        self.free_slots.append(self.slot_of.pop(i))
        self.pump()


def _mk(meth, *args, **kw):
    return lambda e: getattr(e, meth)(*args, **kw)


def _chk(P, aps, reads, writes):
    have = set(id(b) for b in list(reads) + list(writes))
    for ap in aps:
        nm = getattr(ap, "name", None)
        if isinstance(nm, str) and nm.startswith("ps_bank"):
            i = int(nm[len("ps_bank"):].split("_")[0])
            assert id(P.banks[i][1]) in have, "PSUM bank %d accessed without declaring its Buf" % i


def _fsz(ap):
    n = 1
    for d in ap.shape[1:]:
        n *= int(d)
    return n


def ACT(P, out, in_, func, reads, writes, bias=None, scale=None, accum=None):
    kw = {}
    if bias is not None:
        kw["bias"] = bias
    if scale is not None:
        kw["scale"] = scale
    if accum is not None:
        kw["accum_out"] = accum
    _chk(P, [out, in_, bias, scale], reads, writes)
    return P.op("act", lambda e: e.activation(out=out, in_=in_, func=func, **kw), reads, writes,
                dur=220.0 + _fsz(out) / 1.2)


def _vdur(eng, out):
    if eng == "pool":
        return 400.0 + _fsz(out) * 8.0
    return (151.0 + _fsz(out)) / 0.96


def TT(P, eng, out, in0, in1, op, reads, writes):
    _chk(P, [out, in0, in1], reads, writes)
    return P.op(eng, lambda e: e.tensor_tensor(out=out, in0=in0, in1=in1, op=op), reads, writes, dur=_vdur(eng, out))


def TS(P, eng, out, in0, s1, s2, op0, op1, reads, writes):
    _chk(P, [out, in0, s1, s2], reads, writes)
    if s2 is None:
        return P.op(eng, lambda e: e.tensor_scalar(out=out, in0=in0, scalar1=s1, scalar2=None, op0=op0), reads, writes,
                    dur=_vdur(eng, out))
    return P.op(eng, lambda e: e.tensor_scalar(out=out, in0=in0, scalar1=s1, scalar2=s2, op0=op0, op1=op1), reads, writes,
                dur=_vdur(eng, out))


def STT(P, eng, out, in0, scalar, in1, op0, op1, reads, writes):
    _chk(P, [out, in0, scalar, in1], reads, writes)
    return P.op(eng, lambda e: e.scalar_tensor_tensor(out=out, in0=in0, scalar=scalar, in1=in1, op0=op0, op1=op1),
                reads, writes, dur=_vdur(eng, out))


def CP(P, eng, out, in_, reads, writes):
    _chk(P, [out, in_], reads, writes)
    if eng == "act":
        return P.op("act", lambda e: e.activation(out=out, in_=in_, func=AF.Copy), reads, writes,
                    dur=220.0 + _fsz(out) / 1.2)
    return P.op(eng, lambda e: e.tensor_copy(out=out, in_=in_), reads, writes, dur=_vdur(eng, out))


def MM(P, out, lhsT, rhs, start, stop, reads, bank, inc):
    _chk(P, [out], reads, [bank])
    n = max(_fsz(rhs), 64)
    d = n / 2.3 * (4.0 if lhsT.dtype == F32 else 1.0) + 40.0
    return P.op("pe", lambda e: e.matmul(out, lhsT=lhsT, rhs=rhs, start=start, stop=stop, skip_group_check=True),
                reads, [bank], inc=inc, dur=d)


def build_program(debug=None):
    nc = bass.Bass("TRN2", target_bir_lowering=False)
    P = Prog(nc)

    def din(name, shape):
        return nc.dram_tensor(name, list(shape), F32, kind="ExternalInput").ap()

    def dout(name, shape):
        return nc.dram_tensor(name, list(shape), F32, kind="ExternalOutput").ap()

    xp = din("xp", [SEQ, D])
    xs = din("xs", [SB_ * SL, D])
    st_ssd = din("st_ssd", [SB_, 16, 128, 128])
    c_ssd = din("c_ssd", [SB_ * 3, 3072])
    st_lru = din("st_lru", [SB_, D])
    c_lru = din("c_lru", [SB_ * 3, D])
    c_ffn = din("c_ffn", [SB_ * 2, 3072])
    w_in = din("w_in", [D, D_IN])
    w_so = din("w_so", [2048, D])
    w_lo = din("w_lo", [D, D])
    w_o = din("w_o", [D, D])
    w_fg = din("w_fg", [D, 3072])
    w_fu = din("w_fu", [D, 3072])
    w_fd = din("w_fd", [3072, D])
    lru_w = din("lru_w", [2, 8, 128, 128])
    pp_d = din("pp", [128, NPP])
    bc_d = din("bc", [128, 4096])
    cst_d = din("cst", [128, NCST])
    cbf_d = din("cbfsrc", [128, NCBF])

    yp = dout("yp", [SEQ, D])
    ys = dout("ys", [SB_ * SL, D])
    o_pssd = dout("o_pssd", [16, 128, 128])
    o_pssdb = dout("o_pssdb", [3, 3072])
    o_plru = dout("o_plru", [8, 128])
    o_plrub = dout("o_plrub", [3, D])
    o_pffnb = dout("o_pffnb", [2, 3072])
    o_sssd = dout("o_sssd", [SB_, 16, 128, 128])
    o_sssdb = dout("o_sssdb", [SB_, 3, 3072])
    o_slru = dout("o_slru", [SB_, D])
    o_slrub = dout("o_slrub", [SB_, 3, D])
    o_sffnb = dout("o_sffnb", [SB_, 2, 3072])

    dbg_out = {}
    if debug:
        for name, shape in debug.items():
            dbg_out[name] = dout("dbg_" + name, shape)

    for i in range(8):
        t = P.ps("bank%d" % i, [128, 512])
        P.banks.append((t, Buf("bank%d" % i, True)))

    pp = P.sb("pp", [128, NPP], F32)
    b_pp = Buf("pp")
    cst = P.sb("cst", [128, NCST], F32)
    b_cst = Buf("cst")
    cbf = P.sb("cbf", [128, NCBF], BF16)
    b_cbf = Buf("cbf")
    pq = P.sb("pq", [128, NPQ], F32)
    b_pq = Buf("pq")
    lruw = P.sb("lruw", [128, 2, 8, 128], BF16)
    b_lruw = Buf("lruw")
    diagD = P.sb("diagD", [128, 16, 128], BF16)
    b_diagD = Buf("diagD")
    halo_x = P.sb("halo_x", [128, 24, 3], F32)
    b_halo_x = Buf("halo_x")
    halo_l = P.sb("halo_l", [128, 8, 3], F32)
    b_halo_l = Buf("halo_l")
    halo_f = P.sb("halo_f", [128, 24, 2], F32)
    b_halo_f = Buf("halo_f")
    hcar = P.sb("hcar", [128, 8], F32)
    b_hcar = Buf("hcar")
    Hst = P.sb("Hst", [128, 16, 128], F32)
    b_H = Buf("Hst")

    P.dma("sp", _mk("dma_start", out=pp[:], in_=pp_d[:, :]), writes=[b_pp], sembuf=b_pp)
    P.dma("sp", _mk("dma_start", out=cst[:], in_=cst_d[:, :]), writes=[b_cst], sembuf=b_cst)
    P.dma("pool", _mk("dma_start", out=lruw[:], in_=lru_w.rearrange("t n c d -> c t n d")),
          writes=[b_lruw], sembuf=b_lruw)

    ident = cst[:, CS_ID:CS_ID + 128]
    identb = cbf[:, 0:128]
    negm4 = cbf[:, 128:640]
    negs4 = cbf[0:64, 640:896]
    selb = view(cbf[0:32, 896:896 + 4096], 32, 128)
    selpar = cst[0:32, CS_SELPAR:CS_SELPAR + 128]
    mask2 = cst[0:32, CS_MASK2:CS_MASK2 + 16]
    ones32 = cst[0:32, CS_ONES:CS_ONES + 128]
    rmask = cst[0:32, CS_RMASK:CS_RMASK + 64]
    bmask = cst[0:64, CS_BMASK:CS_BMASK + 16]
    bmaskT = view(cst[:, CS_BMASKT:CS_BMASKT + 1024], 16, 64)

    P.dma("pool", _mk("dma_start", out=cbf[:], in_=cbf_d[:, :]), writes=[b_cbf], sembuf=b_cbf)

    def ppc(off, n=1):
        return pp[:, off:off + n]

    def pqc(off, n=1):
        return pq[:, off:off + n]

    TS(P, "dve", pqc(PQ_SCW, 96), ppc(PP_SCW, 96), 0.5, None, ALU.mult, None, [b_pp], [b_pq])
    TS(P, "dve", pqc(PQ_SCB, 24), ppc(PP_SCB, 24), 0.5, None, ALU.mult, None, [b_pp], [b_pq])
    TS(P, "dve", pqc(PQ_HBA, 8), ppc(PP_BA, 8), 0.5, None, ALU.mult, None, [b_pp], [b_pq])
    TS(P, "dve", pqc(PQ_HBX, 8), ppc(PP_BX, 8), 0.5, None, ALU.mult, None, [b_pp], [b_pq])
    TS(P, "dve", pqc(PQ_HBG, 16), ppc(PP_BG, 16), 0.5, None, ALU.mult, None, [b_pp], [b_pq])
    ACT(P, pqc(PQ_CL, 8), ppc(PP_LAM, 8), AF.Exp, [b_pp], [b_pq], scale=-1.0)
    ACT(P, pqc(PQ_CL, 8), pqc(PQ_CL, 8), AF.Ln, [b_pq], [b_pq], bias=1.0)
    TS(P, "dve", pqc(PQ_CL, 8), pqc(PQ_CL, 8), -4.0, None, ALU.mult, None, [b_pq], [b_pq])
    ACT(P, pq[0:32, PQ_A:PQ_A + 1], pp[0:32, PP_ALOG:PP_ALOG + 1], AF.Exp, [b_pp], [b_pq])
    TS(P, "dve", pq[0:32, PQ_A:PQ_A + 1], pq[0:32, PQ_A:PQ_A + 1], -1.0, None, ALU.mult, None, [b_pq], [b_pq])
    for j in range(16):
        TS(P, "dve", diagD[:, j, :], ident, ppc(PP_D + j), None, ALU.mult, None, [b_cst, b_pp], [b_diagD])
    P.op("pool", _mk("memset", halo_x[:], 0.0), [], [b_halo_x])
    P.op("pool", _mk("memset", halo_l[:], 0.0), [], [b_halo_l])
    P.op("pool", _mk("memset", halo_f[:], 0.0), [], [b_halo_f])
    P.op("pool", _mk("memset", hcar[:], 0.0), [], [b_hcar])
    P.op("pool", _mk("memset", Hst[:], 0.0), [], [b_H])

    A = Arena(P, NPAGES)

    def npg(nbytes):
        return max(1, (nbytes + 2047) // 2048)

    blocks = []
    tiles = TILES if TILES is not None else [("p", i) for i in range(4)] + [("s", 0)]
    for _ in tiles:
        for cb in range(2):
            blocks.append((w_in, 0, 8, C_LX + cb * 512, 512))
        for cb in range(2):
            blocks.append((w_in, 0, 8, C_LY + cb * 512, 512))
        for cb in range(6):
            blocks.append((w_in, 0, 8, C_XBC + cb * 512, 512))
        blocks.append((w_in, 0, 8, C_DT, 32))
        for cb in range(4):
            blocks.append((w_in, 0, 8, C_Z + cb * 512, 512))
        for cb in range(2):
            blocks.append((w_in, 0, 8, C_GS + cb * 512, 512))
            blocks.append((w_in, 0, 8, C_GL + cb * 512, 512))
            blocks.append((w_lo, 0, 8, cb * 512, 512))
        for cb in range(2):
            blocks.append((w_so, 0, 8, cb * 512, 512))
            blocks.append((w_so, 1024, 8, cb * 512, 512))
        for cb in range(2):
            blocks.append((w_o, 0, 8, cb * 512, 512))
        for fb in range(6):
            blocks.append((w_fg, 0, 8, fb * 512, 512))
            blocks.append((w_fu, 0, 8, fb * 512, 512))
        for cb in range(2):
            for kb in range(3):
                blocks.append((w_fd, kb * 1024, 8, cb * 512, 512))
    per_pass = len(blocks) // len(tiles)
    wscr = nc.dram_tensor("wscr", [per_pass, 128, 8, 512], BF16, kind="Internal").ap()
    WS = WStream(P, NSLOTS, blocks, per_pass, wscr)

    def dbg(name, ap, bufs):
        if name in dbg_out:
            o = dbg_out[name]
            idx = tuple(slice(None) for _ in o.shape)
            b0 = bufs[0]
            P.dma("sp", _mk("dma_start", out=o[idx], in_=ap), reads=bufs, sembuf=b0)

    def gelu2(dst, src, tmp, rb, wb_dst, wb_tmp):
        ACT(P, tmp, src, AF.Square, rb, wb_tmp)
        STT(P, "dve", tmp, tmp, GK, src, ALU.add, ALU.mult, rb + wb_tmp, wb_tmp)
        ACT(P, tmp, tmp, AF.Tanh, wb_tmp, wb_tmp, scale=GC)
        STT(P, "dve", dst, tmp, 1.0, src, ALU.add, ALU.mult, rb + wb_tmp, wb_dst)

    def load_bc(g0):
        r = A.alloc(4)
        t = r.f(2048)
        P.dma("sp", _mk("dma_start", out=t, in_=bc_d[:, g0:g0 + 2048]), writes=r.b, sembuf=r.b[0])
        return r, t

    def ln_stats(src, CL, c, rb, st, b_st, scratch, b_scr):
        bs = [b_st]
        ACT(P, scratch, src, AF.Identity, rb, b_scr + bs, accum=st[0:CL, 0, c:c + 1])
        ACT(P, scratch, src, AF.Square, rb, b_scr + bs, accum=st[0:CL, 1, c:c + 1])

    def ln_rstd(CL, st, b_st):
        bs = [b_st]
        TS(P, "dve", st[0:CL, 2, :], st[0:CL, 0, :], 1.0 / 1024, None, ALU.mult, None, bs, bs)
        TT(P, "dve", st[0:CL, 3, :], st[0:CL, 2, :], st[0:CL, 2, :], ALU.mult, bs, bs)
        STT(P, "dve", st[0:CL, 4, :], st[0:CL, 1, :], 1.0 / 1024, st[0:CL, 3, :], ALU.mult, ALU.subtract, bs, bs)
        TS(P, "dve", st[0:CL, 4, :], st[0:CL, 4, :], LN_EPS / (ALPHA * ALPHA), None, ALU.add, None, bs, bs)
        ACT(P, st[0:CL, 4, :], st[0:CL, 4, :], AF.Sqrt, bs, bs)
        P.op("dve", _mk("reciprocal", out=st[0:CL, 5, :], in_=st[0:CL, 4, :]), bs, bs)
        STT(P, "dve", st[0:CL, 6, :], st[0:CL, 2, :], -1.0, st[0:CL, 5, :], ALU.mult, ALU.mult, bs, bs)

    def ln_apply(src, dst, CL, c, rb, wb, st, b_st, r_bc, bct, scratch, b_scr):
        TS(P, "dve", scratch, src, st[0:CL, 5, c:c + 1], st[0:CL, 6, c:c + 1], ALU.mult, ALU.add, rb + [b_st], b_scr)
        TT(P, "dve", scratch, scratch, bct[0:CL, 0:1024], ALU.mult, b_scr + r_bc.b, b_scr)
        TT(P, "dve", dst, scratch, bct[0:CL, 1024:2048], ALU.add, b_scr + r_bc.b, wb)

    def tile_params(kind, ti):
        if kind == "s":
            return dict(T=64, CL=64, NCH=1, xsrc=xs, t0=0)
        return dict(T=512, CL=128, NCH=4, xsrc=xp, t0=ti * 512)

    def load_x_for(tp):
        r = A.alloc(npg(tp["NCH"] * 4096))
        xt_ = r.f(tp["NCH"], 1024)
        P.dma("sp", _mk("dma_start", out=xt_[0:tp["CL"]],
                        in_=tp["xsrc"][tp["t0"]:tp["t0"] + tp["T"], :].rearrange("(c p) d -> p c d", p=tp["CL"])),
              writes=r.b, sembuf=r.b[0], nbytes=tp["T"] * 4096)
        return r, xt_

    prefetched_x = None
    for tidx, (kind, ti) in enumerate(tiles):
        samp = kind == "s"
        if samp:
            T, NB, L, CL, NCH = 64, SB_, SL, 64, 1
            xsrc = xs
            ydst = ys
            t0 = 0
        else:
            T, NB, L, CL, NCH = 512, 1, 512, 128, 4
            xsrc = xp
            ydst = yp
            t0 = ti * 512
        last_p = (not samp) and ti == 3
        first_p = (not samp) and ti == 0

        P.tag = "%s%s%d" % ("xT_", kind, ti)
        def load_x():
            r = A.alloc(npg(NCH * 4096))
            xt_ = r.f(NCH, 1024)
            P.dma("sp", _mk("dma_start",
                out=xt_[0:CL], in_=xsrc[t0:t0 + T, :].rearrange("(c p) d -> p c d", p=CL)),
                writes=r.b, sembuf=r.b[0], nbytes=T * 4096)
            return r, xt_

        def build_xT(r_x, x_tm):
            r_xT = A.alloc(npg(8 * T * 2))
            xT = r_xT.h(8, T)
            r_x16 = A.alloc(npg(NCH * 2048))
            x16 = r_x16.h(NCH, 1024)
            for c in range(NCH):
                CP(P, "act" if c % 2 == 0 else "dve", x16[0:CL, c, :], x_tm[0:CL, c, :], r_x.sub(c, 4096), r_x16.sub(c, 2048))
                for half in range(2):
                    bt, bb = P.bank()
                    for j in range(4):
                        kc = half * 4 + j
                        MM(P, bt[:, j * CL:(j + 1) * CL], x16[0:CL, c, kc * 128:(kc + 1) * 128], identb[0:CL, 0:CL],
                           True, True, r_x16.sub(c, 2048) + [b_cbf], bb, j == 3)
                    eng = "act" if (c + half) % 2 == 0 else "dve"
                    CP(P, eng, xT[:, half * 4:half * 4 + 4, c * CL:(c + 1) * CL], view(bt[:, 0:4 * CL], 4, CL),
                       [bb], r_xT.b)
            A.free(r_x16)
            return r_xT, xT

        if prefetched_x is not None:
            r_x, x_tm = prefetched_x
            prefetched_x = None
        else:
            r_x, x_tm = load_x()
        r_xT, xT = build_xT(r_x, x_tm)
        A.free(r_x)

        P.tag = "%s%s%d" % ("lru_", kind, ti)
        r_lx = A.alloc(npg(8 * NB * (3 + L) * 4))
        lx = r_lx.f(8, NB, 3 + L)
        r_gy = A.alloc(npg(8 * T * 4))
        gy = r_gy.f(8, T)
        lx_raw_s = None
        if samp:
            r_c = A.alloc(4)
            ctm = r_c.f(1024)
            P.dma("sp", _mk("dma_start", out=ctm[0:48], in_=c_lru[:, :]), writes=r_c.b, sembuf=r_c.b[0])
            for half in range(2):
                bt, bb = P.bank()
                for j in range(4):
                    ch = half * 4 + j
                    MM(P, bt[:, j * 48:(j + 1) * 48], ctm[0:48, ch * 128:(ch + 1) * 128], ident[0:48, 0:48],
                       True, True, r_c.b + [b_cst], bb, j == 3)
                CP(P, "dve", lx[:, half * 4:half * 4 + 4, :, 0:3], view(bt[:, 0:192], 4, NB, 3), [bb], r_lx.b)
            A.free(r_c)
            r_lxs = A.alloc(1)
            lx_raw_s = r_lxs.f(8, 64)
        for cb in range(2):
            wi, wt, wb_ = WS.get((w_in, 0, 8, C_LX + cb * 512, 512))
            for oc in range(4):
                ch = cb * 4 + oc
                bt, bb = P.bank()
                for kc in range(8):
                    MM(P, bt[:, 0:T], wt[:, kc, oc * 128:(oc + 1) * 128], xT[:, kc, :], kc == 0, kc == 7,
                       [wb_] + r_xT.b, bb, kc == 7)
                if not samp:
                    CP(P, "pool", lx[:, ch, 0, 0:3], halo_l[:, ch, :], [b_halo_l], r_lx.sub(ch, (3 + L) * 4 * NB))
                CP(P, "act", lx[:, ch, :, 3:3 + L], view(bt[:, 0:T], NB, L), [bb], r_lx.sub(ch, (3 + L) * 4 * NB))
                if samp:
                    CP(P, "dve", lx_raw_s[:, ch, :], bt[:, 0:T], [bb], r_lxs.b)
                else:
                    CP(P, "pool", halo_l[:, ch, :], lx[:, ch, 0, L:L + 3], r_lx.sub(ch, (3 + L) * 4 * NB), [b_halo_l])
            WS.release(wi)
        for cb in range(2):
            wi, wt, wb_ = WS.get((w_in, 0, 8, C_LY + cb * 512, 512))
            for oc in range(4):
                ch = cb * 4 + oc
                bt, bb = P.bank()
                for kc in range(8):
                    MM(P, bt[:, 0:T], wt[:, kc, oc * 128:(oc + 1) * 128], xT[:, kc, :], kc == 0, kc == 7,
                       [wb_] + r_xT.b, bb, kc == 7)
                CP(P, "act", gy[:, ch, :], bt[:, 0:T], [bb], r_gy.sub(ch, T * 4))
            WS.release(wi)

        r_tm = A.alloc(2)
        stg = r_tm.f(1024)
        if samp or last_p:
            M = 64 if samp else 128
            for half in range(2):
                bt, bb = P.bank()
                for j in range(4):
                    ch = half * 4 + j
                    src = lx_raw_s[:, ch, :] if samp else lx[:, ch, 0, 3 + L - 128:3 + L]
                    MM(P, bt[0:M, j * 128:(j + 1) * 128], src, ident, True, True,
                       (r_lxs.b if samp else r_lx.b) + [b_cst], bb, j == 3)
                CP(P, "dve", stg[0:M, half * 512:(half + 1) * 512], bt[0:M, :], [bb], r_tm.b)
            if samp:
                for l in range(1, 4):
                    P.dma("sp", _mk("dma_start", out=o_slrub[:, l - 1, :], in_=stg[l:64:4, :]),
                          reads=r_tm.b, sembuf=r_tm.b[0])
            else:
                P.dma("sp", _mk("dma_start", out=o_plrub[:, :], in_=stg[125:128, :]), reads=r_tm.b, sembuf=r_tm.b[0])
        if samp:
            A.free(r_lxs)

        r_xr = A.alloc(npg(8 * T * 4))
        xr = r_xr.f(8, T)
        r_xrb = A.alloc(npg(8 * T * 2))
        xrb = r_xrb.h(8, T)
        for ch in range(8):
            lb = r_lx.sub(ch, (3 + L) * 4 * NB)
            xb_ = r_xr.sub(ch, T * 4)
            o = view(xr[:, ch, :], NB, L)
            ACT(P, o, lx[:, ch, :, 0:L], AF.Identity, lb + [b_pp], xb_,
                scale=ppc(PP_LCW + 0 * 8 + ch), bias=ppc(PP_LCB + ch))
            for k in range(1, 4):
                STT(P, "dve", o, lx[:, ch, :, k:k + L], ppc(PP_LCW + k * 8 + ch), o, ALU.mult, ALU.add,
                    lb + xb_ + [b_pp], xb_)
            CP(P, "act", xrb[:, ch, :], xr[:, ch, :], xb_, r_xrb.sub(ch, T * 2))
        A.free(r_lx)

        if NATIVE_GELU_LRU:
            for ch in range(8):
                gb = r_gy.sub(ch, T * 4)
                ACT(P, gy[:, ch, :], gy[:, ch, :], AF.Gelu_apprx_tanh, gb, gb)
        else:
            r_t1 = A.alloc(npg(2 * T * 4))
            for ch in range(8):
                tmp = r_t1.f(2, T)[:, ch % 2, :]
                gb = r_gy.sub(ch, T * 4)
                tb = [r_t1.b[ch % 2]] if T == 512 else r_t1.b
                gelu2(gy[:, ch, :], gy[:, ch, :], tmp, gb, gb, tb)
            A.free(r_t1)

        h0fm = None
        if samp:
            r_h0 = A.alloc(1)
            r_c = A.alloc(2)
            ctm = r_c.f(1024)
            P.dma("sp", _mk("dma_start", out=ctm[0:16], in_=st_lru[:, :]), writes=r_c.b, sembuf=r_c.b[0])
            h0fm = r_h0.f(8, 16)
            bt, bb = P.bank()
            for ch in range(8):
                MM(P, bt[:, ch * 16:(ch + 1) * 16], ctm[0:16, ch * 128:(ch + 1) * 128], ident[0:16, 0:16],
                   True, True, r_c.b + [b_cst], bb, ch == 7)
            CP(P, "dve", h0fm, view(bt[:, 0:128], 8, 16), [bb], r_h0.b)
            A.free(r_c)

        r_h = A.alloc(npg(8 * T * 4))
        hh = r_h.f(8, T)
        r_ga = A.alloc(npg(8 * T * 4))
        r_gi = A.alloc(npg(8 * T * 4))
        r_gm = A.alloc(npg(8 * T * 4))
        ga, gi, gm = r_ga.f(8, T), r_gi.f(8, T), r_gm.f(8, T)
        for ch in range(8):
            ta, ti_, tm_ = ga[:, ch, :], gi[:, ch, :], gm[:, ch, :]
            ab, ibf, mb = r_ga.sub(ch, T * 4), r_gi.sub(ch, T * 4), r_gm.sub(ch, T * 4)
            xb_ = r_xr.sub(ch, T * 4)
            btr, bbr = P.bank()
            MM(P, btr[:, 0:T], lruw[:, 0, ch, :], xrb[:, ch, :], True, True, [b_lruw] + r_xrb.sub(ch, T * 2), bbr, True)
            bti, bbi = P.bank()
            MM(P, bti[:, 0:T], lruw[:, 1, ch, :], xrb[:, ch, :], True, True, [b_lruw] + r_xrb.sub(ch, T * 2), bbi, True)
            ACT(P, ta, btr[:, 0:T], AF.Tanh, [bbr, b_pq], ab, scale=0.5, bias=pqc(PQ_HBA + ch))
            ACT(P, ta, ta, AF.Exp, ab + [b_pq], ab, scale=pqc(PQ_CL + ch), bias=pqc(PQ_CL + ch))
            ACT(P, ti_, bti[:, 0:T], AF.Tanh, [bbi, b_pq], ibf, scale=0.5, bias=pqc(PQ_HBX + ch))
            STT(P, "dve", ti_, ti_, 1.0, xr[:, ch, :], ALU.add, ALU.mult, ibf + xb_, ibf)
            TT(P, "dve", tm_, ta, ta, ALU.mult, ab, mb)
            TS(P, "dve", tm_, tm_, -0.25, 0.25, ALU.mult, ALU.add, mb, mb)
        for ch in range(8):
            mb = r_gm.sub(ch, T * 4)
            ACT(P, gm[:, ch, :], gm[:, ch, :], AF.Sqrt, mb, mb)
        for ch in range(8):
            ta, ti_, tm_ = ga[:, ch, :], gi[:, ch, :], gm[:, ch, :]
            ab, ibf, mb = r_ga.sub(ch, T * 4), r_gi.sub(ch, T * 4), r_gm.sub(ch, T * 4)
            tu = tm_
            TT(P, "dve", tu, tm_, ti_, ALU.mult, mb + ibf, mb)
            if first_p:
                TS(P, "dve", tu[:, 0:1], ti_[:, 0:1], 0.5, None, ALU.mult, None, ibf, mb)
            if samp:
                a3 = view(ta, NB, L)
                u3 = view(tu, NB, L)
                TT(P, "dve", ti_[:, 0:16], a3[:, :, 0], h0fm[:, ch, :], ALU.mult, ab + r_h0.b + ibf, ibf)
                TT(P, "dve", u3[:, :, 0], u3[:, :, 0], ti_[:, 0:16], ALU.add, mb + ibf, mb)
                P.op("dve", _mk("memset", a3[:, :, 0], 0.0), [], ab)
                init = 0.0
                ib = []
            else:
                init = hcar[:, ch:ch + 1]
                ib = [b_hcar]
            hb = r_h.sub(ch, T * 4)
            P.op("dve", _mk("tensor_tensor_scan", out=hh[:, ch, :], data0=ta, data1=tu, initial=init,
                            op0=ALU.mult, op1=ALU.add), ab + mb + ib, hb)
            if not samp:
                CP(P, "pool", hcar[:, ch:ch + 1], hh[:, ch, T - 1:T], hb, [b_hcar])
        A.free(r_ga, r_gi, r_gm, r_xr, r_xrb)
        if samp:
            A.free(r_h0)
            r_c = A.alloc(1)
            hl = r_c.f(8, 16)
            CP(P, "dve", hl, hh.rearrange("p c (b l) -> p c b l", l=L)[:, :, :, 3], r_h.b, r_c.b)
            for half in range(2):
                bt, bb = P.bank()
                for j in range(4):
                    ch = half * 4 + j
                    MM(P, bt[0:16, j * 128:(j + 1) * 128], hl[:, ch, :], ident, True, True, r_c.b + [b_cst], bb, j == 3)
                CP(P, "dve", stg[0:16, half * 512:(half + 1) * 512], bt[0:16, :], [bb], r_tm.b)
            P.dma("sp", _mk("dma_start", out=o_slru[:, :], in_=stg[0:16, :]), reads=r_tm.b, sembuf=r_tm.b[0])
            A.free(r_c)
        if last_p:
            bt, bb = P.bank()
            MM(P, bt[0:8, 0:128], hcar[:, :], ident, True, True, [b_hcar, b_cst], bb, True)
            CP(P, "dve", stg[0:8, 0:128], bt[0:8, 0:128], [bb], r_tm.b)
            P.dma("sp", _mk("dma_start", out=o_plru[:, :], in_=stg[0:8, 0:128]), reads=r_tm.b, sembuf=r_tm.b[0])
        A.free(r_tm)
        r_yl = A.alloc(npg(8 * T * 2))
        ylru = r_yl.h(8, T)
        for ch in range(8):
            STT(P, "dve", ylru[:, ch, :], hh[:, ch, :], (1.0 if NATIVE_GELU_LRU else 0.5), gy[:, ch, :], ALU.mult, ALU.mult,
                r_h.sub(ch, T * 4) + r_gy.sub(ch, T * 4), r_yl.sub(ch, T * 2))
        A.free(r_h, r_gy)
        dbg("ylru%s%d" % (kind, ti), ylru, r_yl.b)

        P.tag = "%s%s%d" % ("xbc_", kind, ti)
        r_xa = A.alloc(npg(24 * T * 2))
        xact = r_xa.h(24, T)
        r_ext = A.alloc(6)
        r_cs = None
        xraw_s = None
        if samp:
            r_cs = A.alloc(3)
            cfm = r_cs.f(24, 48)
            r_c = A.alloc(6)
            ctm = r_c.f(3072)
            P.dma("sp", _mk("dma_start", out=ctm[0:48], in_=c_ssd[:, :]), writes=r_c.b, sembuf=r_c.b[0])
            for q in range(6):
                bt, bb = P.bank()
                for j in range(4):
                    ch = q * 4 + j
                    MM(P, bt[:, j * 48:(j + 1) * 48], ctm[0:48, ch * 128:(ch + 1) * 128], ident[0:48, 0:48],
                       True, True, r_c.b + [b_cst], bb, j == 3)
                CP(P, "dve", cfm[:, q * 4:q * 4 + 4, :], view(bt[:, 0:192], 4, 48), [bb], r_cs.b)
            A.free(r_c)
            r_xraw = A.alloc(3)
            xraw_s = r_xraw.f(24, 64)
        r_tc = A.alloc(npg(2 * T * 4))
        for cb in range(6):
            wi, wt, wb_ = WS.get((w_in, 0, 8, C_XBC + cb * 512, 512))
            for oc in range(4):
                ch = cb * 4 + oc
                slot = ch % 3
                eb = r_ext.b[slot * 2:slot * 2 + 2]
                ext = view(A.tile[:, (r_ext.lo + slot * 2) * 512:(r_ext.lo + slot * 2) * 512 + NB * (3 + L)], NB, 3 + L)
                bt, bb = P.bank()
                for kc in range(8):
                    MM(P, bt[:, 0:T], wt[:, kc, oc * 128:(oc + 1) * 128], xT[:, kc, :], kc == 0, kc == 7,
                       [wb_] + r_xT.b, bb, kc == 7)
                if samp:
                    CP(P, "pool", ext[:, :, 0:3], view(cfm[:, ch, :], NB, 3), r_cs.b, eb)
                    CP(P, "dve", xraw_s[:, ch, :], bt[:, 0:T], [bb], r_xraw.b)
                else:
                    CP(P, "pool", ext[:, 0, 0:3], halo_x[:, ch, :], [b_halo_x], eb)
                CP(P, "act", ext[:, :, 3:3 + L], view(bt[:, 0:T], NB, L), [bb], eb)
                if not samp:
                    CP(P, "pool", halo_x[:, ch, :], ext[:, 0, L:L + 3], eb, [b_halo_x])
                if last_p:
                    pass
                tcb = [r_tc.b[ch % 2]] if T == 512 else r_tc.b
                tc = view(r_tc.f(2, T)[:, ch % 2, :], NB, L)
                ACT(P, tc, ext[:, :, 0:L], AF.Identity, eb + [b_pq], tcb,
                    scale=pqc(PQ_SCW + 0 * 24 + ch), bias=pqc(PQ_SCB + ch))
                for k in range(1, 4):
                    STT(P, "dve", tc, ext[:, :, k:k + L], pqc(PQ_SCW + k * 24 + ch), tc, ALU.mult, ALU.add,
                        eb + tcb + [b_pq], tcb)
                if last_p:
                    if oc == 0:
                        pbt, pbb = P.bank()
                    MM(P, pbt[:, oc * 128:(oc + 1) * 128], ext[:, 0, 3 + L - 128:3 + L], ident, True, True,
                       eb + [b_cst], pbb, oc == 3)
                    if oc == 3:
                        if cb == 0:
                            r_pb = A.alloc(6)
                        CP(P, "dve", r_pb.f(3072)[:, cb * 512:(cb + 1) * 512], pbt[:, :], [pbb], r_pb.b)
                if NATIVE_SILU:
                    ACT(P, view(xact[:, ch, :], NB, L), tc, AF.Silu, tcb, r_xa.sub(ch, T * 2), scale=2.0)
                else:
                    r_ts = A.alloc(1)
                    ts_ = view(r_ts.f(T), NB, L)
                    ACT(P, ts_, tc, AF.Tanh, tcb, r_ts.b)
                    STT(P, "dve", view(xact[:, ch, :], NB, L), ts_, 1.0, tc, ALU.add, ALU.mult, r_ts.b + tcb,
                        r_xa.sub(ch, T * 2))
                    A.free(r_ts)
            WS.release(wi)
        A.free(r_tc, r_ext)
        if last_p:
            P.dma("sp", _mk("dma_start", out=o_pssdb[:, :], in_=r_pb.f(3072)[125:128, :]), reads=r_pb.b, sembuf=r_pb.b[0])
            A.free(r_pb)
        if samp:
            A.free(r_cs)
            r_st = A.alloc(6)
            stm = r_st.f(3072)
            for q in range(6):
                bt, bb = P.bank()
                for j in range(4):
                    ch = q * 4 + j
                    MM(P, bt[0:64, j * 128:(j + 1) * 128], xraw_s[:, ch, :], ident, True, True, r_xraw.b + [b_cst], bb, j == 3)
                CP(P, "dve", stm[0:64, q * 512:(q + 1) * 512], bt[0:64, :], [bb], r_st.b)
            for l in range(1, 4):
                P.dma("sp", _mk("dma_start", out=o_sssdb[:, l - 1, :], in_=stm[l:64:4, :]),
                      reads=r_st.b, sembuf=r_st.b[0])
            A.free(r_st, r_xraw)
        dbg("xact%s%d" % (kind, ti), xact, r_xa.b)

        P.tag = "%s%s%d" % ("dt_", kind, ti)
        r_dt = A.alloc(4)
        dtf = r_dt.f(4, 512)
        dt_ = dtf[0:32, 0, 0:T]
        acs = dtf[0:32, 1, 0:T]
        rev = dtf[0:32, 2, 0:T]
        tmpd = dtf[0:32, 3, 0:T]
        r_hl = A.alloc(1)
        hilo = r_hl.h(2, 512)
        wi, wt, wb_ = WS.get((w_in, 0, 8, C_DT, 32))
        bt, bb = P.bank()
        for kc in range(8):
            MM(P, bt[0:32, 0:T], wt[:, kc, 0:32], xT[:, kc, :], kc == 0, kc == 7, [wb_] + r_xT.b, bb, kc == 7)
        WS.release(wi)
        dbs = r_dt.b
        TS(P, "dve", dt_, bt[0:32, 0:T], pp[0:32, PP_DTB:PP_DTB + 1], None, ALU.add, None, [bb, b_pp], dbs)
        ACT(P, tmpd, dt_, AF.Abs, dbs, dbs)
        ACT(P, tmpd, tmpd, AF.Exp, dbs, dbs, scale=-1.0)
        ACT(P, tmpd, tmpd, AF.Ln, dbs, dbs, bias=1.0)
        STT(P, "dve", dt_, dt_, 0.0, tmpd, ALU.max, ALU.add, dbs, dbs)
        TS(P, "dve", tmpd, dt_, pq[0:32, PQ_A:PQ_A + 1], None, ALU.mult, None, dbs + [b_pq], dbs)
        if samp:
            P.op("dve", _mk("tensor_tensor_scan", out=acs, data0=rmask, data1=tmpd, initial=0.0,
                                                       op0=ALU.mult, op1=ALU.add), dbs + [b_cst], dbs)
            a3 = view(acs, NB, L)
            TT(P, "dve", view(rev, NB, L), a3[:, :, L - 1:L].to_broadcast([32, NB, L]), a3, ALU.subtract, dbs, dbs)
        else:
            for c in range(NCH):
                sl = slice(c * CL, (c + 1) * CL)
                P.op("dve", _mk("tensor_tensor_scan", out=acs[:, sl], data0=ones32[:, 0:CL], data1=tmpd[:, sl],
                                                                   initial=0.0, op0=ALU.mult, op1=ALU.add),
                     dbs + [b_cst], dbs)
                TS(P, "dve", rev[:, sl], acs[:, sl], -1.0, acs[:, (c + 1) * CL - 1:(c + 1) * CL], ALU.mult, ALU.add, dbs, dbs)
        CP(P, "act", hilo[0:32, 0, 0:T], acs, dbs, r_hl.b)
        TT(P, "dve", hilo[0:32, 1, 0:T], acs, hilo[0:32, 0, 0:T], ALU.subtract, dbs + r_hl.b, r_hl.b)

        P.tag = "%s%s%d" % ("z_", kind, ti)
        r_z = A.alloc(npg(NCH * 4096))
        ztm = r_z.h(NCH, 2048)
        r_zt = A.alloc(2)
        for cb in range(4):
            wi, wt, wb_ = WS.get((w_in, 0, 8, C_Z + cb * 512, 512))
            for c in range(NCH):
                bt, bb = P.bank()
                for kc in range(8):
                    MM(P, bt[0:CL, :], xT[:, kc, c * CL:(c + 1) * CL], wt[:, kc, :], kc == 0, kc == 7,
                       [wb_] + r_xT.b, bb, kc == 7)
                zt = r_zt.f(2, 512)[0:CL, (cb * NCH + c) % 2, :]
                zb = [r_zt.b[(cb * NCH + c) % 2]]
                if NATIVE_SILU:
                    ACT(P, ztm[0:CL, c, cb * 512:(cb + 1) * 512], bt[0:CL, :], AF.Silu, [bb], r_z.sub(c, 4096))
                else:
                    ACT(P, zt, bt[0:CL, :], AF.Tanh, [bb], zb, scale=0.5)
                    STT(P, "dve", ztm[0:CL, c, cb * 512:(cb + 1) * 512], zt, 1.0, bt[0:CL, :], ALU.add, ALU.mult,
                        zb + [bb], r_z.sub(c, 4096))
            WS.release(wi)
        A.free(r_zt)

        P.tag = "%s%s%d" % ("s4a_", kind, ti)
        r_tsg = A.alloc(npg(8 * T * 2))
        tss = r_tsg.h(8, T)
        r_m2 = A.alloc(npg(8 * T * 2))
        m2 = r_m2.h(8, T)
        r_t4 = A.alloc(4)
        tl4 = r_t4.f(4, 512)
        for cb in range(2):
            wi, wt, wb_ = WS.get((w_in, 0, 8, C_GS + cb * 512, 512))
            for oc in range(4):
                bt, bb = P.bank()
                for kc in range(8):
                    MM(P, bt[:, 0:T], wt[:, kc, oc * 128:(oc + 1) * 128], xT[:, kc, :], kc == 0, kc == 7,
                       [wb_] + r_xT.b, bb, kc == 7)
                ACT(P, tss[:, cb * 4 + oc, :], bt[:, 0:T], AF.Tanh, [bb, b_pq], r_tsg.sub(cb * 4 + oc, T * 2), scale=0.5,
                    bias=pqc(PQ_HBG + cb * 4 + oc))
            WS.release(wi)
            wi, wt, wb_ = WS.get((w_in, 0, 8, C_GL + cb * 512, 512))
            for oc in range(4):
                bt, bb = P.bank()
                for kc in range(8):
                    MM(P, bt[:, 0:T], wt[:, kc, oc * 128:(oc + 1) * 128], xT[:, kc, :], kc == 0, kc == 7,
                       [wb_] + r_xT.b, bb, kc == 7)
                ACT(P, tl4[:, oc, 0:T], bt[:, 0:T], AF.Tanh, [bb, b_pq], [r_t4.b[oc]], scale=0.5,
                    bias=pqc(PQ_HBG + 8 + cb * 4 + oc))
            WS.release(wi)
            wi, wt, wb_ = WS.get((w_lo, 0, 8, cb * 512, 512))
            for oc in range(4):
                bt, bb = P.bank()
                for kc in range(8):
                    MM(P, bt[:, 0:T], wt[:, kc, oc * 128:(oc + 1) * 128], ylru[:, kc, :], kc == 0, kc == 7,
                       [wb_] + r_yl.b, bb, kc == 7)
                STT(P, "dve", m2[:, cb * 4 + oc, :], tl4[:, oc, 0:T], 1.0, bt[:, 0:T], ALU.add, ALU.mult,
                    [r_t4.b[oc], bb], r_m2.sub(cb * 4 + oc, T * 2))
            WS.release(wi)
        A.free(r_t4, r_xT, r_yl)
        P.tag = "%s%s%d" % ("core_", kind, ti)
        P.tag = "%s%s%d" % ("core_", kind, ti)
        r_ys = A.alloc(npg(16 * T * 2))
        yssd = r_ys.h(16, T)

        def ssd_front(c):
            X = {}
            tok = slice(c * CL, (c + 1) * CL)
            r_sm = A.alloc(1)
            sm = r_sm.f(4, 32)
            bt, bb = P.bank()
            for q, srcq in enumerate((dt_, acs, rev)):
                MM(P, bt[0:CL, q * 32:(q + 1) * 32], srcq[:, tok], ident[0:32, 0:32], True, True, dbs + [b_cst], bb, q == 2)
            dt_tm = sm[0:CL, 0, :]
            ea_tm = sm[0:CL, 1, :]
            dte_tm = sm[0:CL, 2, :]
            nacs_tm = sm[0:CL, 3, :]
            CP(P, "dve", dt_tm, bt[0:CL, 0:32], [bb], r_sm.b)
            ACT(P, ea_tm, bt[0:CL, 32:64], AF.Exp, [bb], r_sm.b)
            dte_b = r_sm.h(16, 32)[0:CL, 8, :]
            ACT(P, dte_b, bt[0:CL, 64:96], AF.Exp, [bb], r_sm.b)
            TS(P, "dve", nacs_tm, bt[0:CL, 32:64], -1.0, None, ALU.mult, None, [bb], r_sm.b)
            r_xdt = A.alloc(2)
            r_xdte = A.alloc(2)
            xdt = r_xdt.h(2048)[0:CL]
            xdte = r_xdte.h(2048)[0:CL]
            for q in range(4):
                bt, bb = P.bank()
                for j in range(4):
                    jj = q * 4 + j
                    MM(P, bt[0:CL, j * 128:(j + 1) * 128], xact[:, jj, tok], identb, True, True,
                       r_xa.sub(jj, T * 2) + [b_cbf], bb, j == 3)
                TT(P, "dve", view(xdt[:, q * 512:(q + 1) * 512], 8, 64), view(bt[0:CL, :], 8, 64),
                   dt_tm[:, q * 8:(q + 1) * 8].unsqueeze(2).to_broadcast([CL, 8, 64]), ALU.mult,
                   [bb] + r_sm.b, r_xdt.b)
            TT(P, "dve", view(xdte, 32, 64), view(xdt, 32, 64), dte_b.unsqueeze(2).to_broadcast([CL, 32, 64]),
               ALU.mult, r_xdt.b + r_sm.b, r_xdte.b)
            r_b = A.alloc(2)
            Btm = r_b.h(2, 1024)[0:CL, 0, 0:512]
            CBT = r_b.h(2, 1024)[0:CL, 1, 0:4 * CL]
            bt, bb = P.bank()
            for g in range(4):
                MM(P, bt[0:CL, g * 128:(g + 1) * 128], xact[:, 16 + g, tok], identb, True, True,
                   r_xa.sub(16 + g, T * 2) + [b_cbf], bb, g == 3)
            CP(P, "act", Btm, bt[0:CL, :], [bb], [r_b.b[0]])
            bt, bb = P.bank()
            for g in range(4):
                MM(P, bt[0:CL, g * CL:(g + 1) * CL], xact[:, 16 + g, tok], xact[:, 20 + g, tok], True, True,
                   r_xa.sub(16 + g, T * 2) + r_xa.sub(20 + g, T * 2), bb, g == 3)
            CP(P, "act", CBT, bt[0:CL, 0:4 * CL], [bb], [r_b.b[1]])
            r_cd = A.alloc(1)
            if not samp:
                Rm = r_cd.f(2, 256)[0:32, 0, 0:16]
                cdcol = r_cd.f(2, 256)[:, 1, 0:16]
                TS(P, "dve", Rm, mask2, acs[:, (c + 1) * CL - 1:(c + 1) * CL], None, ALU.mult, None, dbs + [b_cst], r_cd.b)
                bt, bb = P.bank()
                MM(P, bt[:, 0:16], selpar, Rm, True, True, r_cd.b + [b_cst], bb, True)
                ACT(P, cdcol, bt[:, 0:16], AF.Exp, [bb], r_cd.b)
            else:
                Rm = r_cd.f(2, 256)[0:32, 0, :]
                cdcol = r_cd.f(2, 256)[:, 1, :]
                a3 = view(acs, NB, L)
                TT(P, "dve", view(Rm, 16, 16), mask2.unsqueeze(1).to_broadcast([32, 16, 16]),
                   a3[:, :, L - 1:L].to_broadcast([32, 16, 16]), ALU.mult, dbs + [b_cst], r_cd.b)
                bt, bb = P.bank()
                MM(P, bt[:, 0:256], selpar, Rm, True, True, r_cd.b + [b_cst], bb, True)
                ACT(P, cdcol, bt[:, 0:256], AF.Exp, [bb], r_cd.b)
            have_off = samp or not (first_p and c == 0)
            r_ht = None
            HT = None
            yo_banks = None
            if not samp:
                if have_off:
                    r_ht = A.alloc(2)
                    HT = r_ht.h(2048)
                    r_hb = A.alloc(2)
                    Hb16 = r_hb.h(16, 128)
                    CP(P, "act", Hb16[:, 0:8, :], Hst[:, 0:8, :], [b_H], [r_hb.b[0]])
                    CP(P, "dve", Hb16[:, 8:16, :], Hst[:, 8:16, :], [b_H], [r_hb.b[1]])
                    for q in range(4):
                        bt, bb = P.bank()
                        for j in range(4):
                            jj = q * 4 + j
                            MM(P, bt[:, j * 128:(j + 1) * 128], Hb16[:, jj, :], identb, True, True,
                               [r_hb.b[jj // 8], b_cbf], bb, j == 3)
                        CP(P, "act", HT[:, q * 512:(q + 1) * 512], bt[:, :], [bb], r_ht.b)
                    A.free(r_hb)
                for q in range(4):
                    bt, bb = P.bank()
                    for j in range(4):
                        jj = q * 4 + j
                        g = jj // 4
                        MM(P, bt[:, j * 128:(j + 1) * 128], xdte[:, jj * 128:(jj + 1) * 128], Btm[:, g * 128:(g + 1) * 128],
                           True, True, r_xdte.b + [r_b.b[0]], bb, j == 3)
                    for j in range(4):
                        jj = q * 4 + j
                        STT(P, "dve", Hst[:, jj, :], Hst[:, jj, :], cdcol[:, jj:jj + 1], bt[:, j * 128:(j + 1) * 128],
                            ALU.mult, ALU.add, [b_H, bb] + r_cd.b, [b_H])
            else:
                yo_banks = []
                for g in range(4):
                    bk = P.bank()
                    yo_banks.append(bk)
                    P.reserved.add(P.banks.index(bk))
                r_ctm = A.alloc(4)
                r_bm = A.alloc(8)
                for g in range(4):
                    TT(P, "dve", r_ctm.h(4, 16, 64)[:, g], xact[:, 20 + g, :].unsqueeze(1).to_broadcast([128, 16, 64]),
                       bmaskT, ALU.mult, r_xa.sub(20 + g, T * 2) + [b_cst], [r_ctm.b[g]])
                    TT(P, "dve", r_bm.h(4, 16, 128)[0:64, g], Btm[:, g * 128:(g + 1) * 128].unsqueeze(1).to_broadcast([64, 16, 128]),
                       bmask.unsqueeze(2).to_broadcast([64, 16, 128]), ALU.mult, [r_b.b[0], b_cst], r_bm.b[g * 2:g * 2 + 2])
                r_hs = A.alloc(4 * NHS)
                r_hts = A.alloc(2 * NHT)
                r_hbs = A.alloc(2 * NHT)
                for b in range(SB_):
                    hs_b = r_hs.b[(b % NHS) * 4:(b % NHS) * 4 + 4]
                    Hb = r_hs.f(NHS, 16, 128)[:, b % NHS]
                    P.dma("sp", _mk("dma_start", out=Hb, in_=st_ssd[b].rearrange("j q n -> q j n")),
                          writes=hs_b, sembuf=hs_b[0], nbytes=5 << 19)
                    ht_b = r_hts.b[(b % NHT) * 2:(b % NHT) * 2 + 2]
                    HTb = r_hts.h(NHT, 2048)[:, b % NHT]
                    hb_b = r_hbs.b[(b % NHT) * 2:(b % NHT) * 2 + 2]
                    Hbb = r_hbs.h(NHT, 16, 128)[:, b % NHT]
                    CP(P, "act", Hbb[:, 0:8, :], Hb[:, 0:8, :], hs_b, [hb_b[0]])
                    CP(P, "act", Hbb[:, 8:16, :], Hb[:, 8:16, :], hs_b, [hb_b[1]])
                    for q in range(4):
                        bt, bb = P.bank()
                        for j in range(4):
                            jj = q * 4 + j
                            MM(P, bt[:, j * 128:(j + 1) * 128], Hbb[:, jj, :], identb, True, True,
                               [hb_b[jj // 8], b_cbf], bb, j == 3)
                        CP(P, "dve" if q == 3 else "act", HTb[:, q * 512:(q + 1) * 512], bt[:, :], [bb], ht_b)
                    for g in range(4):
                        MM(P, yo_banks[g][0][0:64, :], r_ctm.h(4, 16, 64)[:, g, b, :], HTb[:, g * 512:(g + 1) * 512],
                           b == 0, b == SB_ - 1, [r_ctm.b[g]] + ht_b, yo_banks[g][1], b == SB_ - 1 or g == 3)
                    for q in range(4):
                        bt, bb = P.bank()
                        for j in range(4):
                            jj = q * 4 + j
                            g = jj // 4
                            MM(P, bt[:, j * 128:(j + 1) * 128], xdte[:, jj * 128:(jj + 1) * 128],
                               r_bm.h(4, 16, 128)[0:64, g, b, :], True, True, r_xdte.b + r_bm.b[g * 2:g * 2 + 2], bb, j == 3)
                        for j in range(4):
                            jj = q * 4 + j
                            STT(P, "dve", Hb[:, jj, :], Hb[:, jj, :], cdcol[:, b * 16 + jj:b * 16 + jj + 1],
                                bt[:, j * 128:(j + 1) * 128], ALU.mult, ALU.add, hs_b + r_cd.b + [bb], hs_b)
                    P.dma("sp", _mk("dma_start", out=o_sssd[b].rearrange("j q n -> q j n"), in_=Hb),
                          reads=hs_b, sembuf=hs_b[0], nbytes=5 << 19)
                A.free(r_hs, r_hts, r_hbs, r_ctm, r_bm)
            A.free(r_xdte, r_cd)
            X.update(tok=tok, r_sm=r_sm, ea_tm=ea_tm, nacs_tm=nacs_tm, r_xdt=r_xdt, xdt=xdt, r_b=r_b, CBT=CBT,
                     r_ht=r_ht, HT=HT, yo_banks=yo_banks, have_off=have_off)
            return X

        def ssd_back(c, X):
            tok = X["tok"]
            r_sm, ea_tm, nacs_tm, r_xdt, xdt, r_b, CBT = X["r_sm"], X["ea_tm"], X["nacs_tm"], X["r_xdt"], X["xdt"], X["r_b"], X["CBT"]
            r_ht, HT, yo_banks, have_off = X["r_ht"], X["HT"], X["yo_banks"], X["have_off"]
            r_y = A.alloc(4)
            ytm = r_y.f(2048)[0:CL]
            r_dec = A.alloc(2)
            r_mt = A.alloc(2)
            r_yo = A.alloc(2)
            for g in range(4):
                par = g % 2
                dec = r_dec.h(2, 8, 128)[0:CL, par, :, 0:CL]
                decb = [r_dec.b[par]]
                MT = r_mt.h(2, 8, 128)[0:CL, par, :, 0:CL]
                mtb = [r_mt.b[par]]
                for hq in range(2):
                    bt, bb = P.bank()
                    MM(P, bt[0:CL, 0:4 * CL], identb[0:CL, 0:CL], (negs4 if samp else negm4), True, False, [b_cbf], bb, False)
                    for i4 in range(4):
                        h = g * 8 + hq * 4 + i4
                        MM(P, bt[0:CL, i4 * CL:(i4 + 1) * CL], selb[:, h, 0:CL], hilo[0:32, 0, tok], False, False,
                           [b_cbf] + r_hl.b, bb, False)
                        MM(P, bt[0:CL, i4 * CL:(i4 + 1) * CL], selb[:, h, 0:CL], hilo[0:32, 1, tok], False, True,
                           [b_cbf] + r_hl.b, bb, i4 == 3)
                    for i4 in range(4):
                        h = g * 8 + hq * 4 + i4
                        ACT(P, dec[:, hq * 4 + i4, :], bt[0:CL, i4 * CL:(i4 + 1) * CL], AF.Exp, [bb] + r_sm.b, decb,
                            bias=nacs_tm[:, h:h + 1])
                TT(P, "dve", MT, dec, CBT[:, g * CL:(g + 1) * CL].unsqueeze(1).to_broadcast([CL, 8, CL]), ALU.mult,
                   decb + [r_b.b[1]], mtb)
                yt, yb = P.bank()
                for i8 in range(8):
                    h = g * 8 + i8
                    MM(P, yt[0:CL, i8 * 64:(i8 + 1) * 64], MT[:, i8, :], xdt[:, h * 64:(h + 1) * 64], i8 == 0, False,
                       mtb + r_xdt.b, yb, False)
                for j4 in range(4):
                    jj = g * 4 + j4
                    MM(P, yt[0:CL, j4 * 128:(j4 + 1) * 128], xact[:, jj, tok], diagD[:, jj, :], False, True,
                       r_xa.sub(jj, T * 2) + [b_diagD], yb, j4 == 3)
                yo = r_yo.f(2, 512)[0:CL, par]
                yob = [r_yo.b[par]]
                if have_off:
                    if samp:
                        ot, ob = yo_banks[g]
                    else:
                        ot, ob = P.bank()
                        MM(P, ot[0:CL, :], xact[:, 20 + g, tok], HT[:, g * 512:(g + 1) * 512], True, True,
                           r_xa.sub(20 + g, T * 2) + r_ht.b, ob, True)
                    TT(P, "dve", view(yo, 8, 64), view(ot[0:CL, :], 8, 64),
                       ea_tm[:, g * 8:(g + 1) * 8].unsqueeze(2).to_broadcast([CL, 8, 64]), ALU.mult, [ob] + r_sm.b, yob)
                    TT(P, "dve", ytm[:, g * 512:(g + 1) * 512], yt[0:CL, :], yo, ALU.add, [yb] + yob, [r_y.b[g]])
                else:
                    CP(P, "act", ytm[:, g * 512:(g + 1) * 512], yt[0:CL, :], [yb], [r_y.b[g]])
            if samp:
                for g in range(4):
                    P.reserved.discard(P.banks.index(yo_banks[g]))
            A.free(r_dec, r_mt, r_yo, r_xdt, r_b, r_sm)
            if r_ht is not None:
                A.free(r_ht)
            dbg("ytm%s%d_%d" % (kind, ti, c), ytm, r_y.b)
            r_g = A.alloc(1)
            stt_ = r_g.f(8)[0:CL]
            TT(P, "dve", ytm, ytm, ztm[0:CL, c, :], ALU.mult, r_y.b + r_z.sub(c, 4096), r_y.b)
            r_sq = A.alloc(2)
            ACT(P, r_sq.h(2048)[0:CL], ytm, AF.Square, r_y.b, r_sq.b + r_g.b, accum=stt_[:, 0:1])
            A.free(r_sq)
            TS(P, "dve", stt_[:, 1:2], stt_[:, 0:1], 1.0 / 2048, (1.0 if NATIVE_SILU else 4.0) * RMS_EPS, ALU.mult, ALU.add,
               r_g.b, r_g.b)
            ACT(P, stt_[:, 1:2], stt_[:, 1:2], AF.Sqrt, r_g.b, r_g.b)
            P.op("dve", _mk("reciprocal", out=stt_[:, 2:3], in_=stt_[:, 1:2]), r_g.b, r_g.b)
            r_gn = A.alloc(2)
            gyn = r_gn.h(2048)[0:CL]
            TS(P, "dve", gyn, ytm, stt_[:, 2:3], None, ALU.mult, None, r_y.b + r_g.b, r_gn.b)
            A.free(r_y)
            for q in range(4):
                bt, bb = P.bank()
                for j in range(4):
                    jj = q * 4 + j
                    MM(P, bt[:, j * CL:(j + 1) * CL], gyn[:, jj * 128:(jj + 1) * 128], identb[0:CL, 0:CL], True, True,
                       r_gn.b + [b_cbf], bb, j == 3)
                TT(P, "dve", yssd[:, q * 4:q * 4 + 4, tok], view(bt[:, 0:4 * CL], 4, CL),
                   ppc(PP_NG + q * 4, 4).unsqueeze(2).to_broadcast([128, 4, CL]), ALU.mult, [bb, b_pp], r_ys.b)
            A.free(r_g, r_gn)

        Xc = ssd_front(0)
        for c in range(NCH):
            Xn = ssd_front(c + 1) if c + 1 < NCH else None
            ssd_back(c, Xc)
            Xc = Xn
        if last_p:
            P.dma("sp", _mk("dma_start", out=o_pssd.rearrange("j q n -> q j n"), in_=Hst[:]), reads=[b_H], sembuf=b_H)
        A.free(r_dt, r_hl, r_z, r_xa)
        dbg("yssd%s%d" % (kind, ti), yssd, r_ys.b)

        P.tag = "%s%s%d" % ("s4_", kind, ti)
        r_x, x_tm = load_x()
        r_mg = A.alloc(npg(8 * T * 2))
        mg = r_mg.h(8, T)
        r_t4 = A.alloc(4)
        m14 = r_t4.f(4, 512)
        for cb in range(2):
            wi0, wt0, wb0 = WS.get((w_so, 0, 8, cb * 512, 512))
            wi1, wt1, wb1 = WS.get((w_so, 1024, 8, cb * 512, 512))
            for oc in range(4):
                ch = cb * 4 + oc
                bt, bb = P.bank()
                for kc in range(16):
                    wt_, wbb = (wt0, wb0) if kc < 8 else (wt1, wb1)
                    MM(P, bt[:, 0:T], wt_[:, kc % 8, oc * 128:(oc + 1) * 128], yssd[:, kc, :], kc == 0, kc == 15,
                       [wbb] + r_ys.b, bb, kc == 15)
                STT(P, "dve", m14[:, oc, 0:T], tss[:, ch, :], 1.0, bt[:, 0:T], ALU.add, ALU.mult,
                    r_tsg.sub(ch, T * 2) + [bb], [r_t4.b[oc]])
                TT(P, "dve", mg[:, ch, :], m14[:, oc, 0:T], m2[:, ch, :], ALU.add,
                   [r_t4.b[oc]] + r_m2.sub(ch, T * 2), r_mg.sub(ch, T * 2))
            WS.release(wi0)
            WS.release(wi1)
        A.free(r_t4, r_ys, r_tsg, r_m2)
        dbg("mg%s%d" % (kind, ti), mg, r_mg.b)

        P.tag = "%s%s%d" % ("ln1_", kind, ti)
        r_x1 = A.alloc(npg(NCH * 4096))
        x1 = r_x1.f(NCH, 1024)
        r_x1T = A.alloc(npg(8 * T * 2))
        x1T = r_x1T.h(8, T)
        r_st = A.alloc(1)
        stt4 = r_st.f(8, NCH)
        r_scr = A.alloc(2)
        scr = r_scr.f(1024)[0:CL]
        r_xb = A.alloc(2)
        r_bc1, bct1 = load_bc(0)
        wia, wta, wba = WS.get((w_o, 0, 8, 0, 512))
        wib, wtb, wbb_ = WS.get((w_o, 0, 8, 512, 512))
        for c in range(NCH):
            xb = r_x.sub(c, 4096)
            for cb, (wt, wb_) in enumerate(((wta, wba), (wtb, wbb_))):
                bt, bb = P.bank()
                for kc in range(8):
                    MM(P, bt[0:CL, :], mg[:, kc, c * CL:(c + 1) * CL], wt[:, kc, :], kc == 0, kc == 7,
                       [wb_] + r_mg.sub(kc, T * 2), bb, kc == 7)
                xs_ = x_tm[0:CL, c, cb * 512:(cb + 1) * 512]
                STT(P, "dve", xs_, bt[0:CL, :], 0.5 / ALPHA, xs_, ALU.mult, ALU.add, [bb] + xb, xb)
            ln_stats(x_tm[0:CL, c, :], CL, c, xb, stt4, r_st.b[0], scr, r_scr.b)
        ln_rstd(CL, stt4, r_st.b[0])
        for c in range(NCH):
            xb = r_x.sub(c, 4096)
            x1b = r_x1.sub(c, 4096)
            ln_apply(x_tm[0:CL, c, :], x1[0:CL, c, :], CL, c, xb, x1b, stt4, r_st.b[0], r_bc1, bct1, scr, r_scr.b)
            xbf = r_xb.h(2, 1024)[0:CL, c % 2]
            xbfb = [r_xb.b[c % 2]]
            CP(P, "act", xbf, x1[0:CL, c, :], x1b, xbfb)
            for half in range(2):
                bt, bb = P.bank()
                for j in range(4):
                    kc = half * 4 + j
                    MM(P, bt[:, j * CL:(j + 1) * CL], xbf[:, kc * 128:(kc + 1) * 128], identb[0:CL, 0:CL], True, True,
                       xbfb + [b_cbf], bb, j == 3)
                CP(P, "act" if half == 0 else "dve", x1T[:, half * 4:half * 4 + 4, c * CL:(c + 1) * CL],
                   view(bt[:, 0:4 * CL], 4, CL), [bb], r_x1T.b)
        WS.release(wia)
        WS.release(wib)
        A.free(r_x, r_mg, r_xb, r_bc1)
        dbg("x1T%s%d" % (kind, ti), x1T, r_x1T.b)

        P.tag = "%s%s%d" % ("ffn_", kind, ti)
        r_hT = A.alloc(npg(24 * T * 2))
        hT = r_hT.h(24, T)
        r_ef = A.alloc(6)
        r_fc = A.alloc(npg(4 * T * 4))
        r_cf = None
        fraw_s = None
        if samp:
            r_cf = A.alloc(2)
            cff = r_cf.f(24, 32)
            r_c = A.alloc(6)
            ctm = r_c.f(3072)
            P.dma("sp", _mk("dma_start", out=ctm[0:32], in_=c_ffn[:, :]), writes=r_c.b, sembuf=r_c.b[0])
            for q in range(6):
                bt, bb = P.bank()
                for j in range(4):
                    ch = q * 4 + j
                    MM(P, bt[:, j * 32:(j + 1) * 32], ctm[0:32, ch * 128:(ch + 1) * 128], ident[0:32, 0:32],
                       True, True, r_c.b + [b_cst], bb, j == 3)
                CP(P, "dve", cff[:, q * 4:q * 4 + 4, :], view(bt[:, 0:128], 4, 32), [bb], r_cf.b)
            A.free(r_c)
            r_fraw = A.alloc(3)
            fraw_s = r_fraw.f(24, 64)
        if last_p:
            r_pb = A.alloc(6)
        for fb in range(6):
            wig, wtg, wbg = WS.get((w_fg, 0, 8, fb * 512, 512))
            wiu, wtu, wbu = WS.get((w_fu, 0, 8, fb * 512, 512))
            for oc in range(4):
                ch = fb * 4 + oc
                slot = ch % 3
                eb = r_ef.b[slot * 2:slot * 2 + 2]
                ext = view(A.tile[:, (r_ef.lo + slot * 2) * 512:(r_ef.lo + slot * 2) * 512 + NB * (2 + L)], NB, 2 + L)
                bt, bb = P.bank()
                for kc in range(8):
                    MM(P, bt[:, 0:T], wtg[:, kc, oc * 128:(oc + 1) * 128], x1T[:, kc, :], kc == 0, kc == 7,
                       [wbg] + r_x1T.b, bb, kc == 7)
                ut, ub = P.bank()
                for kc in range(8):
                    MM(P, ut[:, 0:T], wtu[:, kc, oc * 128:(oc + 1) * 128], x1T[:, kc, :], kc == 0, kc == 7,
                       [wbu] + r_x1T.b, ub, kc == 7)
                if samp:
                    CP(P, "pool", ext[:, :, 0:2], view(cff[:, ch, :], NB, 2), r_cf.b, eb)
                    CP(P, "dve", fraw_s[:, ch, :], bt[:, 0:T], [bb], r_fraw.b)
                else:
                    CP(P, "pool", ext[:, 0, 0:2], halo_f[:, ch, :], [b_halo_f], eb)
                CP(P, "act", ext[:, :, 2:2 + L], view(bt[:, 0:T], NB, L), [bb], eb)
                if not samp:
                    CP(P, "pool", halo_f[:, ch, :], ext[:, 0, L:L + 2], eb, [b_halo_f])
                if last_p:
                    if oc == 0:
                        pbt, pbb = P.bank()
                    MM(P, pbt[:, oc * 128:(oc + 1) * 128], ext[:, 0, 2 + L - 128:2 + L], ident, True, True,
                       eb + [b_cst], pbb, oc == 3)
                    if oc == 3:
                        CP(P, "dve", r_pb.f(3072)[:, fb * 512:(fb + 1) * 512], pbt[:, :], [pbb], r_pb.b)
                par = ch % 2
                fcb = [r_fc.b[par]] if T == 512 else r_fc.b
                ftb = [r_fc.b[2 + par]] if T == 512 else r_fc.b
                fc2 = r_fc.f(4, T)[:, par, :]
                ft2 = r_fc.f(4, T)[:, 2 + par, :]
                fc = view(fc2, NB, L)
                ACT(P, fc, ext[:, :, 0:L], AF.Identity, eb + [b_pp], fcb,
                    scale=ppc(PP_FCW + 0 * 24 + ch), bias=ppc(PP_FCB + ch))
                for k in range(1, 3):
                    STT(P, "dve", fc, ext[:, :, k:k + L], ppc(PP_FCW + k * 24 + ch), fc, ALU.mult, ALU.add,
                        eb + fcb + [b_pp], fcb)
                if NATIVE_GELU:
                    ACT(P, fc2, fc2, AF.Gelu_apprx_tanh, fcb, fcb)
                    TT(P, "dve", hT[:, ch, :], ut[:, 0:T], fc2, ALU.mult, [ub] + fcb, r_hT.sub(ch, T * 2))
                else:
                    gelu2(fc2, fc2, ft2, fcb, fcb, ftb)
                    STT(P, "dve", hT[:, ch, :], ut[:, 0:T], 0.5, fc2, ALU.mult, ALU.mult, [ub] + fcb, r_hT.sub(ch, T * 2))
            WS.release(wig)
            WS.release(wiu)
        A.free(r_ef, r_fc, r_x1T)
        if last_p:
            P.dma("sp", _mk("dma_start", out=o_pffnb[:, :], in_=r_pb.f(3072)[126:128, :]), reads=r_pb.b, sembuf=r_pb.b[0])
            A.free(r_pb)
        if samp:
            A.free(r_cf)
            r_sf = A.alloc(6)
            sfm = r_sf.f(3072)
            for q in range(6):
                bt, bb = P.bank()
                for j in range(4):
                    ch = q * 4 + j
                    MM(P, bt[0:64, j * 128:(j + 1) * 128], fraw_s[:, ch, :], ident, True, True, r_fraw.b + [b_cst], bb, j == 3)
                CP(P, "dve", sfm[0:64, q * 512:(q + 1) * 512], bt[0:64, :], [bb], r_sf.b)
            for l in range(2, 4):
                P.dma("sp", _mk("dma_start", out=o_sffnb[:, l - 2, :], in_=sfm[l:64:4, :]),
                      reads=r_sf.b, sembuf=r_sf.b[0])
            A.free(r_sf, r_fraw)
        dbg("hT%s%d" % (kind, ti), hT, r_hT.b)

        P.tag = "%s%s%d" % ("down_", kind, ti)
        if tidx + 1 < len(tiles):
            prefetched_x = load_x_for(tile_params(*tiles[tidx + 1]))
        r_yo2 = A.alloc(npg(NCH * 4096))
        yout = r_yo2.f(NCH, 1024)
        for cb in range(2):
            wks = [WS.get((w_fd, kb * 1024, 8, cb * 512, 512)) for kb in range(3)]
            for c in range(NCH):
                bt, bb = P.bank()
                for kc in range(24):
                    wi, wt, wb_ = wks[kc // 8]
                    MM(P, bt[0:CL, :], hT[:, kc, c * CL:(c + 1) * CL], wt[:, kc % 8, :], kc == 0, kc == 23,
                       [wb_] + r_hT.sub(kc, T * 2), bb, kc == 23)
                STT(P, "dve", x1[0:CL, c, cb * 512:(cb + 1) * 512], bt[0:CL, :], 1.0 / ALPHA,
                    x1[0:CL, c, cb * 512:(cb + 1) * 512], ALU.mult, ALU.add, [bb] + r_x1.sub(c, 4096), r_x1.sub(c, 4096))
            for wi, wt, wb_ in wks:
                WS.release(wi)
        r_bc2, bct2 = load_bc(2048)
        for c in range(NCH):
            ln_stats(x1[0:CL, c, :], CL, c, r_x1.sub(c, 4096), stt4, r_st.b[0], scr, r_scr.b)
        ln_rstd(CL, stt4, r_st.b[0])
        for c in range(NCH):
            ln_apply(x1[0:CL, c, :], yout[0:CL, c, :], CL, c, r_x1.sub(c, 4096), r_yo2.sub(c, 4096), stt4, r_st.b[0],
                     r_bc2, bct2, scr, r_scr.b)
        P.dma("sp", _mk("dma_start",
            out=ydst[t0:t0 + T, :].rearrange("(c p) d -> p c d", p=CL), in_=yout[0:CL]),
            reads=r_yo2.b, sembuf=r_yo2.b[0], nbytes=T * 4096)
        A.free(r_hT, r_x1, r_st, r_scr, r_yo2, r_bc2)

    P.finish()
    P.emit()
    nc._sim_end = P.sim_end
    nc._P = P
    return nc


NSLOTS = 5
STRICT_SAME_ENGINE = True
NHS = 6
NHT = 2
SYNC_LAT = 400.0
SYNC_LAT_SAME = 150.0
NATIVE_GELU = True
NATIVE_SILU = True
NATIVE_GELU_LRU = True
SCHED_PRIO = "id"
SCHED_MIX = 0.0
IGNORE_FALSE_DEPS = False
TILES = None
NPAGES = 66

PP_SCW = 0
PP_SCB = 96
PP_NG = 120
PP_LCW = 136
PP_LCB = 168
PP_BA = 176
PP_BX = 184
PP_LAM = 192
PP_BG = 200
PP_FCW = 216
PP_FCB = 288
PP_D = 312
PP_DTB = 328
PP_ALOG = 329
NPP = 330
PQ_SCW = 0
PQ_SCB = 96
PQ_HBA = 120
PQ_HBX = 128
PQ_HBG = 136
PQ_CL = 152
PQ_A = 160
NPQ = 161
CS_ID = 0
CS_SELPAR = 128
CS_MASK2 = 256
CS_ONES = 272
CS_RMASK = 400
CS_BMASK = 464
CS_BMASKT = 480
NCST = 1504
NCBF = 896 + 4096


def _fm(v, nch):
    return np.ascontiguousarray(v.reshape(nch, 128).T)


def _build_consts():
    cst = np.zeros((128, NCST), np.float32)
    cbs = np.zeros((128, NCBF), np.float32)
    cst[:, CS_ID:CS_ID + 128] = np.eye(128, dtype=np.float32)
    cbs[:, 0:128] = np.eye(128, dtype=np.float32)
    s = np.arange(128)[:, None]
    l = np.arange(128)[None, :]
    nm = np.where(l >= s, 0.0, NEG).astype(np.float32)
    cbs[:, 128:640] = np.tile(nm, (1, 4))
    s = np.arange(64)[:, None]
    l = np.arange(64)[None, :]
    nms = np.where((l >= s) & (l // 4 == s // 4), 0.0, NEG).astype(np.float32)
    cbs[0:64, 640:896] = np.tile(nms, (1, 4))
    sel = np.zeros((32, 32, 128), np.float32)
    for h in range(32):
        sel[h, h, :] = 1.0
    cbs[0:32, 896:896 + 4096] = sel.reshape(32, 4096)
    sp = np.zeros((32, 128), np.float32)
    for k in range(32):
        hh = k % 2
        sp[k, hh * 64:(hh + 1) * 64] = 1.0
    cst[0:32, CS_SELPAR:CS_SELPAR + 128] = sp
    m2 = np.zeros((32, 16), np.float32)
    for k in range(32):
        m2[k, k // 2] = 1.0
    cst[0:32, CS_MASK2:CS_MASK2 + 16] = m2
    cst[0:32, CS_ONES:CS_ONES + 128] = 1.0
    rm = np.ones((32, 64), np.float32)
    rm[:, 0::4] = 0.0
    cst[0:32, CS_RMASK:CS_RMASK + 64] = rm
    bm = np.zeros((64, 16), np.float32)
    for t in range(64):
        bm[t, t // 4] = 1.0
    cst[0:64, CS_BMASK:CS_BMASK + 16] = bm
    bmt = np.zeros((16, 64), np.float32)
    for t in range(64):
        bmt[t // 4, t] = 1.0
    cst[:, CS_BMASKT:CS_BMASKT + 1024] = np.broadcast_to(bmt.reshape(1, 1024), (128, 1024))
    return cst, cbs


_CACHE = {}


def kernel(x_prompt, x_sample, state_ssd, cache_ssd_conv, state_lru, cache_lru_conv, cache_ffn_conv,
           w_in, b_gate, ssd_conv_w, ssd_conv_b, ssd_dt_bias, ssd_a_log, ssd_d, ssd_norm_g, w_ssd_out,
           lru_conv_w, lru_conv_b, lru_wa, lru_ba, lru_wx, lru_bx, lru_lambda, w_lru_out, w_o,
           ln1_g, ln1_b, ffn_w_gate, ffn_w_up, ffn_conv_w, ffn_conv_b, ffn_w_down, ln2_g, ln2_b,
           _debug=None):
    f = lambda a: np.ascontiguousarray(np.asarray(a, dtype=np.float32))
    x_prompt, x_sample = f(x_prompt), f(x_sample)
    pp = np.zeros((128, NPP), np.float32)
    scw = f(ssd_conv_w)[0]
    pp[:, PP_SCW:PP_SCW + 96] = scw.reshape(4, 24, 128).transpose(2, 0, 1).reshape(128, 96)
    pp[:, PP_SCB:PP_SCB + 24] = _fm(f(ssd_conv_b)[0], 24)
    pp[:, PP_NG:PP_NG + 16] = _fm(f(ssd_norm_g)[0], 16)
    pp[:, PP_LCW:PP_LCW + 32] = f(lru_conv_w)[0].reshape(4, 8, 128).transpose(2, 0, 1).reshape(128, 32)
    pp[:, PP_LCB:PP_LCB + 8] = _fm(f(lru_conv_b)[0], 8)
    pp[:, PP_BA:PP_BA + 8] = f(lru_ba)[0].T
    pp[:, PP_BX:PP_BX + 8] = f(lru_bx)[0].T
    pp[:, PP_LAM:PP_LAM + 8] = _fm(f(lru_lambda)[0], 8)
    pp[:, PP_BG:PP_BG + 16] = _fm(f(b_gate)[0], 16)
    pp[:, PP_FCW:PP_FCW + 72] = f(ffn_conv_w)[0].reshape(3, 24, 128).transpose(2, 0, 1).reshape(128, 72)
    pp[:, PP_FCB:PP_FCB + 24] = _fm(f(ffn_conv_b)[0], 24)
    dd = f(ssd_d)[0]
    pp[:, PP_D:PP_D + 16] = np.repeat(dd.reshape(16, 2).T, 64, axis=0)
    pp[0:32, PP_DTB] = f(ssd_dt_bias)[0]
    pp[0:32, PP_ALOG] = f(ssd_a_log)[0]
    bcv = np.concatenate([f(ln1_g)[0], f(ln1_b)[0], f(ln2_g)[0], f(ln2_b)[0]])
    bc = np.ascontiguousarray(np.broadcast_to(bcv[None, :], (128, 4096)))
    cst, cbs = _build_consts()
    lru_w = np.ascontiguousarray(np.stack([f(lru_wa)[0], f(lru_wx)[0]]))
    shared = {
        "w_in": f(w_in)[0], "w_so": f(w_ssd_out)[0], "w_lo": f(w_lru_out)[0], "w_o": f(w_o)[0],
        "w_fg": f(ffn_w_gate)[0], "w_fu": f(ffn_w_up)[0], "w_fd": f(ffn_w_down)[0],
        "lru_w": lru_w, "pp": pp, "bc": bc, "cst": cst, "cbfsrc": cbs,
    }
    st = f(state_ssd)[0]
    cs = f(cache_ssd_conv)[0]
    sl = f(state_lru)[0]
    cl = f(cache_lru_conv)[0]
    cf = f(cache_ffn_conv)[0]
    in_maps = []
    for c in range(NCORES):
        b0, b1 = c * SB_, (c + 1) * SB_
        m = dict(shared)
        m["xp"] = x_prompt[c]
        m["xs"] = x_sample[b0:b1].reshape(SB_ * SL, D)
        m["st_ssd"] = st[b0:b1].reshape(SB_, 16, 128, 128)
        m["c_ssd"] = cs[b0:b1].reshape(SB_ * 3, 3072)
        m["st_lru"] = sl[b0:b1]
        m["c_lru"] = cl[b0:b1].reshape(SB_ * 3, D)
        m["c_ffn"] = cf[b0:b1].reshape(SB_ * 2, 3072)
        in_maps.append(m)
    key = repr(sorted(_debug.items())) if _debug else ""
    if key not in _CACHE:
        _CACHE[key] = build_program(_debug)
    nc = _CACHE[key]
    res = run_bass_kernel_spmd(nc, in_maps, core_ids=list(range(NCORES)))
    R = res.results
    cat = lambda name: np.stack([R[c][name] for c in range(NCORES)])
    y_prompt = cat("yp")
    y_sample = np.concatenate([R[c]["ys"].reshape(SB_, SL, D) for c in range(NCORES)])
    p_ssd = cat("o_pssd").reshape(1, NCORES, 32, 64, 128)
    p_ssdb = cat("o_pssdb")[None]
    p_lru = cat("o_plru").reshape(1, NCORES, D)
    p_lrub = cat("o_plrub")[None]
    p_ffnb = cat("o_pffnb")[None]
    s_ssd = np.concatenate([R[c]["o_sssd"] for c in range(NCORES)]).reshape(1, NCORES * SB_, 32, 64, 128)
    s_ssdb = np.concatenate([R[c]["o_sssdb"] for c in range(NCORES)])[None]
    s_lru = np.concatenate([R[c]["o_slru"] for c in range(NCORES)])[None]
    s_lrub = np.concatenate([R[c]["o_slrub"] for c in range(NCORES)])[None]
    s_ffnb = np.concatenate([R[c]["o_sffnb"] for c in range(NCORES)])[None]
    outs = (y_prompt, y_sample, p_ssd, p_ssdb, p_lru, p_lrub, p_ffnb, s_ssd, s_ssdb, s_lru, s_lrub, s_ffnb)
    outs = tuple(np.ascontiguousarray(o, dtype=np.float32) for o in outs)
    if _debug:
        return outs, R
    return outs
```
